# Optimizing a Trainium2 kernel written in Bass

```python
import math
import jax, jax.numpy as jnp
from jax import lax
import numpy as np


D_MODEL = 1024
BATCH = 8
SEQ = 4096
DEPTH = 2
DEC_BATCH = 32
DEC_SEQ = 32
PAST_LEN = 2048

CHUNK = 64
D_MIX = 2 * D_MODEL
D_A = D_MIX // 4
D_B = D_MIX // 2
D_C = D_MIX // 4
CONV_A_W = 3
SSM_HEAD_DIM = 64
SSM_HEADS = D_B // SSM_HEAD_DIM
SSM_GROUPS = 2
SSM_STATE = 128
CONV_B_W = 4
D_XBC = D_B + 2 * SSM_GROUPS * SSM_STATE
SSD_CHUNK = CHUNK
CONV_C_W = 31
D_FF = ((8 * D_MODEL // 3 + 127) // 128) * 128
CONV_F_W = 3
D_IN = 3 * D_A + D_B + D_XBC + SSM_HEADS + 2 * D_C
EPS = 1e-5

kernel_name = 'hybrid_stream_encoder_step'


def _rmsnorm(x, g):
    xf = x.astype(jnp.float32)
    y = xf * lax.rsqrt(jnp.mean(xf * xf, axis=-1, keepdims=True) + EPS)
    return (y * g.astype(jnp.float32)).astype(x.dtype)


def _layernorm(x, g, b):
    xf = x.astype(jnp.float32)
    mu = jnp.mean(xf, axis=-1, keepdims=True)
    var = jnp.mean(jnp.square(xf - mu), axis=-1, keepdims=True)
    y = (xf - mu) * lax.rsqrt(var + EPS)
    return (y * g.astype(jnp.float32) + b.astype(jnp.float32)).astype(x.dtype)


def _causal_dwconv(x, prev, w):
    k = w.shape[0]
    xp = jnp.concatenate([prev.astype(x.dtype), x], axis=1)
    y = lax.conv_general_dilated(xp, w[:, None, :].astype(x.dtype), window_strides=(1,),
                                 padding='VALID', dimension_numbers=('NWC', 'WIO', 'NWC'),
                                 feature_group_count=x.shape[-1])
    return y, xp[:, xp.shape[1] - (k - 1):]


def _ssd(xh, dt, a_head, bm, cm, s0, chunk):
    f32 = jnp.float32
    b, l, h, p = xh.shape
    g, n = bm.shape[2], bm.shape[3]
    r = h // g
    c = l // chunk
    x = xh.astype(f32).reshape(b, c, chunk, g, r, p)
    dtc = dt.reshape(b, c, chunk, g, r)
    bmc = bm.astype(f32).reshape(b, c, chunk, g, n)
    cmc = cm.astype(f32).reshape(b, c, chunk, g, n)
    a_cum = jnp.cumsum(dtc * a_head.reshape(g, r), axis=2)
    causal = jnp.tril(jnp.ones((chunk, chunk), dtype=bool))[None, None, :, :, None, None]
    seg = a_cum[:, :, :, None] - a_cum[:, :, None, :]
    decay_qk = jnp.exp(jnp.where(causal, seg, -jnp.inf))
    cb = jnp.einsum('bcqgn,bckgn->bcqkg', cmc, bmc)
    y_diag = jnp.einsum('bcqkg,bcqkgr,bckgr,bckgrp->bcqgrp', cb, decay_qk, dtc, x)
    decay_end = jnp.exp(a_cum[:, :, -1:] - a_cum)
    chunk_states = jnp.einsum('bckgn,bckgr,bckgrp->bcgrpn', bmc, decay_end * dtc, x)
    chunk_decay = jnp.exp(a_cum[:, :, -1])

    def step(s, inp):
        dec, st = inp
        return dec[..., None, None] * s + st, s

    s_last, s_prev = lax.scan(step, s0.astype(f32).reshape(b, g, r, p, n),
                              (jnp.moveaxis(chunk_decay, 1, 0), jnp.moveaxis(chunk_states, 1, 0)))
    s_prev = jnp.moveaxis(s_prev, 0, 1)
    y_off = jnp.einsum('bcqgn,bcgrpn,bcqgr->bcqgrp', cmc, s_prev, jnp.exp(a_cum))
    y = (y_diag + y_off).reshape(b, l, h, p)
    return y, s_last.reshape(b, h, p, n).astype(s0.dtype)


def _layer(x, st_a, st_ssm, st_b, st_c, st_f, norm_mix_g, w_in, conv_a_w, conv_b_w,
           conv_b_bias, dt_bias, a_log, d_skip, ssm_norm_g, conv_c_w, conv_c_bias, ln_c_g,
           ln_c_b, w_out, norm_ffn_g, w_up, conv_ffn_w, w_down):
    f32 = jnp.float32
    b, l, _ = x.shape
    h = _rmsnorm(x, norm_mix_g)
    proj = jnp.einsum('bld,de->ble', h, w_in)
    cuts = [D_A, 2 * D_A, 3 * D_A, 3 * D_A + D_B, 3 * D_A + D_B + D_XBC,
            3 * D_A + D_B + D_XBC + SSM_HEADS]
    a_v, a_b, a_c, z, xbc, dt_raw, c_in = jnp.split(proj, cuts, axis=-1)

    u_conv, new_a = _causal_dwconv(a_c * a_v, st_a, conv_a_w)
    y_a = a_b * u_conv

    xbc_c, new_b = _causal_dwconv(xbc, st_b, conv_b_w)
    xbc_c = jax.nn.silu(xbc_c + conv_b_bias.astype(x.dtype))
    xs, bm, cm = jnp.split(xbc_c, [D_B, D_B + SSM_GROUPS * SSM_STATE], axis=-1)
    xs = xs.reshape(b, l, SSM_HEADS, SSM_HEAD_DIM)
    bm = bm.reshape(b, l, SSM_GROUPS, SSM_STATE)
    cm = cm.reshape(b, l, SSM_GROUPS, SSM_STATE)
    dt = jax.nn.softplus(dt_raw.astype(f32) + dt_bias.astype(f32))
    a_head = -jnp.exp(a_log.astype(f32))
    y_ssm, new_ssm = _ssd(xs, dt, a_head, bm, cm, st_ssm, min(SSD_CHUNK, l))
    y_ssm = y_ssm + d_skip.astype(f32)[:, None] * xs.astype(f32)
    y_b = _rmsnorm(y_ssm.reshape(b, l, D_B) * jax.nn.silu(z.astype(f32)), ssm_norm_g).astype(x.dtype)

    c_glu = c_in[..., :D_C] * jax.nn.sigmoid(c_in[..., D_C:])
    c_conv, new_c = _causal_dwconv(c_glu, st_c, conv_c_w)
    y_c = jax.nn.silu(_layernorm(c_conv + conv_c_bias.astype(x.dtype), ln_c_g, ln_c_b))

    x = x + jnp.einsum('ble,ed->bld', jnp.concatenate([y_a, y_b, y_c], axis=-1), w_out)

    h2 = _rmsnorm(x, norm_ffn_g)
    up, gate = jnp.split(jnp.einsum('bld,df->blf', h2, w_up), 2, axis=-1)
    g_conv, new_f = _causal_dwconv(gate, st_f, conv_ffn_w)
    x = x + jnp.einsum('blf,fd->bld', jax.nn.silu(g_conv) * up, w_down)
    return x, (new_a, new_ssm, new_b, new_c, new_f)


def setup_inputs(seed: int = 0) -> dict:
    key = jax.random.key(seed)
    ks = jax.random.split(key, 26)
    f32 = jnp.float32

    def nrm(k, shape, scale):
        return jax.random.normal(k, shape, f32) * scale

    def gain(k, shape):
        return 1.0 + nrm(k, shape, 0.02)

    dt0 = jnp.exp(jax.random.uniform(ks[12], (DEPTH, SSM_HEADS), f32, math.log(1e-3), math.log(1e-1)))
    return {
        'x_prompt': nrm(ks[0], (BATCH, SEQ, D_MODEL), 1.0),
        'x_sample': nrm(ks[1], (DEC_BATCH, DEC_SEQ, D_MODEL), 1.0),
        'state_conv_a': nrm(ks[2], (DEPTH, DEC_BATCH, CONV_A_W - 1, D_A), 1.0),
        'state_ssm': nrm(ks[3], (DEPTH, DEC_BATCH, SSM_HEADS, SSM_HEAD_DIM, SSM_STATE), 0.1),
        'state_conv_b': nrm(ks[4], (DEPTH, DEC_BATCH, CONV_B_W - 1, D_XBC), 1.0),
        'state_conv_c': nrm(ks[5], (DEPTH, DEC_BATCH, CONV_C_W - 1, D_C), 1.0),
        'state_conv_ffn': nrm(ks[6], (DEPTH, DEC_BATCH, CONV_F_W - 1, D_FF), 1.0),
        'norm_mix_g': gain(ks[7], (DEPTH, D_MODEL)),
        'w_in': nrm(ks[8], (DEPTH, D_MODEL, D_IN), D_MODEL ** -0.5),
        'conv_a_w': nrm(ks[9], (DEPTH, CONV_A_W, D_A), CONV_A_W ** -0.5),
        'conv_b_w': nrm(ks[10], (DEPTH, CONV_B_W, D_XBC), CONV_B_W ** -0.5),
        'conv_b_bias': nrm(ks[11], (DEPTH, D_XBC), 0.02),
        'dt_bias': dt0 + jnp.log(-jnp.expm1(-dt0)),
        'a_log': jnp.log(jax.random.uniform(ks[13], (DEPTH, SSM_HEADS), f32, 1.0, 16.0)),
        'd_skip': gain(ks[14], (DEPTH, SSM_HEADS)),
        'ssm_norm_g': gain(ks[15], (DEPTH, D_B)),
        'conv_c_w': nrm(ks[16], (DEPTH, CONV_C_W, D_C), CONV_C_W ** -0.5),
        'conv_c_bias': nrm(ks[17], (DEPTH, D_C), 0.02),
        'ln_c_g': gain(ks[18], (DEPTH, D_C)),
        'ln_c_b': nrm(ks[19], (DEPTH, D_C), 0.02),
        'w_out': nrm(ks[20], (DEPTH, D_MIX, D_MODEL), D_MIX ** -0.5),
        'norm_ffn_g': gain(ks[21], (DEPTH, D_MODEL)),
        'w_up': nrm(ks[22], (DEPTH, D_MODEL, 2 * D_FF), D_MODEL ** -0.5),
        'conv_ffn_w': nrm(ks[23], (DEPTH, CONV_F_W, D_FF), CONV_F_W ** -0.5),
        'w_down': nrm(ks[24], (DEPTH, D_FF, D_MODEL), D_FF ** -0.5),
        'final_norm_g': gain(ks[25], (D_MODEL,)),
    }


def reference(x_prompt, x_sample, state_conv_a, state_ssm, state_conv_b, state_conv_c,
              state_conv_ffn, norm_mix_g, w_in, conv_a_w, conv_b_w, conv_b_bias, dt_bias,
              a_log, d_skip, ssm_norm_g, conv_c_w, conv_c_bias, ln_c_g, ln_c_b, w_out,
              norm_ffn_g, w_up, conv_ffn_w, w_down, final_norm_g):
    bp = x_prompt.shape[0]
    dtp = x_prompt.dtype
    zero_states = (jnp.zeros((bp, CONV_A_W - 1, D_A), dtp),
                   jnp.zeros((bp, SSM_HEADS, SSM_HEAD_DIM, SSM_STATE), dtp),
                   jnp.zeros((bp, CONV_B_W - 1, D_XBC), dtp),
                   jnp.zeros((bp, CONV_C_W - 1, D_C), dtp),
                   jnp.zeros((bp, CONV_F_W - 1, D_FF), dtp))
    hp, hs = x_prompt, x_sample
    p_a, p_ssm, p_b, p_c, p_f = [], [], [], [], []
    s_a, s_ssm, s_b, s_c, s_f = [], [], [], [], []
    for i in range(DEPTH):
        params = (norm_mix_g[i], w_in[i], conv_a_w[i], conv_b_w[i], conv_b_bias[i], dt_bias[i],
                  a_log[i], d_skip[i], ssm_norm_g[i], conv_c_w[i], conv_c_bias[i], ln_c_g[i],
                  ln_c_b[i], w_out[i], norm_ffn_g[i], w_up[i], conv_ffn_w[i], w_down[i])
        hp, (na, nssm, nb, nc, nf) = _layer(hp, *zero_states, *params)
        p_a.append(na); p_ssm.append(nssm); p_b.append(nb); p_c.append(nc); p_f.append(nf)
        hs, (na, nssm, nb, nc, nf) = _layer(hs, state_conv_a[i], state_ssm[i], state_conv_b[i],
                                            state_conv_c[i], state_conv_ffn[i], *params)
        s_a.append(na); s_ssm.append(nssm); s_b.append(nb); s_c.append(nc); s_f.append(nf)
    y_prompt = _rmsnorm(hp, final_norm_g)
    y_sample = _rmsnorm(hs, final_norm_g)
    return (y_prompt, y_sample,
            jnp.stack(p_a), jnp.stack(p_ssm), jnp.stack(p_b), jnp.stack(p_c), jnp.stack(p_f),
            jnp.stack(s_a), jnp.stack(s_ssm), jnp.stack(s_b), jnp.stack(s_c), jnp.stack(s_f))
```

```python
import numpy as np
from contextlib import ExitStack
import concourse.bass as bass
import concourse.mybir as mybir
from concourse.bass_utils import run_bass_kernel_spmd

F32 = mybir.dt.float32
BF16 = mybir.dt.bfloat16
AF = mybir.ActivationFunctionType
ALU = mybir.AluOpType

NCORES = 8
D = 1024
DIN = 5136
DFF = 2816
NL = 2
TP = 512
EPS = 1e-5
NSLOT = 4
SLOTW = 2048

_PC = {}
_off = 0
for _n, _w in (("g1", 8), ("g2", 8), ("gssm", 8), ("dsk", 8), ("caw", 12), ("cbw", 48), ("cbb", 12),
               ("ccw", 124), ("ccb", 4), ("lng", 4), ("lnb", 4), ("cfw", 66), ("dtb", 16), ("alog", 16)):
    _PC[_n] = (_off, _w)
    _off += _w
NPAR = _off
NPARTOT = NL * NPAR + 8


class Sched:
    ENGS = ("pe", "act", "dve", "pool", "sp")

    def __init__(self, nc, es):
        self.nc = nc
        self.es = es
        self.sem = {e: es.enter_context(nc.semaphore("s_" + e)) for e in self.ENGS}
        self.cnt = {e: 0 for e in self.ENGS}
        self.prog = {e: [] for e in self.ENGS}
        self.waited = {e: {} for e in self.ENGS}
        self.last_w = {}
        self.readers = {}
        self.dsem = {}

    def _handle(self, key):
        return self.sem[key] if key in self.sem else self.dsem[key][0]

    def _deps(self, eng, reads, writes, extra):
        deps = {}

        def add(tok, kind):
            if tok is None:
                return
            k, v = tok
            if k == eng and kind == "war":
                return
            if v > deps.get(k, 0):
                deps[k] = v

        for r in reads:
            add(self.last_w.get(r), "raw")
        for w in writes:
            add(self.last_w.get(w), "waw")
            for t in self.readers.get(w, ()):
                add(t, "war")
        for t in extra:
            add(t, "raw")
        out = []
        for k, v in deps.items():
            if v > self.waited[eng].get(k, 0):
                self.waited[eng][k] = v
                out.append((k, v))
        return out

    def _record(self, tok, reads, writes):
        for r in reads:
            lst = self.readers.setdefault(r, [])
            lst[:] = [t for t in lst if t[0] != tok[0]]
            lst.append(tok)
        for w in writes:
            self.last_w[w] = tok
            self.readers[w] = []

    MPREF = ("AV", "UA", "CA", "TG", "UC", "CC", "PTMP", "MEAN", "VAR", "RSTC", "TN", "DTP", "DTE", "DTT", "DAA",
             "E32", "CDE", "NAC", "SCXW", "ACS", "XW", "BTOK", "LT", "MT", "EB", "T1", "ZT", "FA", "UF", "CF", "GS", "YF")

    def op(self, eng, fn, reads=(), writes=(), extra=()):
        reads = list(reads)
        writes = list(writes)
        if "M" not in writes and "M" not in reads:
            if any(k.startswith(p) for k in reads + writes for p in self.MPREF):
                reads.append("M")
        for b in sorted({k[2] for k in reads + writes if k.startswith("ps")}):
            writes.append("bk" + b)
        waits = self._deps(eng, reads, writes, extra)
        self.cnt[eng] += 1
        tok = (eng, self.cnt[eng])
        sem = self.sem[eng]
        hw = [(self._handle(k), v) for k, v in waits]

        def thunk(e):
            for h, v in hw:
                e.wait_ge(h, v)
            fn(e).then_inc(sem, 1)

        self.prog[eng].append(thunk)
        self._record(tok, reads, writes)
        if not hasattr(self, "log"):
            self.log = []
        self.log.append((tok, waits, list(reads), list(writes)))
        return tok

    def dma(self, eng, semname, fn, reads=(), writes=(), extra=()):
        if semname not in self.dsem:
            self.dsem[semname] = [self.es.enter_context(self.nc.semaphore("d_" + semname)), 0]
        ent = self.dsem[semname]
        prev = (semname, ent[1] * 16) if ent[1] > 0 else None
        ex = list(extra) + ([prev] if prev else [])
        waits = self._deps(eng, reads, writes, ex)
        ent[1] += 1
        tok = (semname, ent[1] * 16)
        hw = [(self._handle(k), v) for k, v in waits]
        h = ent[0]

        def thunk(e):
            for hh, v in hw:
                e.wait_ge(hh, v)
            fn(e).then_inc(h, 16)

        self.prog[eng].append(thunk)
        self._record(tok, reads, writes)
        return tok

    def final_wait(self, eng):
        hw = [(v[0], v[1] * 16) for v in self.dsem.values()]

        def thunk(e):
            for hh, v in hw:
                e.wait_ge(hh, v)

        self.prog[eng].append(thunk)

    def emit(self, block):
        prog = self.prog

        @block.tensor
        def _(e):
            for t in prog["pe"]:
                t(e)

        @block.scalar
        def _(e):
            for t in prog["act"]:
                t(e)

        @block.vector
        def _(e):
            for t in prog["dve"]:
                t(e)

        @block.gpsimd
        def _(e):
            for t in prog["pool"]:
                t(e)

        @block.sync
        def _(e):
            for t in prog["sp"]:
                t(e)


class Arena:
    def __init__(self, ap, total):
        self.ap = ap
        self.total = total
        self.off = 0

    def f32(self, n):
        a = self.ap[:, self.off:self.off + n]
        self.off += n
        assert self.off <= self.total, ("arena overflow", self.off, self.total)
        return a

    def bf16(self, n):
        assert n % 2 == 0
        return self.f32(n // 2).bitcast(BF16)


def v3(ap, a):
    return ap.rearrange("p (a b) -> p a b", a=a)


DBG = {"layers": NL, "phase": 99, "casts": True, "ssdcut": 99}
HENG = "act"


def build_program(n_ptiles=8, with_sample=True):
    nc = bass.Bass("TRN2", target_bir_lowering=False)
    NT = n_ptiles
    SEQ = NT * TP

    def din(name, shape, dt=F32):
        return nc.dram_tensor(name, shape, dt, kind="ExternalInput").ap()

    def dout(name, shape):
        return nc.dram_tensor(name, shape, F32, kind="ExternalOutput").ap()

    def dint(name, shape, dt):
        return nc.dram_tensor(name, shape, dt, kind="Internal").ap()

    xp = din("xp", [SEQ, D])
    xs = din("xs", [128, D])
    sta = din("sta", [NL * 4 * 2, 512])
    sts = din("sts", [NL * 4 * 1024, 128])
    stb = din("stb", [NL * 4 * 3, 1536])
    stc = din("stc", [NL * 4 * 30, 512])
    stf = din("stf", [NL * 4 * 2, DFF])
    win = din("win", [NL * D, DIN])
    wout = din("wout", [NL * 2048, D])
    wup = din("wup", [NL * D, 2 * DFF])
    wdn = din("wdn", [NL * DFF, D])
    par = din("par", [128, NPARTOT])

    yp = dout("yp", [SEQ, D])
    ys = dout("ys", [128, D])
    o_pa = dout("pa", [NL * 2, 512])
    o_pss = dout("pss", [NL * 1024, 128])
    o_pb = dout("pb", [NL * 3, 1536])
    o_pc = dout("pc", [NL * 30, 512])
    o_pf = dout("pf", [NL * 2, DFF])
    o_sa = dout("sa", [NL * 4 * 2, 512])
    o_sss = dout("sss", [NL * 4 * 1024, 128])
    o_sb = dout("sb", [NL * 4 * 3, 1536])
    o_sc = dout("sc", [NL * 4 * 30, 512])
    o_sf = dout("sf", [NL * 4 * 2, DFF])

    s_win = dint("s_win", [NL * 10, 128, 8 * 512], BF16)
    s_wdt = dint("s_wdt", [NL, 128, 8 * 16], BF16)
    s_wout = dint("s_wout", [NL * 4, 128, 16 * 256], BF16)
    s_wup = dint("s_wup", [NL * 11, 128, 8 * 512], BF16)
    s_wdn = dint("s_wdn", [NL * 8, 128, 22 * 128], BF16)

    es = ExitStack()
    with es:
        S = Sched(nc, es)
        ARENA_WORDS = 53200
        arena_t = es.enter_context(nc.sbuf_tensor("arena", [128, ARENA_WORDS], F32))
        psum = es.enter_context(nc.psum_tensor("psum", [128, 4096], F32))
        A = Arena(arena_t, ARENA_WORDS)

        def bank(b):
            return psum[:, b * 512:(b + 1) * 512]

        XT = A.f32(8 * TP)
        XT3 = v3(XT, 8)
        HB = A.bf16(8 * TP)
        HB3 = v3(HB, 8)
        WSLOT = [A.f32(SLOTW).bitcast(BF16) for _ in range(NSLOT)]
        XIN = [A.f32(1024) for _ in range(2)]
        PAR = A.f32(NPARTOT)
        IDENT = A.f32(128)
        IDENTB = A.bf16(128)
        TRI_I = A.f32(128)
        TRI_G = A.f32(128)
        ONES = A.f32(128)
        ONESN = A.f32(128)
        ONES5 = A.f32(128)
        NEGM128 = A.f32(512)
        NEGM32 = A.f32(128)
        SEL2 = A.f32(1024)
        NEGHALF = A.f32(TP)
        ABC = A.f32(NL * 16)
        CCWH = A.f32(NL * 124)
        WDT = [A.bf16(128) for _ in range(NL)]
        SP_ = [A.f32(1024) for _ in range(NL)]
        SSMP = A.f32(1024)
        SBFP = [A.bf16(1024) for _ in range(NL)]
        SBFS = A.bf16(1024)
        XDTP = A.bf16(16 * 128)
        HIST_W = {"a": (4, 2), "b": (12, 3), "c": (4, 30), "f": (22, 2)}
        HISTP = {k: [A.f32(nch * H) for _ in range(NL)] for k, (nch, H) in HIST_W.items()}
        HISTS = {k: A.f32(nch * 4 * H) for k, (nch, H) in HIST_W.items()}
        YCAT = A.bf16(16 * TP)
        YC3 = v3(YCAT, 16)
        SQ = [A.f32(TP) for _ in range(2)]
        STAT = A.f32(TP)
        RSTD = A.f32(TP)
        XS = A.f32(8 * TP)
        XS3 = v3(XS, 8)
        BT = A.bf16(2 * TP)
        BT3 = v3(BT, 2)
        CT = A.bf16(2 * TP)
        CT3 = v3(CT, 2)
        UBt = [A.f32(TP + 4 * 3) for _ in range(3)]
        CBo = [A.f32(TP) for _ in range(2)]
        STG = XIN[0][:, 0:512]
        STG2 = XIN[1]
        m_base = A.off
        print("m_base", m_base)
        AVt = [A.f32(TP) for _ in range(2)]
        UAt = [A.f32(TP + 4 * 2) for _ in range(2)]
        CAt = [A.f32(TP) for _ in range(2)]
        TG = [A.f32(TP) for _ in range(2)]
        UCt = [A.f32(TP + 4 * 30) for _ in range(2)]
        CC = A.f32(4 * TP)
        CC3 = v3(CC, 4)
        PTMP = A.f32(TP)
        MEAN = A.f32(TP)
        VAR = A.f32(TP)
        RSTC = A.f32(TP)
        TN = [A.f32(TP) for _ in range(2)]
        m_end_ac = A.off
        A.off = m_base
        DTP = A.f32(64)
        DTE = A.f32(64)
        DTT = A.f32(64)
        DAA = A.f32(64)
        E32 = A.f32(32)
        CDE = A.f32(16)
        CDEb = A.f32(16)
        NAC = A.f32(16)
        SCXW = A.f32(16)
        ACS = A.f32(128)
        XW = A.bf16(1024)
        BTOK = A.bf16(256)
        LT = [A.f32(4 * 128) for _ in range(2)]
        MT = A.bf16(16 * 128)
        EB = A.f32(1024)
        T1 = A.f32(1024)
        ZT = [A.f32(TP) for _ in range(2)]
        m_end_ssd = A.off
        A.off = m_base
        FACT = A.bf16(22 * TP)
        FA3 = v3(FACT, 22)
        UFt = [A.f32(TP + 4 * 2) for _ in range(3)]
        CF_base = A.off
        CFt = [A.f32(TP) for _ in range(2)]
        GS = [A.f32(TP) for _ in range(2)]
        YFt = [arena_t[:, CF_base:CF_base + 1024], arena_t[:, CF_base + 1024:CF_base + 2048]]
        m_end_ffn = A.off
        A.off = max(m_end_ac, m_end_ssd, m_end_ffn)
        print("arena words used", A.off, "of", ARENA_WORDS, "(AC %d SSD %d FFN %d)" % (
            m_end_ac - m_base, m_end_ssd - m_base, m_end_ffn - m_base))

        def pcol(l, name, j=0, w=1):
            o, _ = _PC[name]
            return PAR[:, l * NPAR + o + j: l * NPAR + o + j + w]

        S.dma("sp", "par", lambda e: e.dma_start(out=PAR, in_=par), writes=["PAR"])

        def consts(e):
            e.memset(arena_t[:, m_base:A.off], 0.0)
            e.memset(TRI_I, 1.0)
            e.affine_select(out=TRI_I, in_=TRI_I, compare_op=ALU.is_ge, fill=0.0, base=0,
                            pattern=[[1, 128]], channel_multiplier=-1)
            e.memset(TRI_G, 1.0)
            e.affine_select(out=TRI_G, in_=TRI_G, compare_op=ALU.is_gt, fill=0.0, base=0,
                            pattern=[[-1, 128]], channel_multiplier=1)
            e.memset(IDENT, 0.0)
            e.affine_select(out=IDENT, in_=IDENT, compare_op=ALU.not_equal, fill=1.0, base=0,
                            pattern=[[-1, 128]], channel_multiplier=1)
            e.memset(ONES, 1.0)
            e.memset(ONESN, 1.0 / 1024.0)
            e.memset(ONES5, 1.0 / 512.0)
            e.memset(NEGHALF, -0.5)
            e.memset(NEGM128, 0.0)
            for i in range(4):
                e.affine_select(out=NEGM128[:, i * 128:(i + 1) * 128], in_=NEGM128[:, i * 128:(i + 1) * 128],
                                compare_op=ALU.is_ge, fill=-30000.0, base=0, pattern=[[1, 128]],
                                channel_multiplier=-1)
            e.memset(NEGM32, 0.0)
            for i in range(4):
                e.affine_select(out=NEGM32[:, i * 32:(i + 1) * 32], in_=NEGM32[:, i * 32:(i + 1) * 32],
                                compare_op=ALU.is_ge, fill=-30000.0, base=0, pattern=[[1, 32]],
                                channel_multiplier=-1)
            e.memset(SEL2, 1.0)
            s4 = SEL2.rearrange("p (c t m) -> p c t m", c=8, t=2)
            e.affine_select(out=s4, in_=s4, compare_op=ALU.is_equal, fill=0.0, base=0,
                            pattern=[[2, 8], [1, 2], [0, 64]], channel_multiplier=-1)
            for k in HISTP:
                for l in range(NL):
                    e.memset(HISTP[k][l], 0.0)
            for l in range(NL):
                e.memset(SP_[l], 0.0)
            e.memset(XDTP, 0.0)
            for l in range(NL):
                e.memset(SBFP[l], 0.0)
            return e.memset(SBFS, 0.0)

        S.op("pool", consts, writes=["CONST", "M", "XDTP", "SBFS_0", "SBFS_1"] + ["HP%s%d" % (k, l) for k in "abcf" for l in range(NL)]
             + ["SP%d_%d" % (l, g) for l in range(NL) for g in range(2)] + ["SBFP%d_%d" % (l, g) for l in range(NL) for g in range(2)])
        S.op("dve", lambda e: e.tensor_copy(out=IDENTB, in_=IDENT), reads=["CONST"], writes=["IDENTB"])

        def abc_fn(e):
            for l in range(NL):
                e.activation(out=ABC[:, l * 16:(l + 1) * 16], in_=pcol(l, "alog", 0, 16), func=AF.Exp)
            return e.mul(ABC, ABC, -1.0)

        S.op("act", abc_fn, reads=["PAR"], writes=["ABC"])

        def ccwh_fn(e):
            i = None
            for l in range(NL):
                i = e.tensor_scalar(out=CCWH[:, l * 124:(l + 1) * 124], in0=pcol(l, "ccw", 0, 124),
                                    scalar1=0.5, scalar2=None, op0=ALU.mult)
            return i

        S.op("dve", ccwh_fn, reads=["PAR"], writes=["CCWH"])

        cast_i = [0]

        scr_keys = {}

        import os as _os
        _sel = _os.environ.get("CASTSEL", "win,wdt,wout,wup,wdn").split(",")

        def cast(dst, src, key):
            if not any(key.startswith("scr_" + q) for q in _sel):
                scr_keys.setdefault(key, [])
                return
            n = cast_i[0] % 6
            cast_i[0] += 1
            sub = key + "#%d" % len(scr_keys.setdefault(key, []))
            scr_keys[key].append(sub)
            S.dma("pool", "cast%d" % n, lambda e: e.dma_start(out=dst, in_=src), writes=[sub])

        def win_src(l, c0, w):
            return win[l * D:(l + 1) * D, c0:c0 + w].rearrange("(kc p) e -> p kc e", p=128)

        ordA = []
        for c in range(4):
            ordA += [0 + c * 128, 1024 + c * 128, 512 + c * 128]
        ordC = []
        for c in range(4):
            ordC += [4112 + 512 + c * 128, 4112 + c * 128]
        ordB = [2560 + j * 128 for j in range(12)]
        ordZ = [1536 + j * 128 for j in range(8)]
        win_order = ordA + ordC + ordB + ordZ

        def emit_casts(l):
            for si in range(10):
                cols = win_order[si * 4:(si + 1) * 4]
                dst = s_win[l * 10 + si].rearrange("p (kc e) -> p kc e", kc=8)
                if cols[3] - cols[0] == 384 and cols[1] - cols[0] == 128:
                    cast(dst, win_src(l, cols[0], 512), "scr_win%d_%d" % (l, si))
                else:
                    for j, c0 in enumerate(cols):
                        cast(dst[:, :, j * 128:(j + 1) * 128], win_src(l, c0, 128), "scr_win%d_%d" % (l, si))
            cast(s_wdt[l].rearrange("p (kc e) -> p kc e", kc=8), win_src(l, 4096, 16), "scr_wdt%d" % l)
            for si in range(4):
                cast(s_wout[l * 4 + si].rearrange("p (kc e) -> p kc e", kc=16),
                     wout[l * 2048:(l + 1) * 2048, si * 256:(si + 1) * 256].rearrange("(kc p) e -> p kc e", p=128),
                     "scr_wout%d_%d" % (l, si))
            for si in range(11):
                dst = s_wup[l * 11 + si].rearrange("p (kc j t e) -> p kc j t e", kc=8, j=2, t=2)
                for t, base in ((0, DFF), (1, 0)):
                    for j in range(2):
                        c0 = base + si * 256 + j * 128
                        src = wup[l * D:(l + 1) * D, c0:c0 + 128].rearrange("(kc p) e -> p kc e", p=128)
                        cast(dst[:, :, j, t, :], src, "scr_wup%d_%d" % (l, si))
            for si in range(8):
                cast(s_wdn[l * 8 + si].rearrange("p (kc e) -> p kc e", kc=22),
                     wdn[l * DFF:(l + 1) * DFF, si * 128:(si + 1) * 128].rearrange("(kc p) e -> p kc e", p=128),
                     "scr_wdn%d_%d" % (l, si))

        for l in range(NL if DBG["casts"] else 0):
            emit_casts(l)
        for l in range(NL if DBG["casts"] else 0):
            S.dma("sp", "wdt", lambda e, l=l: e.dma_start(out=WDT[l], in_=s_wdt[l]),
                  reads=scr_keys["scr_wdt%d" % l], writes=["WDT%d" % l])

        def layer_slabs(l):
            seq = []
            for si in range(10):
                seq.append(("win", l, si))
            for si in range(4):
                seq.append(("wout", l, si))
            for si in range(11):
                seq.append(("wup", l, si))
            for si in range(8):
                seq.append(("wdn", l, si))
            return seq

        ntiles_total = NT + (1 if with_sample else 0)
        wseq = []
        for _t in range(ntiles_total):
            for l in range(NL):
                wseq += layer_slabs(l)
        wstate = {"next": 0, "use": 0}

        def slab_src(kind, l, si):
            if kind == "win":
                return s_win[l * 10 + si], 4096, "scr_win%d_%d" % (l, si)
            if kind == "wout":
                return s_wout[l * 4 + si], 4096, "scr_wout%d_%d" % (l, si)
            if kind == "wup":
                return s_wup[l * 11 + si], 4096, "scr_wup%d_%d" % (l, si)
            return s_wdn[l * 8 + si], 22 * 128, "scr_wdn%d_%d" % (l, si)

        def issue_load(i):
            kind, l, si = wseq[i]
            src, n, key = slab_src(kind, l, si)
            slot = i % NSLOT
            S.dma("sp", "w%d" % slot, lambda e: e.dma_start(out=WSLOT[slot][:, 0:n], in_=src),
                  reads=scr_keys[key], writes=["W%d" % slot])

        def get_slab(kind, l, si):
            i = wstate["use"]
            while wseq[i] != (kind, l, si):
                assert DBG["phase"] < 99 or DBG["layers"] < NL
                i += 1
            wstate["use"] = i + 1
            while wstate["next"] < min(len(wseq), i + NSLOT):
                issue_load(wstate["next"])
                wstate["next"] += 1
            slot = i % NSLOT
            return WSLOT[slot], "W%d" % slot

        mmrr = [0]

        def next_bank():
            b = mmrr[0] % 4
            mmrr[0] += 1
            return b

        FENCE = A.f32(2)
        wcur = {}

        def wchunk(l, gi):
            si, jj = divmod(gi, 4)
            if jj == 0:
                wcur["s"] = get_slab("win", l, si)
            return wcur["s"][0], wcur["s"][1], jj

        def stat_chunk(T, c, src3, keys):
            sq = SQ[c % 2]
            S.op("act", lambda e: e.activation(out=sq[:, 0:T], in_=src3[:, c, 0:T], func=AF.Square),
                 reads=keys, writes=["SQ%d" % (c % 2)])
            S.op("pe", lambda e: e.matmul(bank(4)[:, 0:T], lhsT=ONESN, rhs=sq[:, 0:T], start=(c == 0), stop=(c == 7)),
                 reads=["SQ%d" % (c % 2), "CONST"], writes=["ps4"])

        def rmsnorm(T, gcol, out_fn, xkeys_r, outkeys, eps, src3=None, stats_done=False):
            src3 = XT3 if src3 is None else src3
            for c in range(0 if stats_done else 8):
                stat_chunk(T, c, src3, xkeys_r(c))
            S.op("dve", lambda e: e.tensor_scalar(out=STAT[:, 0:T], in0=bank(4)[:, 0:T], scalar1=eps, scalar2=None,
                                                  op0=ALU.add), reads=["ps4"], writes=["STAT"])
            def rs_fn(e):
                e.activation(out=RSTD[:, 0:T], in_=STAT[:, 0:T], func=AF.Ln)
                return e.activation(out=RSTD[:, 0:T], in_=RSTD[:, 0:T], func=AF.Exp, scale=-0.5)

            S.op("act", rs_fn, reads=["STAT"], writes=["RSTD"])
            for c in range(8):
                S.op("dve", lambda e, c=c: e.scalar_tensor_tensor(out=out_fn(c), in0=src3[:, c, 0:T], scalar=gcol(c),
                                                                   in1=RSTD[:, 0:T], op0=ALU.mult, op1=ALU.mult),
                     reads=xkeys_r(c) + ["RSTD", "PAR"], writes=[outkeys(c)])

        def proj(T, slab, slabkey, j, width=512, nk=8, rhs3=None, rkeys=None):
            rhs3 = HB3 if rhs3 is None else rhs3
            rkeys = ["HB"] if rkeys is None else rkeys
            b = next_bank()
            sl3 = slab[:, 0:nk * width].rearrange("p (k e) -> p k e", k=nk)

            def fn(e):
                i = None
                for k in range(nk):
                    i = e.matmul(bank(b)[:, 0:T], lhsT=sl3[:, k, j * 128:(j + 1) * 128], rhs=rhs3[:, k, 0:T],
                                 start=(k == 0), stop=(k == nk - 1))
                return i

            S.op("pe", fn, reads=[slabkey] + rkeys, writes=["ps%d" % b])
            return bank(b)[:, 0:T], "ps%d" % b

        def _cp(e, eng, out, in_):
            if eng == "act":
                return e.activation(out=out, in_=in_, func=AF.Copy)
            return e.tensor_copy(out=out, in_=in_)

        def hist_in(eng, hist, c, nseg, H, U3, ukey, hkey):
            S.op(eng, lambda e: _cp(e, eng, U3[:, :, 0:H],
                                    hist[:, c * nseg * H:(c + 1) * nseg * H].rearrange("p (s h) -> p s h", s=nseg)),
                 reads=[hkey], writes=[ukey + "h"])

        def hist_out(eng, hist, c, nseg, H, L, U3, ukey, hkey):
            S.op(eng, lambda e: _cp(e, eng, hist[:, c * nseg * H:(c + 1) * nseg * H].rearrange("p (s h) -> p s h", s=nseg),
                                    U3[:, :, L:L + H]),
                 reads=[ukey, ukey + "h"], writes=[hkey])

        def state_out(hist, nch, R, dst2d, scale, hkey):
            for c0 in range(0, nch, 4):
                n = min(4, nch - c0)
                b = next_bank()

                def fn(e, c0=c0, n=n, b=b):
                    i = None
                    for j in range(n):
                        i = e.transpose(out=bank(b)[0:R, j * 128:(j + 1) * 128],
                                        in_=hist[:, (c0 + j) * R:(c0 + j + 1) * R], identity=IDENT)
                    return i

                S.op("pe", fn, reads=[hkey, "CONST"], writes=["ps%d" % b])
                S.op("act", lambda e, n=n, b=b: e.activation(out=STG[0:R, 0:n * 128], in_=bank(b)[0:R, 0:n * 128],
                                                             func=AF.Copy, scale=scale),
                     reads=["ps%d" % b], writes=["XIN0"])
                S.dma("act", "stout", lambda e, c0=c0, n=n: e.dma_start(out=dst2d[:, c0 * 128:(c0 + n) * 128],
                                                                         in_=STG[0:R, 0:n * 128]),
                      reads=["XIN0"])

        def state_in(hist, nch, R, src2d, scale, hkey):
            for c0 in range(0, nch, 4):
                n = min(4, nch - c0)
                S.dma("sp", "stin", lambda e, c0=c0, n=n: e.dma_start(out=STG[0:R, 0:n * 128],
                                                                       in_=src2d[:, c0 * 128:(c0 + n) * 128]),
                      writes=["XIN0"])
                b = next_bank()

                def fn(e, n=n, b=b):
                    i = None
                    for j in range(n):
                        i = e.transpose(out=bank(b)[:, j * R:(j + 1) * R], in_=STG[0:R, j * 128:(j + 1) * 128],
                                        identity=IDENT[0:R, 0:R])
                    return i

                S.op("pe", fn, reads=["XIN0", "CONST"], writes=["ps%d" % b])
                S.op("act", lambda e, c0=c0, n=n, b=b: e.activation(out=hist[:, c0 * R:(c0 + n) * R],
                                                                     in_=bank(b)[:, 0:n * R], func=AF.Copy, scale=scale),
                     reads=["ps%d" % b], writes=[hkey])

        def ssm_in(Sbuf, skey, src2d, SBF, bk):
            S.dma("sp", "ssin", lambda e: e.dma_start(out=v3(STG2, 8), in_=src2d.rearrange("(c q) n -> q c n", q=128)),
                  writes=["XIN1"])
            for hf in range(2):
                b = next_bank()

                def fn(e, hf=hf, b=b):
                    i = None
                    for j in range(4):
                        c = hf * 4 + j
                        i = e.transpose(out=bank(b)[:, j * 128:(j + 1) * 128], in_=STG2[:, c * 128:(c + 1) * 128],
                                        identity=IDENT)
                    return i

                S.op("pe", fn, reads=["XIN1", "CONST"], writes=["ps%d" % b])
                S.op("act", lambda e, hf=hf, b=b: e.activation(out=Sbuf[:, hf * 512:(hf + 1) * 512], in_=bank(b),
                                                               func=AF.Copy), reads=["ps%d" % b], writes=[skey + "_%d" % hf])
                S.op("dve", lambda e, hf=hf, b=b: e.tensor_copy(out=SBF[:, hf * 512:(hf + 1) * 512], in_=bank(b)),
                     reads=["ps%d" % b], writes=[bk + "_%d" % hf])

        def ssm_out(Sbuf, skey, dst2d):
            for hf in range(2):
                b = next_bank()

                def fn(e, hf=hf, b=b):
                    i = None
                    for j in range(4):
                        c = hf * 4 + j
                        i = e.transpose(out=bank(b)[:, j * 128:(j + 1) * 128], in_=Sbuf[:, c * 128:(c + 1) * 128],
                                        identity=IDENT)
                    return i

                S.op("pe", fn, reads=[skey + "_%d" % hf, "CONST"], writes=["ps%d" % b])
                S.op("act", lambda e, hf=hf, b=b: e.activation(out=STG2[:, hf * 512:(hf + 1) * 512], in_=bank(b),
                                                               func=AF.Copy), reads=["ps%d" % b], writes=["XIN1"])
                S.dma("act", "ssout", lambda e, hf=hf: e.dma_start(
                    out=dst2d.rearrange("(c q) n -> q c n", q=128)[:, hf * 4:(hf + 1) * 4, :],
                    in_=STG2[:, hf * 512:(hf + 1) * 512].rearrange("q (c n) -> q c n", c=4)), reads=["XIN1"])

        def layer(l, T, nseg, sample, last):
            L = T // nseg
            Q = min(128, L)
            nck = T // Q
            hist = {k: (HISTS[k] if sample else HISTP[k][l]) for k in HIST_W}
            hkey = {k: ("HS" + k if sample else "HP%s%d" % (k, l)) for k in HIST_W}

            def seg3(ap):
                return ap.rearrange("p (s t) -> p s t", s=nseg)

            if sample:
                state_in(hist["a"], 4, 8, sta[l * 8:(l + 1) * 8, :], 1.0, hkey["a"])
                state_in(hist["b"], 12, 12, stb[l * 12:(l + 1) * 12, :], 1.0, hkey["b"])
                state_in(hist["c"], 4, 120, stc[l * 120:(l + 1) * 120, :], 2.0, hkey["c"])
                state_in(hist["f"], 22, 8, stf[l * 8:(l + 1) * 8, :], 1.0, hkey["f"])

            S.op("dve", lambda e: e.memset(FENCE, 0.0), reads=["M"], writes=["M"])
            rmsnorm(T, lambda c: pcol(l, "g1", c), lambda c: HB3[:, c, 0:T], lambda c: ["XT%d" % c],
                    lambda c: "HB", EPS)

            if DBG['phase'] < 2:
                return
            for c in range(4):
                slab, skey, jj = wchunk(l, c * 3)
                pv, pvk = proj(T, slab, skey, jj)
                av = AVt[c % 2]
                S.op("act", lambda e, av=av, pv=pv: e.activation(out=av[:, 0:T], in_=pv, func=AF.Copy),
                     reads=[pvk, "M"], writes=["AV%d" % (c % 2)])
                slab, skey, jj = wchunk(l, c * 3 + 1)
                pc_, pck = proj(T, slab, skey, jj)
                ua = UAt[c % 2]
                U3 = ua[:, 0:nseg * (L + 2)].rearrange("p (s t) -> p s t", s=nseg)
                uk = "UA%d" % (c % 2)
                hist_in(HENG, hist["a"], c, nseg, 2, U3, uk, hkey["a"])
                S.op("dve", lambda e, U3=U3, pc_=pc_, av=av: e.tensor_tensor(out=U3[:, :, 2:2 + L], in0=seg3(pc_),
                                                                              in1=seg3(av[:, 0:T]), op=ALU.mult),
                     reads=[pck, "AV%d" % (c % 2), "M"], writes=[uk])
                hist_out(HENG, hist["a"], c, nseg, 2, L, U3, uk, hkey["a"])
                ca = CAt[c % 2]

                def convA(e, U3=U3, ca=ca, c=c):
                    o = seg3(ca[:, 0:T])
                    e.tensor_scalar(out=o, in0=U3[:, :, 0:L], scalar1=pcol(l, "caw", c * 3 + 0), scalar2=None, op0=ALU.mult)
                    e.scalar_tensor_tensor(out=o, in0=U3[:, :, 1:1 + L], scalar=pcol(l, "caw", c * 3 + 1), in1=o,
                                           op0=ALU.mult, op1=ALU.add)
                    return e.scalar_tensor_tensor(out=o, in0=U3[:, :, 2:2 + L], scalar=pcol(l, "caw", c * 3 + 2), in1=o,
                                                  op0=ALU.mult, op1=ALU.add)

                S.op("dve", convA, reads=[uk, uk + "h", "PAR", "M"], writes=["CA%d" % (c % 2)])
                slab, skey, jj = wchunk(l, c * 3 + 2)
                pb_, pbk = proj(T, slab, skey, jj)
                S.op("dve", lambda e, pb_=pb_, ca=ca, c=c: e.tensor_tensor(out=YC3[:, c, 0:T], in0=pb_, in1=ca[:, 0:T],
                                                                           op=ALU.mult),
                     reads=[pbk, "CA%d" % (c % 2)], writes=["YC%d" % c])
            if sample or last:
                R = nseg * 2
                dst = (o_sa if sample else o_pa)[l * R:(l + 1) * R, :]
                state_out(hist["a"], 4, R, dst, 1.0, hkey["a"])

            if DBG['phase'] < 3:
                return
            for c in range(4):
                slab, skey, jj = wchunk(l, 12 + c * 2)
                pg, pgk = proj(T, slab, skey, jj)
                tg = TG[c % 2]
                S.op("act", lambda e, tg=tg, pg=pg: e.activation(out=tg[:, 0:T], in_=pg, func=AF.Tanh, scale=0.5),
                     reads=[pgk, "M"], writes=["TG%d" % (c % 2)])
                slab, skey, jj = wchunk(l, 12 + c * 2 + 1)
                pa_, pak = proj(T, slab, skey, jj)
                uc = UCt[c % 2]
                U3 = uc[:, 0:nseg * (L + 30)].rearrange("p (s t) -> p s t", s=nseg)
                uk = "UC%d" % (c % 2)
                hist_in(HENG, hist["c"], c, nseg, 30, U3, uk, hkey["c"])
                S.op("dve", lambda e, U3=U3, tg=tg, pa_=pa_: e.scalar_tensor_tensor(
                    out=U3[:, :, 30:30 + L], in0=seg3(tg[:, 0:T]), scalar=1.0, in1=seg3(pa_), op0=ALU.add, op1=ALU.mult),
                     reads=[pak, "TG%d" % (c % 2), "M"], writes=[uk])
                hist_out(HENG, hist["c"], c, nseg, 30, L, U3, uk, hkey["c"])

                def convC(e, U3=U3, c=c):
                    o = seg3(CC3[:, c, 0:T])
                    tmp = seg3(PTMP[:, 0:T])
                    wh = CCWH[:, l * 124 + c * 31: l * 124 + (c + 1) * 31]
                    i = e.tensor_scalar(out=o, in0=U3[:, :, 0:L], scalar1=wh[:, 0:1], scalar2=pcol(l, "ccb", c),
                                        op0=ALU.mult, op1=ALU.add)
                    for k in range(1, 31):
                        i = e.scalar_tensor_tensor(out=o, in0=U3[:, :, k:k + L], scalar=wh[:, k:k + 1], in1=o,
                                                   op0=ALU.mult, op1=ALU.add)
                    return i

                S.op("dve", convC, reads=[uk, uk + "h", "CCWH", "PAR", "M"], writes=["CC%d" % c])
            if sample or last:
                R = nseg * 30
                dst = (o_sc if sample else o_pc)[l * R:(l + 1) * R, :]
                state_out(hist["c"], 4, R, dst, 0.5, hkey["c"])
            for c in range(4):
                sq = SQ[c % 2]
                S.op("act", lambda e, c=c, sq=sq: e.activation(out=sq[:, 0:T], in_=CC3[:, c, 0:T], func=AF.Square),
                     reads=["CC%d" % c], writes=["SQ%d" % (c % 2)])
                S.op("pe", lambda e, c=c: e.matmul(bank(4)[:, 0:T], lhsT=ONES5, rhs=CC3[:, c, 0:T], start=(c == 0),
                                                   stop=(c == 3)), reads=["CC%d" % c, "CONST"], writes=["ps4"])
                S.op("pe", lambda e, c=c, sq=sq: e.matmul(bank(5)[:, 0:T], lhsT=ONES5, rhs=sq[:, 0:T], start=(c == 0),
                                                          stop=(c == 3)), reads=["SQ%d" % (c % 2), "CONST"], writes=["ps5", "ps5c", "ps5b"])
            S.op("act", lambda e: e.activation(out=MEAN[:, 0:T], in_=bank(4)[:, 0:T], func=AF.Copy),
                 reads=["ps4", "M"], writes=["MEAN"])
            S.op("dve", lambda e: e.tensor_tensor(out=VAR[:, 0:T], in0=MEAN[:, 0:T], in1=MEAN[:, 0:T], op=ALU.mult),
                 reads=["MEAN", "M"], writes=["VAR"])
            S.op("dve", lambda e: e.scalar_tensor_tensor(out=VAR[:, 0:T], in0=bank(5)[:, 0:T], scalar=EPS, in1=VAR[:, 0:T],
                                                         op0=ALU.add, op1=ALU.subtract),
                 reads=["ps5", "VAR"], writes=["VAR"])
            def rsc_fn(e):
                e.activation(out=RSTC[:, 0:T], in_=VAR[:, 0:T], func=AF.Ln)
                return e.activation(out=RSTC[:, 0:T], in_=RSTC[:, 0:T], func=AF.Exp, scale=-0.5)

            S.op("act", rsc_fn, reads=["VAR", "M"], writes=["RSTC"])
            for c in range(4):
                tn = TN[c % 2]

                def lnn(e, c=c, tn=tn):
                    e.tensor_tensor(out=tn[:, 0:T], in0=CC3[:, c, 0:T], in1=MEAN[:, 0:T], op=ALU.subtract)
                    return e.tensor_tensor(out=tn[:, 0:T], in0=tn[:, 0:T], in1=RSTC[:, 0:T], op=ALU.mult)

                S.op("dve", lnn, reads=["CC%d" % c, "MEAN", "RSTC", "M"], writes=["TN%d" % (c % 2)])
                S.op("act", lambda e, c=c, tn=tn: e.activation(out=YC3[:, 12 + c, 0:T], in_=tn[:, 0:T], func=AF.Silu,
                                                               scale=pcol(l, "lng", c), bias=pcol(l, "lnb", c)),
                     reads=["TN%d" % (c % 2), "PAR"], writes=["YC%d" % (12 + c)])

            if DBG['phase'] < 4:
                return
            pend_b = []

            def flush_b():
                while pend_b:
                    dst_, co_, j_, dk_ = pend_b.pop(0)
                    S.op("act", lambda e, dst_=dst_, co_=co_: e.activation(out=dst_, in_=co_[:, 0:T], func=AF.Silu),
                         reads=["CBo%d" % (j_ % 2)], writes=dk_)

            for j in range(12):
                slab, skey, jj = wchunk(l, 20 + j)
                px, pxk = proj(T, slab, skey, jj)
                ub = UBt[j % 3]
                U3 = ub[:, 0:nseg * (L + 3)].rearrange("p (s t) -> p s t", s=nseg)
                uk = "UB%d" % (j % 3)
                hist_in(HENG, hist["b"], j, nseg, 3, U3, uk, hkey["b"])
                S.op("act", lambda e, U3=U3, px=px: e.activation(out=U3[:, :, 3:3 + L], in_=seg3(px), func=AF.Copy),
                     reads=[pxk], writes=[uk])
                flush_b()
                hist_out(HENG, hist["b"], j, nseg, 3, L, U3, uk, hkey["b"])
                co = CBo[j % 2]

                def convB(e, U3=U3, co=co, j=j):
                    o = seg3(co[:, 0:T])
                    e.tensor_scalar(out=o, in0=U3[:, :, 0:L], scalar1=pcol(l, "cbw", j * 4), scalar2=pcol(l, "cbb", j),
                                    op0=ALU.mult, op1=ALU.add)
                    i = None
                    for k in range(1, 4):
                        i = e.scalar_tensor_tensor(out=o, in0=U3[:, :, k:k + L], scalar=pcol(l, "cbw", j * 4 + k), in1=o,
                                                   op0=ALU.mult, op1=ALU.add)
                    return i

                S.op("dve", convB, reads=[uk, uk + "h", "PAR"], writes=["CBo%d" % (j % 2)])
                if j < 8:
                    dst, dk = XS3[:, j, 0:T], ["XS%d_%d" % (j, ci) for ci in range(nck)]
                elif j < 10:
                    dst, dk = BT3[:, j - 8, 0:T], ["BT%d" % (j - 8)]
                else:
                    dst, dk = CT3[:, j - 10, 0:T], ["CT%d" % (j - 10)]
                pend_b.append((dst, co, j, dk))
            flush_b()
            if sample or last:
                R = nseg * 3
                dst = (o_sb if sample else o_pb)[l * R:(l + 1) * R, :]
                state_out(hist["b"], 12, R, dst, 1.0, hkey["b"])

            if DBG['phase'] < 5:
                return
            S.op("dve", lambda e: e.memset(FENCE, 0.0), reads=["M"], writes=["M"])
            b5 = bank(5)
            for ci in range(nck if DBG.get("dtv", 9) >= 1 else 0):
                def dtmm(e, ci=ci):
                    i = None
                    w3 = WDT[l].rearrange("p (k e) -> p k e", k=8)
                    for k in range(8):
                        i = e.matmul(b5[0:Q, ci * 16:(ci + 1) * 16], lhsT=HB3[:, k, ci * Q:(ci + 1) * Q], rhs=w3[:, k, :],
                                     start=(k == 0), stop=(k == 7))
                    return i
                S.op("pe", dtmm, reads=["HB", "WDT%d" % l], writes=["ps5"])
            nd = nck * 16
            if DBG.get("dtv", 9) < 2:
                return
            S.op("dve", lambda e: e.tensor_tensor(out=DTP[0:Q, 0:nd].rearrange("p (c h) -> p c h", c=nck),
                                                  in0=b5[0:Q, 0:nd].rearrange("p (c h) -> p c h", c=nck),
                                                  in1=pcol(l, "dtb", 0, 16)[0:Q].unsqueeze(1).to_broadcast([Q, nck, 16]),
                                                  op=ALU.add), reads=["ps5", "PAR", "M"], writes=["DTP"])
            if DBG.get("dtv", 9) < 3:
                return
            if DBG.get("dtact", 3) & 1:
                S.op("act", lambda e: e.activation(out=DTE[0:Q, 0:nd], in_=DTP[0:Q, 0:nd], func=AF.Exp),
                     reads=["DTP", "M"], writes=["DTE"])
            else:
                S.op("dve", lambda e: e.tensor_copy(out=DTE[0:Q, 0:nd], in_=DTP[0:Q, 0:nd]), reads=["DTP", "M"], writes=["DTE"])
            if DBG.get("dtact", 3) & 2:
                S.op("act", lambda e: e.activation(out=DTT[0:Q, 0:nd], in_=DTE[0:Q, 0:nd], func=AF.Ln, bias=ONES[0:Q, 0:1]),
                     reads=["DTE", "CONST", "M"], writes=["DTT"])
            else:
                S.op("dve", lambda e: e.tensor_copy(out=DTT[0:Q, 0:nd], in_=DTE[0:Q, 0:nd]), reads=["DTE", "M"], writes=["DTT"])
            for _ in range(DBG.get("xfence", 0)):
                S.op(DBG.get("xeng", "dve"), lambda e: e.memset(FENCE, 0.0), reads=["M"], writes=["M"])
            if DBG.get("dtv", 9) < 4:
                return
            def daa_fn(e):
                i = None
                for ci_ in range(nck):
                    i = e.tensor_tensor(out=DAA[0:Q, ci_ * 16:(ci_ + 1) * 16], in0=DTT[0:Q, ci_ * 16:(ci_ + 1) * 16],
                                        in1=(ABC[0:Q, l * 16:(l + 1) * 16] if DBG.get("useabc", 1) else pcol(l, "alog", 0, 16)[0:Q]), op=ALU.mult)
                return i

            if DBG.get("useabc", 1) == 2:
                S.op("dve", lambda e: e.tensor_tensor(out=DAA[0:Q, 0:nd], in0=DTT[0:Q, 0:nd], in1=DTE[0:Q, 0:nd], op=ALU.mult),
                     reads=["DTT", "DTE", "M"], writes=["DAA"])
            elif DBG.get("useabc", 1) == 3:
                S.op("dve", lambda e: e.tensor_copy(out=DAA[0:Q, 0:nd], in_=DTT[0:Q, 0:nd]),
                     reads=["DTT", "M"], writes=["DAA"])
            elif DBG.get("useabc", 1) == 4:
                S.op("pool", daa_fn, reads=["DTT", "ABC", "M"], writes=["DAA"])
            else:
                S.op("dve", daa_fn, reads=["DTT", "ABC", "M"], writes=["DAA"])

            if DBG["ssdcut"] < 1:
                return
            Sb = SSMP if sample else SP_[l]
            sk = "SS" if sample else "SP%d" % l
            SBF = SBFS if sample else SBFP[l]
            bk = "SBFS" if sample else "SBFP%d" % l
            negm = NEGM128 if Q == 128 else NEGM32
            p01 = psum[:, 0:1024]
            p67 = psum[:, 6 * 512: 6 * 512 + 8 * Q]
            psB = b5[:, 256:384].bitcast(BF16)
            MT3 = MT.rearrange("p (h q) -> p h q", h=16)
            X3 = XDTP.rearrange("p (h m) -> p h m", h=16)
            CDEs = [CDE, CDEb]
            xsk = lambda ci: ["XS%d_%d" % (c, ci) for c in range(8)]

            def F1(ci):
                dA = DAA[0:Q, ci * 16:(ci + 1) * 16]
                dtc = DTT[0:Q, ci * 16:(ci + 1) * 16]
                cde = CDEs[ci % 2]

                def cums(e):
                    e.matmul(b5[0:Q, 64:80], lhsT=TRI_I[0:Q, 0:Q], rhs=dA, start=True, stop=True)
                    e.matmul(b5[0:Q, 80:96], lhsT=TRI_G[0:Q, 0:Q], rhs=dA, start=True, stop=True)
                    e.matmul(b5[:, 96:112], lhsT=ONES[0:Q, :], rhs=dA, start=True, stop=True)
                    return e.matmul(b5[0:16, 128:128 + Q], lhsT=dA, rhs=TRI_I[0:Q, 0:Q], start=True, stop=True)

                S.op("pe", cums, reads=["DAA", "CONST"], writes=["ps5c"])
                S.op("act", lambda e: e.activation(out=E32[0:Q, :], in_=b5[0:Q, 64:96], func=AF.Exp),
                     reads=["ps5c", "M"], writes=["E32"])
                S.op("act", lambda e: e.activation(out=cde, in_=b5[:, 96:112], func=AF.Exp),
                     reads=["ps5c", "M"], writes=["CDE%d" % (ci % 2)])
                S.op("dve", lambda e: e.tensor_copy(out=ACS[0:16, 0:Q], in_=b5[0:16, 128:128 + Q]),
                     reads=["ps5c", "M"], writes=["ACS"])
                S.op("dve", lambda e: e.tensor_scalar(out=NAC[0:Q, :], in0=b5[0:Q, 64:80], scalar1=-1.0, scalar2=None,
                                                      op0=ALU.mult), reads=["ps5c", "M"], writes=["NAC"])
                S.op("dve", lambda e: e.tensor_tensor(out=SCXW[0:Q, :], in0=dtc, in1=E32[0:Q, 16:32], op=ALU.mult),
                     reads=["DTT", "E32", "M"], writes=["SCXW"])

            def F2(ci):
                q0 = ci * Q
                dtc = DTT[0:Q, ci * 16:(ci + 1) * 16]

                def xtr(e):
                    i = None
                    for c in range(8):
                        i = e.transpose(out=p01[0:Q, c * 128:(c + 1) * 128], in_=XS3[:, c, q0:q0 + Q], identity=IDENT)
                    return i

                S.op("pe", xtr, reads=xsk(ci) + ["CONST"], writes=["ps0", "ps1"])

                def btr(e):
                    i = None
                    for g in range(2):
                        i = e.transpose(out=psB[0:Q, g * 128:(g + 1) * 128], in_=BT3[:, g, q0:q0 + Q], identity=IDENTB)
                    return i

                S.op("pe", btr, reads=["BT0", "BT1", "IDENTB"], writes=["ps5b"])

                def xdtop(e):
                    i = None
                    for hf in range(2):
                        xdo = bass.AP(XDTP.tensor, XDTP.offset + hf * 1024, [list(XDTP.ap[0]), [256, 4], [192, 2], [1, 64]])
                        i = e.tensor_tensor(
                            out=xdo[0:Q], in0=p01[0:Q, hf * 512:(hf + 1) * 512].rearrange("p (c t m) -> p c t m", c=4, t=2),
                            in1=dtc[:, hf * 8:(hf + 1) * 8].rearrange("p (c t) -> p c t", c=4).unsqueeze(3).to_broadcast([Q, 4, 2, 64]),
                            op=ALU.mult)
                    return i

                S.op("dve", xdtop, reads=["ps0", "ps1", "DTT", "M"], writes=["XDTP"])

                def xwop(e):
                    i = None
                    for hf in range(2):
                        i = e.tensor_tensor(
                            out=XW[0:Q, hf * 512:(hf + 1) * 512].rearrange("p (h m) -> p h m", h=8),
                            in0=p01[0:Q, hf * 512:(hf + 1) * 512].rearrange("p (h m) -> p h m", h=8),
                            in1=SCXW[0:Q, hf * 8:(hf + 1) * 8].unsqueeze(2).to_broadcast([Q, 8, 64]), op=ALU.mult)
                    return i

                S.op("dve", xwop, reads=["ps0", "ps1", "SCXW", "M"], writes=["XW"])
                S.op("act", lambda e: e.activation(out=BTOK[0:Q, :], in_=psB[0:Q, :], func=AF.Copy),
                     reads=["ps5b", "M"], writes=["BTOK"])

                def cbt(e):
                    i = None
                    for g in range(2):
                        i = e.matmul(bank(4)[0:Q, g * 128:g * 128 + Q], lhsT=BT3[:, g, q0:q0 + Q], rhs=CT3[:, g, q0:q0 + Q],
                                     start=True, stop=True)
                    return i

                S.op("pe", cbt, reads=["BT0", "BT1", "CT0", "CT1"], writes=["ps4"])

                def ebc(e):
                    i = None
                    for c in range(8):
                        i = e.matmul(p67[:, c * Q:(c + 1) * Q], lhsT=SEL2[0:16, c * 128:(c + 1) * 128], rhs=ACS[0:16, 0:Q],
                                     start=True, stop=True)
                    return i

                S.op("pe", ebc, reads=["ACS", "CONST"], writes=["ps6", "ps7"])

                def ebexp(e):
                    hw_ = 4 * Q
                    e.activation(out=EB[:, 0:hw_], in_=p67[:, 0:hw_], func=AF.Exp)
                    return e.activation(out=EB[:, hw_:2 * hw_], in_=p67[:, hw_:2 * hw_], func=AF.Exp)

                S.op("act", ebexp, reads=["ps6", "ps7", "M"], writes=["EB"])

            def F5(ci):
                dA = DAA[0:Q, ci * 16:(ci + 1) * 16]
                for r in range(4):
                    bb = 2 + r % 2
                    pl = bank(bb)

                    def lmm(e, r=r, pl=pl):
                        e.matmul(pl[0:Q, 0:4 * Q], lhsT=IDENT[0:Q, 0:Q], rhs=negm[0:Q, 0:4 * Q], start=True, stop=False)
                        i = None
                        for i4 in range(4):
                            h = 4 * r + i4
                            i = e.matmul(pl[0:Q, i4 * Q:(i4 + 1) * Q], lhsT=dA[:, h:h + 1].to_broadcast([Q, Q]),
                                         rhs=TRI_I[0:Q, 0:Q], start=False, stop=(i4 == 3), skip_group_check=True)
                        return i

                    S.op("pe", lmm, reads=["DAA", "CONST"], writes=["ps%d" % bb])
                    lt = LT[r % 2]

                    def lexp(e, r=r, pl=pl, lt=lt):
                        i = None
                        for i4 in range(4):
                            h = 4 * r + i4
                            i = e.activation(out=lt[0:Q, i4 * Q:(i4 + 1) * Q], in_=pl[0:Q, i4 * Q:(i4 + 1) * Q], func=AF.Exp,
                                             bias=NAC[0:Q, h:h + 1])
                        return i

                    S.op("act", lexp, reads=["ps%d" % bb, "NAC", "M"], writes=["LT%d" % (r % 2)])
                    g = r // 2
                    S.op("dve", lambda e, r=r, lt=lt, g=g: e.tensor_tensor(
                        out=MT3[0:Q, 4 * r:4 * r + 4, 0:Q], in0=lt[0:Q, 0:4 * Q].rearrange("p (h q) -> p h q", h=4),
                        in1=bank(4)[0:Q, g * 128:g * 128 + Q].unsqueeze(1).to_broadcast([Q, 4, Q]), op=ALU.mult),
                         reads=["LT%d" % (r % 2), "ps4", "M"], writes=["MT%d" % r])

            def B1(ci):
                q0 = ci * Q

                def ymm(e):
                    i = None
                    for c in range(8):
                        e.matmul(p01[:, c * Q:(c + 1) * Q], lhsT=X3[0:Q, 2 * c, :], rhs=MT3[0:Q, 2 * c, 0:Q],
                                 start=True, stop=False)
                        i = e.matmul(p01[:, c * Q:(c + 1) * Q], lhsT=X3[0:Q, 2 * c + 1, :], rhs=MT3[0:Q, 2 * c + 1, 0:Q],
                                     start=False, stop=True)
                    return i

                S.op("pe", ymm, reads=["XDTP", "MT0", "MT1", "MT2", "MT3"], writes=["ps0", "ps1"])

                def omm(e):
                    i = None
                    for c in range(8):
                        i = e.matmul(p67[:, c * Q:(c + 1) * Q], lhsT=SBF[:, c * 128:(c + 1) * 128],
                                     rhs=CT3[:, c // 4, q0:q0 + Q], start=True, stop=True)
                    return i

                S.op("pe", omm, reads=[bk + "_0", bk + "_1", "CT0", "CT1"], writes=["ps6", "ps7"])
                for g in range(2):
                    S.op("pe", lambda e, g=g: e.matmul(bank(2 + g), lhsT=BTOK[0:Q, g * 128:(g + 1) * 128],
                                                       rhs=XW[0:Q, g * 512:(g + 1) * 512], start=True, stop=True),
                         reads=["BTOK", "XW"], writes=["ps%d" % (2 + g)])

            def B2(ci):
                def t1a(e):
                    hw_ = 4 * Q
                    e.tensor_tensor(out=T1[:, 0:hw_], in0=p67[:, 0:hw_], in1=EB[:, 0:hw_], op=ALU.mult)
                    return e.tensor_tensor(out=T1[:, hw_:2 * hw_], in0=p67[:, hw_:2 * hw_], in1=EB[:, hw_:2 * hw_], op=ALU.mult)

                def t1b(e):
                    hw_ = 4 * Q
                    e.tensor_tensor(out=T1[:, 0:hw_], in0=p01[:, 0:hw_], in1=T1[:, 0:hw_], op=ALU.add)
                    return e.tensor_tensor(out=T1[:, hw_:2 * hw_], in0=p01[:, hw_:2 * hw_], in1=T1[:, hw_:2 * hw_], op=ALU.add)

                S.op("dve", t1a, reads=["ps6", "ps7", "EB", "M"], writes=["T1"])
                S.op("dve", t1b, reads=["ps0", "ps1", "T1", "M"], writes=["T1"])

            def B3(ci):
                q0 = ci * Q

                def dsk(e):
                    i = None
                    for c in range(8):
                        i = e.scalar_tensor_tensor(out=XS3[:, c, q0:q0 + Q], in0=XS3[:, c, q0:q0 + Q],
                                                   scalar=pcol(l, "dsk", c), in1=T1[:, c * Q:(c + 1) * Q],
                                                   op0=ALU.mult, op1=ALU.add)
                    return i

                S.op("dve", dsk, reads=["T1", "PAR", "M"] + xsk(ci), writes=xsk(ci))

            def B4(ci):
                cde = CDEs[ci % 2]
                for g in range(2):
                    Sg = Sb[:, g * 512:(g + 1) * 512]
                    S.op("dve", lambda e, g=g, Sg=Sg: e.tensor_tensor(
                        out=Sg.rearrange("p (h m) -> p h m", h=8), in0=Sg.rearrange("p (h m) -> p h m", h=8),
                        in1=cde[:, g * 8:(g + 1) * 8].unsqueeze(2).to_broadcast([128, 8, 64]), op=ALU.mult),
                         reads=["CDE%d" % (ci % 2), sk + "_%d" % g, "M"], writes=[sk + "_%d" % g])
                    S.op("dve", lambda e, g=g, Sg=Sg: e.tensor_tensor(out=Sg, in0=bank(2 + g), in1=Sg, op=ALU.add),
                         reads=["ps%d" % (2 + g), sk + "_%d" % g], writes=[sk + "_%d" % g])
                    S.op("act", lambda e, g=g, Sg=Sg: e.activation(out=SBF[:, g * 512:(g + 1) * 512], in_=Sg, func=AF.Copy),
                         reads=[sk + "_%d" % g], writes=[bk + "_%d" % g])

            if sample:
                for ci in range(nck):
                    ssm_in(Sb, sk, sts[(l * 4 + ci) * 1024:(l * 4 + ci + 1) * 1024, :], SBF, bk)
                    F1(ci); F2(ci); F5(ci); B1(ci); B2(ci); B3(ci); B4(ci)
                    ssm_out(Sb, sk, o_sss[(l * 4 + ci) * 1024:(l * 4 + ci + 1) * 1024, :])
            else:
                F1(0); F2(0); F5(0)
                for ci in range(nck):
                    nxt = ci + 1 < nck
                    B1(ci)
                    B2(ci)
                    if nxt:
                        F1(ci + 1)
                    B4(ci)
                    if nxt:
                        F2(ci + 1)
                    B3(ci)
                    if nxt:
                        F5(ci + 1)
            if last and not sample and DBG.get("ssmout", 1):
                ssm_out(Sb, sk, o_pss[l * 1024:(l + 1) * 1024, :])

            if DBG['phase'] < 6:
                return
            allxs = lambda c: ["XS%d_%d" % (c, ci) for ci in range(nck)]
            for c in range(8):
                slab, skey, jj = wchunk(l, 32 + c)
                pz, pzk = proj(T, slab, skey, jj)
                zt = ZT[c % 2]
                S.op("act", lambda e, zt=zt, pz=pz: e.activation(out=zt[:, 0:T], in_=pz, func=AF.Silu),
                     reads=[pzk], writes=["ZT%d" % (c % 2)])
                S.op("dve", lambda e, zt=zt, c=c: e.tensor_tensor(out=XS3[:, c, 0:T], in0=XS3[:, c, 0:T], in1=zt[:, 0:T],
                                                                 op=ALU.mult),
                     reads=["ZT%d" % (c % 2)] + allxs(c), writes=allxs(c))
            rmsnorm(T, lambda c: pcol(l, "gssm", c), lambda c: YC3[:, 4 + c, 0:T], lambda c: allxs(c),
                    lambda c: "YC%d" % (4 + c), EPS, src3=XS3)

            if DBG['phase'] < 7:
                return
            for si in range(4):
                slab, skey = get_slab("wout", l, si)
                for m in range(2):
                    po, pok = proj(T, slab, skey, m, width=256, nk=16, rhs3=YC3, rkeys=["YC%d" % k for k in range(16)])
                    cc = si * 2 + m
                    S.op("dve", lambda e, po=po, cc=cc: e.tensor_tensor(out=XT3[:, cc, 0:T], in0=po, in1=XT3[:, cc, 0:T],
                                                                        op=ALU.add),
                         reads=[pok, "XT%d" % cc], writes=["XT%d" % cc])

            if DBG['phase'] < 8:
                return
            rmsnorm(T, lambda c: pcol(l, "g2", c), lambda c: HB3[:, c, 0:T], lambda c: ["XT%d" % c],
                    lambda c: "HB", EPS)
            S.op("dve", lambda e: e.memset(FENCE, 0.0), reads=["M"], writes=["M"])
            pend_f = []

            def flush_f():
                while pend_f:
                    gs_, cf_, j_, pu_, puk_ = pend_f.pop(0)
                    S.op("act", lambda e, gs_=gs_, cf_=cf_: e.activation(out=gs_[:, 0:T], in_=cf_[:, 0:T], func=AF.Silu),
                         reads=["CF%d" % (j_ % 2), "M"], writes=["GS%d" % (j_ % 2)])
                    S.op("dve", lambda e, pu_=pu_, gs_=gs_, j_=j_: e.tensor_tensor(out=FA3[:, j_, 0:T], in0=pu_, in1=gs_[:, 0:T],
                                                                                   op=ALU.mult),
                         reads=[puk_, "GS%d" % (j_ % 2), "M"], writes=["FA%d" % j_])

            for si in range(11):
                slab, skey = get_slab("wup", l, si)
                for jj in range(2):
                    j = si * 2 + jj
                    pg, pgk = proj(T, slab, skey, 2 * jj)
                    uf = UFt[j % 3]
                    U3 = uf[:, 0:nseg * (L + 2)].rearrange("p (s t) -> p s t", s=nseg)
                    uk = "UF%d" % (j % 3)
                    hist_in(HENG, hist["f"], j, nseg, 2, U3, uk, hkey["f"])
                    S.op("act", lambda e, U3=U3, pg=pg: e.activation(out=U3[:, :, 2:2 + L], in_=seg3(pg), func=AF.Copy),
                         reads=[pgk, "M"], writes=[uk])
                    flush_f()
                    hist_out(HENG, hist["f"], j, nseg, 2, L, U3, uk, hkey["f"])
                    cf = CFt[j % 2]

                    def convF(e, U3=U3, cf=cf, j=j):
                        o = seg3(cf[:, 0:T])
                        e.tensor_scalar(out=o, in0=U3[:, :, 0:L], scalar1=pcol(l, "cfw", j * 3), scalar2=None, op0=ALU.mult)
                        e.scalar_tensor_tensor(out=o, in0=U3[:, :, 1:1 + L], scalar=pcol(l, "cfw", j * 3 + 1), in1=o,
                                               op0=ALU.mult, op1=ALU.add)
                        return e.scalar_tensor_tensor(out=o, in0=U3[:, :, 2:2 + L], scalar=pcol(l, "cfw", j * 3 + 2), in1=o,
                                                      op0=ALU.mult, op1=ALU.add)

                    S.op("dve", convF, reads=[uk, uk + "h", "PAR", "M"], writes=["CF%d" % (j % 2)])
                    gs = GS[j % 2]
                    pu, puk = proj(T, slab, skey, 2 * jj + 1)
                    pend_f.append((gs, cf, j, pu, puk))
            flush_f()
            if sample or last:
                R = nseg * 2
                dst = (o_sf if sample else o_pf)[l * R:(l + 1) * R, :]
                state_out(hist["f"], 22, R, dst, 1.0, hkey["f"])
            for si in range(8):
                slab, skey = get_slab("wdn", l, si)
                pd, pdk = proj(T, slab, skey, 0, width=128, nk=22, rhs3=FA3, rkeys=["FA%d" % k for k in range(22)])
                S.op("dve", lambda e, pd=pd, si=si: e.tensor_tensor(out=XT3[:, si, 0:T], in0=pd, in1=XT3[:, si, 0:T],
                                                                    op=ALU.add),
                     reads=[pdk, "XT%d" % si], writes=["XT%d" % si])

        def tile(xsrc, ydst, T, nseg, sample, last):
            nblk = T // 128
            for blk in range(nblk):
                xi = XIN[blk % 2]
                S.dma("sp", "xin%d" % (blk % 2), lambda e, blk=blk, xi=xi: e.dma_start(out=xi, in_=xsrc[blk * 128:(blk + 1) * 128, :]),
                      writes=["XIN%d" % (blk % 2)])
                for hf in range(2):
                    b = next_bank()

                    def fn(e, hf=hf, b=b, xi=xi):
                        i = None
                        for j in range(4):
                            c = hf * 4 + j
                            i = e.transpose(out=bank(b)[:, j * 128:(j + 1) * 128], in_=xi[:, c * 128:(c + 1) * 128],
                                            identity=IDENT)
                        return i

                    S.op("pe", fn, reads=["XIN%d" % (blk % 2), "CONST"], writes=["ps%d" % b])
                    S.op("act", lambda e, hf=hf, b=b, blk=blk: e.activation(
                        out=XT3[:, hf * 4:(hf + 1) * 4, blk * 128:(blk + 1) * 128],
                        in_=bank(b).rearrange("p (c t) -> p c t", c=4), func=AF.Copy),
                         reads=["ps%d" % b], writes=["XT%d" % c for c in range(hf * 4, hf * 4 + 4)])
            for l in range(DBG["layers"]):
                layer(l, T, nseg, sample, last)
            S.op("dve", lambda e: e.memset(FENCE, 0.0), reads=["M"], writes=["M"])
            for c in range(8):
                stat_chunk(T, c, XT3, ["XT%d" % c])
            S.op("dve", lambda e: e.tensor_scalar(out=STAT[:, 0:T], in0=bank(4)[:, 0:T], scalar1=EPS, scalar2=None,
                                                  op0=ALU.add), reads=["ps4"], writes=["STAT"])
            def rs_fn(e):
                e.activation(out=RSTD[:, 0:T], in_=STAT[:, 0:T], func=AF.Ln)
                return e.activation(out=RSTD[:, 0:T], in_=RSTD[:, 0:T], func=AF.Exp, scale=-0.5)

            S.op("act", rs_fn, reads=["STAT"], writes=["RSTD"])
            gf = PAR[:, NL * NPAR: NL * NPAR + 8]
            for blk in range(nblk):
                yf = YFt[blk % 2]
                yf3 = v3(yf, 8)

                def fin(e, blk=blk, yf3=yf3):
                    i = None
                    for c in range(8):
                        i = e.scalar_tensor_tensor(out=yf3[:, c, :], in0=XT3[:, c, blk * 128:(blk + 1) * 128],
                                                   scalar=gf[:, c:c + 1], in1=RSTD[:, blk * 128:(blk + 1) * 128],
                                                   op0=ALU.mult, op1=ALU.mult)
                    return i

                S.op("dve", fin, reads=["XT%d" % c for c in range(8)] + ["RSTD", "PAR", "M"], writes=["YF%d" % (blk % 2)])
                xi = XIN[blk % 2]
                for hf in range(2):
                    b = next_bank()

                    def fn(e, hf=hf, b=b, yf3=yf3):
                        i = None
                        for j in range(4):
                            c = hf * 4 + j
                            i = e.transpose(out=bank(b)[:, j * 128:(j + 1) * 128], in_=yf3[:, c, :], identity=IDENT)
                        return i

                    S.op("pe", fn, reads=["YF%d" % (blk % 2), "CONST"], writes=["ps%d" % b])
                    S.op("act", lambda e, hf=hf, b=b, xi=xi: e.activation(out=xi[:, hf * 512:(hf + 1) * 512], in_=bank(b),
                                                                         func=AF.Copy),
                         reads=["ps%d" % b], writes=["XIN%d" % (blk % 2)])
                S.dma("act", "yout%d" % (blk % 2), lambda e, blk=blk, xi=xi: e.dma_start(out=ydst[blk * 128:(blk + 1) * 128, :], in_=xi),
                      reads=["XIN%d" % (blk % 2)])

        for t in range(NT):
            tile(xp[t * TP:(t + 1) * TP, :], yp[t * TP:(t + 1) * TP, :], TP, 1, False, t == NT - 1)
        if with_sample:
            tile(xs, ys, 128, 4, True, False)

        for e_ in ("sp", "act", "pool"):
            pass
        S.final_wait("sp")
        import os as _os2
        if _os2.environ.get("DUMPLOG"):
            with open(_os2.environ["DUMPLOG"], "w") as f:
                for t in S.log:
                    f.write(repr(t) + "\n")
            print("counts", S.cnt, {k: v[1] for k, v in S.dsem.items()})
        block = es.enter_context(nc.Block())
        S.emit(block)
    return nc


def _pack_params(inp):
    out = np.zeros((128, NPARTOT), np.float32)

    def fm(v, nch):
        return np.ascontiguousarray(v.reshape(nch, 128).T)

    def put(l, name, arr):
        o, w = _PC[name]
        assert arr.shape == (128, w), (name, arr.shape, w)
        out[:, l * NPAR + o: l * NPAR + o + w] = arr

    for l in range(NL):
        put(l, "g1", fm(inp["norm_mix_g"][l], 8))
        put(l, "g2", fm(inp["norm_ffn_g"][l], 8))
        put(l, "gssm", fm(inp["ssm_norm_g"][l], 8))
        put(l, "dsk", fm(np.repeat(inp["d_skip"][l], 64), 8))
        def cw(w, nch):
            K = w.shape[0]
            return np.ascontiguousarray(w.reshape(K, nch, 128).transpose(2, 1, 0).reshape(128, nch * K))
        put(l, "caw", cw(inp["conv_a_w"][l], 4))
        put(l, "cbw", cw(inp["conv_b_w"][l], 12))
        put(l, "cbb", fm(inp["conv_b_bias"][l], 12))
        put(l, "ccw", cw(inp["conv_c_w"][l], 4))
        put(l, "ccb", fm(inp["conv_c_bias"][l], 4))
        put(l, "lng", fm(inp["ln_c_g"][l], 4))
        put(l, "lnb", fm(inp["ln_c_b"][l], 4))
        put(l, "cfw", cw(inp["conv_ffn_w"][l], 22))
        put(l, "dtb", np.broadcast_to(inp["dt_bias"][l][None, :], (128, 16)))
        put(l, "alog", np.broadcast_to(inp["a_log"][l][None, :], (128, 16)))
    out[:, NL * NPAR: NL * NPAR + 8] = fm(inp["final_norm_g"], 8)
    return out


_NC_CACHE = {}


def kernel(**inp):
    inp = {k: np.asarray(v) for k, v in inp.items()}
    xp = inp["x_prompt"]
    seq = xp.shape[1]
    nt = seq // TP
    key = (nt,)
    if key not in _NC_CACHE:
        _NC_CACHE[key] = build_program(nt, True)
    nc = _NC_CACHE[key]
    par = _pack_params(inp)
    win = np.ascontiguousarray(inp["w_in"].reshape(NL * D, DIN))
    wout = np.ascontiguousarray(inp["w_out"].reshape(NL * 2048, D))
    wup = np.ascontiguousarray(inp["w_up"].reshape(NL * D, 2 * DFF))
    wdn = np.ascontiguousarray(inp["w_down"].reshape(NL * DFF, D))
    in_maps = []
    for c in range(NCORES):
        b0 = c * 4
        in_maps.append({
            "xp": np.ascontiguousarray(xp[c]),
            "xs": np.ascontiguousarray(inp["x_sample"][b0:b0 + 4].reshape(128, D)),
            "sta": np.ascontiguousarray(inp["state_conv_a"][:, b0:b0 + 4].reshape(NL * 8, 512)),
            "sts": np.ascontiguousarray(inp["state_ssm"][:, b0:b0 + 4].reshape(NL * 4 * 1024, 128)),
            "stb": np.ascontiguousarray(inp["state_conv_b"][:, b0:b0 + 4].reshape(NL * 12, 1536)),
            "stc": np.ascontiguousarray(inp["state_conv_c"][:, b0:b0 + 4].reshape(NL * 120, 512)),
            "stf": np.ascontiguousarray(inp["state_conv_ffn"][:, b0:b0 + 4].reshape(NL * 8, DFF)),
            "win": win, "wout": wout, "wup": wup, "wdn": wdn, "par": par,
        })
    res = run_bass_kernel_spmd(nc, in_maps, core_ids=list(range(NCORES)))
    R = res.results
    y_prompt = np.stack([R[c]["yp"] for c in range(NCORES)], 0)
    y_sample = np.concatenate([R[c]["ys"].reshape(4, 32, D) for c in range(NCORES)], 0)

    def pst(name, shp):
        return np.stack([R[c][name].reshape((NL,) + shp) for c in range(NCORES)], 1)

    def sst(name, shp):
        return np.concatenate([R[c][name].reshape((NL, 4) + shp) for c in range(NCORES)], 1)

    return (y_prompt.astype(np.float32), y_sample.astype(np.float32),
            pst("pa", (2, 512)), pst("pss", (16, 64, 128)), pst("pb", (3, 1536)), pst("pc", (30, 512)),
            pst("pf", (2, DFF)),
            sst("sa", (2, 512)), sst("sss", (16, 64, 128)), sst("sb", (3, 1536)), sst("sc", (30, 512)),
            sst("sf", (2, DFF)))
```

```python
import numpy as np
from contextlib import ExitStack
import concourse.bass as bass
import concourse.mybir as mybir
from concourse.bass_utils import run_bass_kernel_spmd

F32 = mybir.dt.float32
BF16 = mybir.dt.bfloat16
AF = mybir.ActivationFunctionType
ALU = mybir.AluOpType

NCORES = 8
D = 1024
DIN = 5136
DFF = 2816
NL = 2
TP = 512
EPS = 1e-5
NSLOT = 4
SLOTW = 2048

_PC = {}
_off = 0
for _n, _w in (("g1", 8), ("g2", 8), ("gssm", 8), ("dsk", 8), ("caw", 12), ("cbw", 48), ("cbb", 12),
               ("ccw", 124), ("ccb", 4), ("lng", 4), ("lnb", 4), ("cfw", 66), ("dtb", 16), ("alog", 16)):
    _PC[_n] = (_off, _w)
    _off += _w
NPAR = _off
NPARTOT = NL * NPAR + 8


class Sched:
    ENGS = ("pe", "act", "dve", "pool", "sp")

    def __init__(self, nc, es):
        self.nc = nc
        self.es = es
        self.sem = {e: es.enter_context(nc.semaphore("s_" + e)) for e in self.ENGS}
        self.cnt = {e: 0 for e in self.ENGS}
        self.prog = {e: [] for e in self.ENGS}
        self.waited = {e: {} for e in self.ENGS}
        self.last_w = {}
        self.readers = {}
        self.dsem = {}

    def _handle(self, key):
        return self.sem[key] if key in self.sem else self.dsem[key][0]

    def _deps(self, eng, reads, writes, extra):
        deps = {}

        def add(tok, kind):
            if tok is None:
                return
            k, v = tok
            if k == eng and kind == "war":
                return
            if v > deps.get(k, 0):
                deps[k] = v

        for r in reads:
            add(self.last_w.get(r), "raw")
        for w in writes:
            add(self.last_w.get(w), "waw")
            for t in self.readers.get(w, ()):
                add(t, "war")
        for t in extra:
            add(t, "raw")
        out = []
        for k, v in deps.items():
            if v > self.waited[eng].get(k, 0):
                self.waited[eng][k] = v
                out.append((k, v))
        return out

    def _record(self, tok, reads, writes):
        for r in reads:
            lst = self.readers.setdefault(r, [])
            lst[:] = [t for t in lst if t[0] != tok[0]]
            lst.append(tok)
        for w in writes:
            self.last_w[w] = tok
            self.readers[w] = []

    MPREF = ("AV", "UA", "CA", "TG", "UC", "CC", "PTMP", "MEAN", "VAR", "RSTC", "TN", "DTP", "DTE", "DTT", "DAA",
             "E32", "CDE", "NAC", "SCXW", "ACS", "XW", "BTOK", "LT", "MT", "EB", "T1", "ZT", "FA", "UF", "CF", "GS", "YF")

    def op(self, eng, fn, reads=(), writes=(), extra=()):
        reads = list(reads)
        writes = list(writes)
        if "M" not in writes and "M" not in reads:
            if any(k.startswith(p) for k in reads + writes for p in self.MPREF):
                reads.append("M")
        for b in sorted({k[2] for k in reads + writes if k.startswith("ps")}):
            writes.append("bk" + b)
        waits = self._deps(eng, reads, writes, extra)
        self.cnt[eng] += 1
        tok = (eng, self.cnt[eng])
        sem = self.sem[eng]
        hw = [(self._handle(k), v) for k, v in waits]

        def thunk(e):
            for h, v in hw:
                e.wait_ge(h, v)
            fn(e).then_inc(sem, 1)

        self.prog[eng].append(thunk)
        self._record(tok, reads, writes)
        if not hasattr(self, "log"):
            self.log = []
        self.log.append((tok, waits, list(reads), list(writes)))
        return tok

    def dma(self, eng, semname, fn, reads=(), writes=(), extra=()):
        if semname not in self.dsem:
            self.dsem[semname] = [self.es.enter_context(self.nc.semaphore("d_" + semname)), 0]
        ent = self.dsem[semname]
        prev = (semname, ent[1] * 16) if ent[1] > 0 else None
        ex = list(extra) + ([prev] if prev else [])
        waits = self._deps(eng, reads, writes, ex)
        ent[1] += 1
        tok = (semname, ent[1] * 16)
        hw = [(self._handle(k), v) for k, v in waits]
        h = ent[0]

        def thunk(e):
            for hh, v in hw:
                e.wait_ge(hh, v)
            fn(e).then_inc(h, 16)

        self.prog[eng].append(thunk)
        self._record(tok, reads, writes)
        return tok

    def final_wait(self, eng):
        hw = [(v[0], v[1] * 16) for v in self.dsem.values()]

        def thunk(e):
            for hh, v in hw:
                e.wait_ge(hh, v)

        self.prog[eng].append(thunk)

    def emit(self, block):
        prog = self.prog

        @block.tensor
        def _(e):
            for t in prog["pe"]:
                t(e)

        @block.scalar
        def _(e):
            for t in prog["act"]:
                t(e)

        @block.vector
        def _(e):
            for t in prog["dve"]:
                t(e)

        @block.gpsimd
        def _(e):
            for t in prog["pool"]:
                t(e)

        @block.sync
        def _(e):
            for t in prog["sp"]:
                t(e)


class Arena:
    def __init__(self, ap, total):
        self.ap = ap
        self.total = total
        self.off = 0

    def f32(self, n):
        a = self.ap[:, self.off:self.off + n]
        self.off += n
        assert self.off <= self.total, ("arena overflow", self.off, self.total)
        return a

    def bf16(self, n):
        assert n % 2 == 0
        return self.f32(n // 2).bitcast(BF16)


def v3(ap, a):
    return ap.rearrange("p (a b) -> p a b", a=a)


DBG = {"layers": NL, "phase": 99, "casts": True, "ssdcut": 99}


def build_program(n_ptiles=8, with_sample=True):
    nc = bass.Bass("TRN2", target_bir_lowering=False)
    NT = n_ptiles
    SEQ = NT * TP

    def din(name, shape, dt=F32):
        return nc.dram_tensor(name, shape, dt, kind="ExternalInput").ap()

    def dout(name, shape):
        return nc.dram_tensor(name, shape, F32, kind="ExternalOutput").ap()

    def dint(name, shape, dt):
        return nc.dram_tensor(name, shape, dt, kind="Internal").ap()

    xp = din("xp", [SEQ, D])
    xs = din("xs", [128, D])
    sta = din("sta", [NL * 4 * 2, 512])
    sts = din("sts", [NL * 4 * 1024, 128])
    stb = din("stb", [NL * 4 * 3, 1536])
    stc = din("stc", [NL * 4 * 30, 512])
    stf = din("stf", [NL * 4 * 2, DFF])
    win = din("win", [NL * D, DIN])
    wout = din("wout", [NL * 2048, D])
    wup = din("wup", [NL * D, 2 * DFF])
    wdn = din("wdn", [NL * DFF, D])
    par = din("par", [128, NPARTOT])

    yp = dout("yp", [SEQ, D])
    ys = dout("ys", [128, D])
    o_pa = dout("pa", [NL * 2, 512])
    o_pss = dout("pss", [NL * 1024, 128])
    o_pb = dout("pb", [NL * 3, 1536])
    o_pc = dout("pc", [NL * 30, 512])
    o_pf = dout("pf", [NL * 2, DFF])
    o_sa = dout("sa", [NL * 4 * 2, 512])
    o_sss = dout("sss", [NL * 4 * 1024, 128])
    o_sb = dout("sb", [NL * 4 * 3, 1536])
    o_sc = dout("sc", [NL * 4 * 30, 512])
    o_sf = dout("sf", [NL * 4 * 2, DFF])

    s_win = dint("s_win", [NL * 10, 128, 8 * 512], BF16)
    s_wdt = dint("s_wdt", [NL, 128, 8 * 16], BF16)
    s_wout = dint("s_wout", [NL * 4, 128, 16 * 256], BF16)
    s_wup = dint("s_wup", [NL * 11, 128, 8 * 512], BF16)
    s_wdn = dint("s_wdn", [NL * 8, 128, 22 * 128], BF16)

    es = ExitStack()
    with es:
        S = Sched(nc, es)
        ARENA_WORDS = 53200
        arena_t = es.enter_context(nc.sbuf_tensor("arena", [128, ARENA_WORDS], F32))
        psum = es.enter_context(nc.psum_tensor("psum", [128, 4096], F32))
        A = Arena(arena_t, ARENA_WORDS)

        def bank(b):
            return psum[:, b * 512:(b + 1) * 512]

        XT = A.f32(8 * TP)
        XT3 = v3(XT, 8)
        HB = A.bf16(8 * TP)
        HB3 = v3(HB, 8)
        WSLOT = [A.f32(SLOTW).bitcast(BF16) for _ in range(NSLOT)]
        XIN = [A.f32(1024) for _ in range(2)]
        PAR = A.f32(NPARTOT)
        IDENT = A.f32(128)
        IDENTB = A.bf16(128)
        TRI_I = A.f32(128)
        TRI_G = A.f32(128)
        ONES = A.f32(128)
        ONESN = A.f32(128)
        ONES5 = A.f32(128)
        NEGM128 = A.f32(512)
        NEGM32 = A.f32(128)
        SEL2 = A.f32(1024)
        NEGHALF = A.f32(TP)
        ABC = A.f32(NL * 16)
        CCWH = A.f32(NL * 124)
        WDT = [A.bf16(128) for _ in range(NL)]
        SP_ = [A.f32(1024) for _ in range(NL)]
        SSMP = A.f32(1024)
        SBFP = [A.bf16(1024) for _ in range(NL)]
        SBFS = A.bf16(1024)
        XDTP = A.bf16(16 * 128)
        HIST_W = {"a": (4, 2), "b": (12, 3), "c": (4, 30), "f": (22, 2)}
        HISTP = {k: [A.f32(nch * H) for _ in range(NL)] for k, (nch, H) in HIST_W.items()}
        HISTS = {k: A.f32(nch * 4 * H) for k, (nch, H) in HIST_W.items()}
        YCAT = A.bf16(16 * TP)
        YC3 = v3(YCAT, 16)
        SQ = [A.f32(TP) for _ in range(2)]
        STAT = A.f32(TP)
        RSTD = A.f32(TP)
        XS = A.f32(8 * TP)
        XS3 = v3(XS, 8)
        BT = A.bf16(2 * TP)
        BT3 = v3(BT, 2)
        CT = A.bf16(2 * TP)
        CT3 = v3(CT, 2)
        UBt = [A.f32(TP + 4 * 3) for _ in range(3)]
        CBo = [A.f32(TP) for _ in range(2)]
        STG = XIN[0][:, 0:512]
        STG2 = XIN[1]
        m_base = A.off
        print("m_base", m_base)
        AVt = [A.f32(TP) for _ in range(2)]
        UAt = [A.f32(TP + 4 * 2) for _ in range(2)]
        CAt = [A.f32(TP) for _ in range(2)]
        TG = [A.f32(TP) for _ in range(2)]
        UCt = [A.f32(TP + 4 * 30) for _ in range(2)]
        CC = A.f32(4 * TP)
        CC3 = v3(CC, 4)
        PTMP = A.f32(TP)
        MEAN = A.f32(TP)
        VAR = A.f32(TP)
        RSTC = A.f32(TP)
        TN = [A.f32(TP) for _ in range(2)]
        m_end_ac = A.off
        A.off = m_base
        DTP = A.f32(64)
        DTE = A.f32(64)
        DTT = A.f32(64)
        DAA = A.f32(64)
        E32 = A.f32(32)
        CDE = A.f32(16)
        CDEb = A.f32(16)
        NAC = A.f32(16)
        SCXW = A.f32(16)
        ACS = A.f32(128)
        XW = A.bf16(1024)
        BTOK = A.bf16(256)
        LT = [A.f32(4 * 128) for _ in range(2)]
        MT = A.bf16(16 * 128)
        EB = A.f32(1024)
        T1 = A.f32(1024)
        ZT = [A.f32(TP) for _ in range(2)]
        m_end_ssd = A.off
        A.off = m_base
        FACT = A.bf16(22 * TP)
        FA3 = v3(FACT, 22)
        UFt = [A.f32(TP + 4 * 2) for _ in range(3)]
        CF_base = A.off
        CFt = [A.f32(TP) for _ in range(2)]
        GS = [A.f32(TP) for _ in range(2)]
        YFt = [arena_t[:, CF_base:CF_base + 1024], arena_t[:, CF_base + 1024:CF_base + 2048]]
        m_end_ffn = A.off
        A.off = max(m_end_ac, m_end_ssd, m_end_ffn)
        print("arena words used", A.off, "of", ARENA_WORDS, "(AC %d SSD %d FFN %d)" % (
            m_end_ac - m_base, m_end_ssd - m_base, m_end_ffn - m_base))

        def pcol(l, name, j=0, w=1):
            o, _ = _PC[name]
            return PAR[:, l * NPAR + o + j: l * NPAR + o + j + w]

        S.dma("sp", "par", lambda e: e.dma_start(out=PAR, in_=par), writes=["PAR"])

        def consts(e):
            e.memset(arena_t[:, m_base:A.off], 0.0)
            e.memset(TRI_I, 1.0)
            e.affine_select(out=TRI_I, in_=TRI_I, compare_op=ALU.is_ge, fill=0.0, base=0,
                            pattern=[[1, 128]], channel_multiplier=-1)
            e.memset(TRI_G, 1.0)
            e.affine_select(out=TRI_G, in_=TRI_G, compare_op=ALU.is_gt, fill=0.0, base=0,
                            pattern=[[-1, 128]], channel_multiplier=1)
            e.memset(IDENT, 0.0)
            e.affine_select(out=IDENT, in_=IDENT, compare_op=ALU.not_equal, fill=1.0, base=0,
                            pattern=[[-1, 128]], channel_multiplier=1)
            e.memset(ONES, 1.0)
            e.memset(ONESN, 1.0 / 1024.0)
            e.memset(ONES5, 1.0 / 512.0)
            e.memset(NEGHALF, -0.5)
            e.memset(NEGM128, 0.0)
            for i in range(4):
                e.affine_select(out=NEGM128[:, i * 128:(i + 1) * 128], in_=NEGM128[:, i * 128:(i + 1) * 128],
                                compare_op=ALU.is_ge, fill=-30000.0, base=0, pattern=[[1, 128]],
                                channel_multiplier=-1)
            e.memset(NEGM32, 0.0)
            for i in range(4):
                e.affine_select(out=NEGM32[:, i * 32:(i + 1) * 32], in_=NEGM32[:, i * 32:(i + 1) * 32],
                                compare_op=ALU.is_ge, fill=-30000.0, base=0, pattern=[[1, 32]],
                                channel_multiplier=-1)
            e.memset(SEL2, 1.0)
            s4 = SEL2.rearrange("p (c t m) -> p c t m", c=8, t=2)
            e.affine_select(out=s4, in_=s4, compare_op=ALU.is_equal, fill=0.0, base=0,
                            pattern=[[2, 8], [1, 2], [0, 64]], channel_multiplier=-1)
            for k in HISTP:
                for l in range(NL):
                    e.memset(HISTP[k][l], 0.0)
            for l in range(NL):
                e.memset(SP_[l], 0.0)
            e.memset(XDTP, 0.0)
            for l in range(NL):
                e.memset(SBFP[l], 0.0)
            return e.memset(SBFS, 0.0)

        S.op("pool", consts, writes=["CONST", "M", "XDTP", "SBFS_0", "SBFS_1"] + ["HP%s%d" % (k, l) for k in "abcf" for l in range(NL)]
             + ["SP%d_%d" % (l, g) for l in range(NL) for g in range(2)] + ["SBFP%d_%d" % (l, g) for l in range(NL) for g in range(2)])
        S.op("dve", lambda e: e.tensor_copy(out=IDENTB, in_=IDENT), reads=["CONST"], writes=["IDENTB"])

        def abc_fn(e):
            for l in range(NL):
                e.activation(out=ABC[:, l * 16:(l + 1) * 16], in_=pcol(l, "alog", 0, 16), func=AF.Exp)
            return e.mul(ABC, ABC, -1.0)

        S.op("act", abc_fn, reads=["PAR"], writes=["ABC"])

        def ccwh_fn(e):
            i = None
            for l in range(NL):
                i = e.tensor_scalar(out=CCWH[:, l * 124:(l + 1) * 124], in0=pcol(l, "ccw", 0, 124),
                                    scalar1=0.5, scalar2=None, op0=ALU.mult)
            return i

        S.op("dve", ccwh_fn, reads=["PAR"], writes=["CCWH"])

        cast_i = [0]

        scr_keys = {}

        import os as _os
        _sel = _os.environ.get("CASTSEL", "win,wdt,wout,wup,wdn").split(",")

        def cast(dst, src, key):
            if not any(key.startswith("scr_" + q) for q in _sel):
                scr_keys.setdefault(key, [])
                return
            n = cast_i[0] % 6
            cast_i[0] += 1
            sub = key + "#%d" % len(scr_keys.setdefault(key, []))
            scr_keys[key].append(sub)
            S.dma("pool", "cast%d" % n, lambda e: e.dma_start(out=dst, in_=src), writes=[sub])

        def win_src(l, c0, w):
            return win[l * D:(l + 1) * D, c0:c0 + w].rearrange("(kc p) e -> p kc e", p=128)

        ordA = []
        for c in range(4):
            ordA += [0 + c * 128, 1024 + c * 128, 512 + c * 128]
        ordC = []
        for c in range(4):
            ordC += [4112 + 512 + c * 128, 4112 + c * 128]
        ordB = [2560 + j * 128 for j in range(12)]
        ordZ = [1536 + j * 128 for j in range(8)]
        win_order = ordA + ordC + ordB + ordZ

        def emit_casts(l):
            for si in range(10):
                cols = win_order[si * 4:(si + 1) * 4]
                dst = s_win[l * 10 + si].rearrange("p (kc e) -> p kc e", kc=8)
                if cols[3] - cols[0] == 384 and cols[1] - cols[0] == 128:
                    cast(dst, win_src(l, cols[0], 512), "scr_win%d_%d" % (l, si))
                else:
                    for j, c0 in enumerate(cols):
                        cast(dst[:, :, j * 128:(j + 1) * 128], win_src(l, c0, 128), "scr_win%d_%d" % (l, si))
            cast(s_wdt[l].rearrange("p (kc e) -> p kc e", kc=8), win_src(l, 4096, 16), "scr_wdt%d" % l)
            for si in range(4):
                cast(s_wout[l * 4 + si].rearrange("p (kc e) -> p kc e", kc=16),
                     wout[l * 2048:(l + 1) * 2048, si * 256:(si + 1) * 256].rearrange("(kc p) e -> p kc e", p=128),
                     "scr_wout%d_%d" % (l, si))
            for si in range(11):
                dst = s_wup[l * 11 + si].rearrange("p (kc j t e) -> p kc j t e", kc=8, j=2, t=2)
                for t, base in ((0, DFF), (1, 0)):
                    for j in range(2):
                        c0 = base + si * 256 + j * 128
                        src = wup[l * D:(l + 1) * D, c0:c0 + 128].rearrange("(kc p) e -> p kc e", p=128)
                        cast(dst[:, :, j, t, :], src, "scr_wup%d_%d" % (l, si))
            for si in range(8):
                cast(s_wdn[l * 8 + si].rearrange("p (kc e) -> p kc e", kc=22),
                     wdn[l * DFF:(l + 1) * DFF, si * 128:(si + 1) * 128].rearrange("(kc p) e -> p kc e", p=128),
                     "scr_wdn%d_%d" % (l, si))

        for l in range(NL if DBG["casts"] else 0):
            emit_casts(l)
        for l in range(NL if DBG["casts"] else 0):
            S.dma("sp", "wdt", lambda e, l=l: e.dma_start(out=WDT[l], in_=s_wdt[l]),
                  reads=scr_keys["scr_wdt%d" % l], writes=["WDT%d" % l])

        def layer_slabs(l):
            seq = []
            for si in range(10):
                seq.append(("win", l, si))
            for si in range(4):
                seq.append(("wout", l, si))
            for si in range(11):
                seq.append(("wup", l, si))
            for si in range(8):
                seq.append(("wdn", l, si))
            return seq

        ntiles_total = NT + (1 if with_sample else 0)
        wseq = []
        for _t in range(ntiles_total):
            for l in range(NL):
                wseq += layer_slabs(l)
        wstate = {"next": 0, "use": 0}

        def slab_src(kind, l, si):
            if kind == "win":
                return s_win[l * 10 + si], 4096, "scr_win%d_%d" % (l, si)
            if kind == "wout":
                return s_wout[l * 4 + si], 4096, "scr_wout%d_%d" % (l, si)
            if kind == "wup":
                return s_wup[l * 11 + si], 4096, "scr_wup%d_%d" % (l, si)
            return s_wdn[l * 8 + si], 22 * 128, "scr_wdn%d_%d" % (l, si)

        def issue_load(i):
            kind, l, si = wseq[i]
            src, n, key = slab_src(kind, l, si)
            slot = i % NSLOT
            S.dma("sp", "w%d" % slot, lambda e: e.dma_start(out=WSLOT[slot][:, 0:n], in_=src),
                  reads=sum(scr_keys.values(), []), writes=["W%d" % slot])

        def get_slab(kind, l, si):
            i = wstate["use"]
            while wseq[i] != (kind, l, si):
                assert DBG["phase"] < 99 or DBG["layers"] < NL
                i += 1
            wstate["use"] = i + 1
            while wstate["next"] < min(len(wseq), i + NSLOT):
                issue_load(wstate["next"])
                wstate["next"] += 1
            slot = i % NSLOT
            return WSLOT[slot], "W%d" % slot

        mmrr = [0]

        def next_bank():
            b = mmrr[0] % 4
            mmrr[0] += 1
            return b

        FENCE = A.f32(2)
        wcur = {}

        def wchunk(l, gi):
            si, jj = divmod(gi, 4)
            if jj == 0:
                wcur["s"] = get_slab("win", l, si)
            return wcur["s"][0], wcur["s"][1], jj

        def stat_chunk(T, c, src3, keys):
            sq = SQ[c % 2]
            S.op("act", lambda e: e.activation(out=sq[:, 0:T], in_=src3[:, c, 0:T], func=AF.Square),
                 reads=keys, writes=["SQ%d" % (c % 2)])
            S.op("pe", lambda e: e.matmul(bank(4)[:, 0:T], lhsT=ONESN, rhs=sq[:, 0:T], start=(c == 0), stop=(c == 7)),
                 reads=["SQ%d" % (c % 2), "CONST"], writes=["ps4"])

        def rmsnorm(T, gcol, out_fn, xkeys_r, outkeys, eps, src3=None, stats_done=False):
            src3 = XT3 if src3 is None else src3
            for c in range(0 if stats_done else 8):
                stat_chunk(T, c, src3, xkeys_r(c))
            S.op("dve", lambda e: e.tensor_scalar(out=STAT[:, 0:T], in0=bank(4)[:, 0:T], scalar1=eps, scalar2=None,
                                                  op0=ALU.add), reads=["ps4"], writes=["STAT"])
            def rs_fn(e):
                e.activation(out=RSTD[:, 0:T], in_=STAT[:, 0:T], func=AF.Ln)
                return e.activation(out=RSTD[:, 0:T], in_=RSTD[:, 0:T], func=AF.Exp, scale=-0.5)

            S.op("act", rs_fn, reads=["STAT"], writes=["RSTD"])
            for c in range(8):
                S.op("dve", lambda e, c=c: e.scalar_tensor_tensor(out=out_fn(c), in0=src3[:, c, 0:T], scalar=gcol(c),
                                                                   in1=RSTD[:, 0:T], op0=ALU.mult, op1=ALU.mult),
                     reads=xkeys_r(c) + ["RSTD", "PAR"], writes=[outkeys(c)])

        def proj(T, slab, slabkey, j, width=512, nk=8, rhs3=None, rkeys=None):
            rhs3 = HB3 if rhs3 is None else rhs3
            rkeys = ["HB"] if rkeys is None else rkeys
            b = next_bank()
            sl3 = slab[:, 0:nk * width].rearrange("p (k e) -> p k e", k=nk)

            def fn(e):
                i = None
                for k in range(nk):
                    i = e.matmul(bank(b)[:, 0:T], lhsT=sl3[:, k, j * 128:(j + 1) * 128], rhs=rhs3[:, k, 0:T],
                                 start=(k == 0), stop=(k == nk - 1))
                return i

            S.op("pe", fn, reads=[slabkey] + rkeys, writes=["ps%d" % b])
            return bank(b)[:, 0:T], "ps%d" % b

        def hist_in(eng, hist, c, nseg, H, U3, ukey, hkey):
            S.op(eng, lambda e: e.tensor_copy(out=U3[:, :, 0:H],
                                              in_=hist[:, c * nseg * H:(c + 1) * nseg * H].rearrange("p (s h) -> p s h", s=nseg)),
                 reads=[hkey], writes=[ukey + "h"])

        def hist_out(eng, hist, c, nseg, H, L, U3, ukey, hkey):
            S.op(eng, lambda e: e.tensor_copy(out=hist[:, c * nseg * H:(c + 1) * nseg * H].rearrange("p (s h) -> p s h", s=nseg),
                                              in_=U3[:, :, L:L + H]),
                 reads=[ukey, ukey + "h"], writes=[hkey])

        def state_out(hist, nch, R, dst2d, scale, hkey):
            for c0 in range(0, nch, 4):
                n = min(4, nch - c0)
                b = next_bank()

                def fn(e, c0=c0, n=n, b=b):
                    i = None
                    for j in range(n):
                        i = e.transpose(out=bank(b)[0:R, j * 128:(j + 1) * 128],
                                        in_=hist[:, (c0 + j) * R:(c0 + j + 1) * R], identity=IDENT)
                    return i

                S.op("pe", fn, reads=[hkey, "CONST"], writes=["ps%d" % b])
                S.op("act", lambda e, n=n, b=b: e.activation(out=STG[0:R, 0:n * 128], in_=bank(b)[0:R, 0:n * 128],
                                                             func=AF.Copy, scale=scale),
                     reads=["ps%d" % b], writes=["XIN0"])
                S.dma("act", "stout", lambda e, c0=c0, n=n: e.dma_start(out=dst2d[:, c0 * 128:(c0 + n) * 128],
                                                                         in_=STG[0:R, 0:n * 128]),
                      reads=["XIN0"])

        def state_in(hist, nch, R, src2d, scale, hkey):
            for c0 in range(0, nch, 4):
                n = min(4, nch - c0)
                S.dma("sp", "stin", lambda e, c0=c0, n=n: e.dma_start(out=STG[0:R, 0:n * 128],
                                                                       in_=src2d[:, c0 * 128:(c0 + n) * 128]),
                      writes=["XIN0"])
                b = next_bank()

                def fn(e, n=n, b=b):
                    i = None
                    for j in range(n):
                        i = e.transpose(out=bank(b)[:, j * R:(j + 1) * R], in_=STG[0:R, j * 128:(j + 1) * 128],
                                        identity=IDENT[0:R, 0:R])
                    return i

                S.op("pe", fn, reads=["XIN0", "CONST"], writes=["ps%d" % b])
                S.op("act", lambda e, c0=c0, n=n, b=b: e.activation(out=hist[:, c0 * R:(c0 + n) * R],
                                                                     in_=bank(b)[:, 0:n * R], func=AF.Copy, scale=scale),
                     reads=["ps%d" % b], writes=[hkey])

        def ssm_in(Sbuf, skey, src2d, SBF, bk):
            S.dma("sp", "ssin", lambda e: e.dma_start(out=v3(STG2, 8), in_=src2d.rearrange("(c q) n -> q c n", q=128)),
                  writes=["XIN1"])
            for hf in range(2):
                b = next_bank()

                def fn(e, hf=hf, b=b):
                    i = None
                    for j in range(4):
                        c = hf * 4 + j
                        i = e.transpose(out=bank(b)[:, j * 128:(j + 1) * 128], in_=STG2[:, c * 128:(c + 1) * 128],
                                        identity=IDENT)
                    return i

                S.op("pe", fn, reads=["XIN1", "CONST"], writes=["ps%d" % b])
                S.op("act", lambda e, hf=hf, b=b: e.activation(out=Sbuf[:, hf * 512:(hf + 1) * 512], in_=bank(b),
                                                               func=AF.Copy), reads=["ps%d" % b], writes=[skey + "_%d" % hf])
                S.op("dve", lambda e, hf=hf, b=b: e.tensor_copy(out=SBF[:, hf * 512:(hf + 1) * 512], in_=bank(b)),
                     reads=["ps%d" % b], writes=[bk + "_%d" % hf])

        def ssm_out(Sbuf, skey, dst2d):
            for hf in range(2):
                b = next_bank()

                def fn(e, hf=hf, b=b):
                    i = None
                    for j in range(4):
                        c = hf * 4 + j
                        i = e.transpose(out=bank(b)[:, j * 128:(j + 1) * 128], in_=Sbuf[:, c * 128:(c + 1) * 128],
                                        identity=IDENT)
                    return i

                S.op("pe", fn, reads=[skey + "_%d" % hf, "CONST"], writes=["ps%d" % b])
                S.op("act", lambda e, hf=hf, b=b: e.activation(out=STG2[:, hf * 512:(hf + 1) * 512], in_=bank(b),
                                                               func=AF.Copy), reads=["ps%d" % b], writes=["XIN1"])
                S.dma("act", "ssout", lambda e, hf=hf: e.dma_start(
                    out=dst2d.rearrange("(c q) n -> q c n", q=128)[:, hf * 4:(hf + 1) * 4, :],
                    in_=STG2[:, hf * 512:(hf + 1) * 512].rearrange("q (c n) -> q c n", c=4)), reads=["XIN1"])

        def layer(l, T, nseg, sample, last):
            L = T // nseg
            Q = min(128, L)
            nck = T // Q
            hist = {k: (HISTS[k] if sample else HISTP[k][l]) for k in HIST_W}
            hkey = {k: ("HS" + k if sample else "HP%s%d" % (k, l)) for k in HIST_W}

            def seg3(ap):
                return ap.rearrange("p (s t) -> p s t", s=nseg)

            if sample:
                state_in(hist["a"], 4, 8, sta[l * 8:(l + 1) * 8, :], 1.0, hkey["a"])
                state_in(hist["b"], 12, 12, stb[l * 12:(l + 1) * 12, :], 1.0, hkey["b"])
                state_in(hist["c"], 4, 120, stc[l * 120:(l + 1) * 120, :], 2.0, hkey["c"])
                state_in(hist["f"], 22, 8, stf[l * 8:(l + 1) * 8, :], 1.0, hkey["f"])

            S.op("dve", lambda e: e.memset(FENCE, 0.0), reads=["M"], writes=["M"])
            rmsnorm(T, lambda c: pcol(l, "g1", c), lambda c: HB3[:, c, 0:T], lambda c: ["XT%d" % c],
                    lambda c: "HB", EPS)

            if DBG['phase'] < 2:
                return
            for c in range(4):
                slab, skey, jj = wchunk(l, c * 3)
                pv, pvk = proj(T, slab, skey, jj)
                av = AVt[c % 2]
                S.op("act", lambda e, av=av, pv=pv: e.activation(out=av[:, 0:T], in_=pv, func=AF.Copy),
                     reads=[pvk, "M"], writes=["AV%d" % (c % 2)])
                slab, skey, jj = wchunk(l, c * 3 + 1)
                pc_, pck = proj(T, slab, skey, jj)
                ua = UAt[c % 2]
                U3 = ua[:, 0:nseg * (L + 2)].rearrange("p (s t) -> p s t", s=nseg)
                uk = "UA%d" % (c % 2)
                hist_in("dve", hist["a"], c, nseg, 2, U3, uk, hkey["a"])
                S.op("dve", lambda e, U3=U3, pc_=pc_, av=av: e.tensor_tensor(out=U3[:, :, 2:2 + L], in0=seg3(pc_),
                                                                              in1=seg3(av[:, 0:T]), op=ALU.mult),
                     reads=[pck, "AV%d" % (c % 2), "M"], writes=[uk])
                hist_out("dve", hist["a"], c, nseg, 2, L, U3, uk, hkey["a"])
                ca = CAt[c % 2]

                def convA(e, U3=U3, ca=ca, c=c):
                    o = seg3(ca[:, 0:T])
                    e.tensor_scalar(out=o, in0=U3[:, :, 0:L], scalar1=pcol(l, "caw", c * 3 + 0), scalar2=None, op0=ALU.mult)
                    e.scalar_tensor_tensor(out=o, in0=U3[:, :, 1:1 + L], scalar=pcol(l, "caw", c * 3 + 1), in1=o,
                                           op0=ALU.mult, op1=ALU.add)
                    return e.scalar_tensor_tensor(out=o, in0=U3[:, :, 2:2 + L], scalar=pcol(l, "caw", c * 3 + 2), in1=o,
                                                  op0=ALU.mult, op1=ALU.add)

                S.op("dve", convA, reads=[uk, uk + "h", "PAR", "M"], writes=["CA%d" % (c % 2)])
                slab, skey, jj = wchunk(l, c * 3 + 2)
                pb_, pbk = proj(T, slab, skey, jj)
                S.op("dve", lambda e, pb_=pb_, ca=ca, c=c: e.tensor_tensor(out=YC3[:, c, 0:T], in0=pb_, in1=ca[:, 0:T],
                                                                           op=ALU.mult),
                     reads=[pbk, "CA%d" % (c % 2)], writes=["YC%d" % c])
            if sample or last:
                R = nseg * 2
                dst = (o_sa if sample else o_pa)[l * R:(l + 1) * R, :]
                state_out(hist["a"], 4, R, dst, 1.0, hkey["a"])

            if DBG['phase'] < 3:
                return
            for c in range(4):
                slab, skey, jj = wchunk(l, 12 + c * 2)
                pg, pgk = proj(T, slab, skey, jj)
                tg = TG[c % 2]
                S.op("act", lambda e, tg=tg, pg=pg: e.activation(out=tg[:, 0:T], in_=pg, func=AF.Tanh, scale=0.5),
                     reads=[pgk, "M"], writes=["TG%d" % (c % 2)])
                slab, skey, jj = wchunk(l, 12 + c * 2 + 1)
                pa_, pak = proj(T, slab, skey, jj)
                uc = UCt[c % 2]
                U3 = uc[:, 0:nseg * (L + 30)].rearrange("p (s t) -> p s t", s=nseg)
                uk = "UC%d" % (c % 2)
                hist_in("dve", hist["c"], c, nseg, 30, U3, uk, hkey["c"])
                S.op("dve", lambda e, U3=U3, tg=tg, pa_=pa_: e.scalar_tensor_tensor(
                    out=U3[:, :, 30:30 + L], in0=seg3(tg[:, 0:T]), scalar=1.0, in1=seg3(pa_), op0=ALU.add, op1=ALU.mult),
                     reads=[pak, "TG%d" % (c % 2), "M"], writes=[uk])
                hist_out("dve", hist["c"], c, nseg, 30, L, U3, uk, hkey["c"])

                def convC(e, U3=U3, c=c):
                    o = seg3(CC3[:, c, 0:T])
                    tmp = seg3(PTMP[:, 0:T])
                    wh = CCWH[:, l * 124 + c * 31: l * 124 + (c + 1) * 31]
                    i = e.tensor_scalar(out=o, in0=U3[:, :, 0:L], scalar1=wh[:, 0:1], scalar2=pcol(l, "ccb", c),
                                        op0=ALU.mult, op1=ALU.add)
                    for k in range(1, 31):
                        i = e.scalar_tensor_tensor(out=o, in0=U3[:, :, k:k + L], scalar=wh[:, k:k + 1], in1=o,
                                                   op0=ALU.mult, op1=ALU.add)
                    return i

                S.op("dve", convC, reads=[uk, uk + "h", "CCWH", "PAR", "M"], writes=["CC%d" % c])
            if sample or last:
                R = nseg * 30
                dst = (o_sc if sample else o_pc)[l * R:(l + 1) * R, :]
                state_out(hist["c"], 4, R, dst, 0.5, hkey["c"])
            for c in range(4):
                sq = SQ[c % 2]
                S.op("act", lambda e, c=c, sq=sq: e.activation(out=sq[:, 0:T], in_=CC3[:, c, 0:T], func=AF.Square),
                     reads=["CC%d" % c], writes=["SQ%d" % (c % 2)])
                S.op("pe", lambda e, c=c: e.matmul(bank(4)[:, 0:T], lhsT=ONES5, rhs=CC3[:, c, 0:T], start=(c == 0),
                                                   stop=(c == 3)), reads=["CC%d" % c, "CONST"], writes=["ps4"])
                S.op("pe", lambda e, c=c, sq=sq: e.matmul(bank(5)[:, 0:T], lhsT=ONES5, rhs=sq[:, 0:T], start=(c == 0),
                                                          stop=(c == 3)), reads=["SQ%d" % (c % 2), "CONST"], writes=["ps5", "ps5c", "ps5b"])
            S.op("act", lambda e: e.activation(out=MEAN[:, 0:T], in_=bank(4)[:, 0:T], func=AF.Copy),
                 reads=["ps4", "M"], writes=["MEAN"])
            S.op("dve", lambda e: e.tensor_tensor(out=VAR[:, 0:T], in0=MEAN[:, 0:T], in1=MEAN[:, 0:T], op=ALU.mult),
                 reads=["MEAN", "M"], writes=["VAR"])
            S.op("dve", lambda e: e.scalar_tensor_tensor(out=VAR[:, 0:T], in0=bank(5)[:, 0:T], scalar=EPS, in1=VAR[:, 0:T],
                                                         op0=ALU.add, op1=ALU.subtract),
                 reads=["ps5", "VAR"], writes=["VAR"])
            def rsc_fn(e):
                e.activation(out=RSTC[:, 0:T], in_=VAR[:, 0:T], func=AF.Ln)
                return e.activation(out=RSTC[:, 0:T], in_=RSTC[:, 0:T], func=AF.Exp, scale=-0.5)

            S.op("act", rsc_fn, reads=["VAR", "M"], writes=["RSTC"])
            for c in range(4):
                tn = TN[c % 2]

                def lnn(e, c=c, tn=tn):
                    e.tensor_tensor(out=tn[:, 0:T], in0=CC3[:, c, 0:T], in1=MEAN[:, 0:T], op=ALU.subtract)
                    return e.tensor_tensor(out=tn[:, 0:T], in0=tn[:, 0:T], in1=RSTC[:, 0:T], op=ALU.mult)

                S.op("dve", lnn, reads=["CC%d" % c, "MEAN", "RSTC", "M"], writes=["TN%d" % (c % 2)])
                S.op("act", lambda e, c=c, tn=tn: e.activation(out=YC3[:, 12 + c, 0:T], in_=tn[:, 0:T], func=AF.Silu,
                                                               scale=pcol(l, "lng", c), bias=pcol(l, "lnb", c)),
                     reads=["TN%d" % (c % 2), "PAR"], writes=["YC%d" % (12 + c)])

            if DBG['phase'] < 4:
                return
            pend_b = []

            def flush_b():
                while pend_b:
                    dst_, co_, j_, dk_ = pend_b.pop(0)
                    S.op("act", lambda e, dst_=dst_, co_=co_: e.activation(out=dst_, in_=co_[:, 0:T], func=AF.Silu),
                         reads=["CBo%d" % (j_ % 2)], writes=dk_)

            for j in range(12):
                slab, skey, jj = wchunk(l, 20 + j)
                px, pxk = proj(T, slab, skey, jj)
                ub = UBt[j % 3]
                U3 = ub[:, 0:nseg * (L + 3)].rearrange("p (s t) -> p s t", s=nseg)
                uk = "UB%d" % (j % 3)
                hist_in("dve", hist["b"], j, nseg, 3, U3, uk, hkey["b"])
                S.op("act", lambda e, U3=U3, px=px: e.activation(out=U3[:, :, 3:3 + L], in_=seg3(px), func=AF.Copy),
                     reads=[pxk], writes=[uk])
                flush_b()
                hist_out("dve", hist["b"], j, nseg, 3, L, U3, uk, hkey["b"])
                co = CBo[j % 2]

                def convB(e, U3=U3, co=co, j=j):
                    o = seg3(co[:, 0:T])
                    e.tensor_scalar(out=o, in0=U3[:, :, 0:L], scalar1=pcol(l, "cbw", j * 4), scalar2=pcol(l, "cbb", j),
                                    op0=ALU.mult, op1=ALU.add)
                    i = None
                    for k in range(1, 4):
                        i = e.scalar_tensor_tensor(out=o, in0=U3[:, :, k:k + L], scalar=pcol(l, "cbw", j * 4 + k), in1=o,
                                                   op0=ALU.mult, op1=ALU.add)
                    return i

                S.op("dve", convB, reads=[uk, uk + "h", "PAR"], writes=["CBo%d" % (j % 2)])
                if j < 8:
                    dst, dk = XS3[:, j, 0:T], ["XS%d_%d" % (j, ci) for ci in range(nck)]
                elif j < 10:
                    dst, dk = BT3[:, j - 8, 0:T], ["BT%d" % (j - 8)]
                else:
                    dst, dk = CT3[:, j - 10, 0:T], ["CT%d" % (j - 10)]
                pend_b.append((dst, co, j, dk))
            flush_b()
            if sample or last:
                R = nseg * 3
                dst = (o_sb if sample else o_pb)[l * R:(l + 1) * R, :]
                state_out(hist["b"], 12, R, dst, 1.0, hkey["b"])

            if DBG['phase'] < 5:
                return
            S.op("dve", lambda e: e.memset(FENCE, 0.0), reads=["M"], writes=["M"])
            b5 = bank(5)
            for ci in range(nck if DBG.get("dtv", 9) >= 1 else 0):
                def dtmm(e, ci=ci):
                    i = None
                    w3 = WDT[l].rearrange("p (k e) -> p k e", k=8)
                    for k in range(8):
                        i = e.matmul(b5[0:Q, ci * 16:(ci + 1) * 16], lhsT=HB3[:, k, ci * Q:(ci + 1) * Q], rhs=w3[:, k, :],
                                     start=(k == 0), stop=(k == 7))
                    return i
                S.op("pe", dtmm, reads=["HB", "WDT%d" % l], writes=["ps5"])
            nd = nck * 16
            if DBG.get("dtv", 9) < 2:
                return
            S.op("dve", lambda e: e.tensor_tensor(out=DTP[0:Q, 0:nd].rearrange("p (c h) -> p c h", c=nck),
                                                  in0=b5[0:Q, 0:nd].rearrange("p (c h) -> p c h", c=nck),
                                                  in1=pcol(l, "dtb", 0, 16)[0:Q].unsqueeze(1).to_broadcast([Q, nck, 16]),
                                                  op=ALU.add), reads=["ps5", "PAR", "M"], writes=["DTP"])
            if DBG.get("dtv", 9) < 3:
                return
            if DBG.get("dtact", 3) & 1:
                S.op("act", lambda e: e.activation(out=DTE[0:Q, 0:nd], in_=DTP[0:Q, 0:nd], func=AF.Exp),
                     reads=["DTP", "M"], writes=["DTE"])
            else:
                S.op("dve", lambda e: e.tensor_copy(out=DTE[0:Q, 0:nd], in_=DTP[0:Q, 0:nd]), reads=["DTP", "M"], writes=["DTE"])
            if DBG.get("dtact", 3) & 2:
                S.op("act", lambda e: e.activation(out=DTT[0:Q, 0:nd], in_=DTE[0:Q, 0:nd], func=AF.Ln, bias=ONES[0:Q, 0:1]),
                     reads=["DTE", "CONST", "M"], writes=["DTT"])
            else:
                S.op("dve", lambda e: e.tensor_copy(out=DTT[0:Q, 0:nd], in_=DTE[0:Q, 0:nd]), reads=["DTE", "M"], writes=["DTT"])
            for _ in range(DBG.get("xfence", 0)):
                S.op(DBG.get("xeng", "dve"), lambda e: e.memset(FENCE, 0.0), reads=["M"], writes=["M"])
            if DBG.get("dtv", 9) < 4:
                return
            def daa_fn(e):
                i = None
                for ci_ in range(nck):
                    i = e.tensor_tensor(out=DAA[0:Q, ci_ * 16:(ci_ + 1) * 16], in0=DTT[0:Q, ci_ * 16:(ci_ + 1) * 16],
                                        in1=(ABC[0:Q, l * 16:(l + 1) * 16] if DBG.get("useabc", 1) else pcol(l, "alog", 0, 16)[0:Q]), op=ALU.mult)
                return i

            if DBG.get("useabc", 1) == 2:
                S.op("dve", lambda e: e.tensor_tensor(out=DAA[0:Q, 0:nd], in0=DTT[0:Q, 0:nd], in1=DTE[0:Q, 0:nd], op=ALU.mult),
                     reads=["DTT", "DTE", "M"], writes=["DAA"])
            elif DBG.get("useabc", 1) == 3:
                S.op("dve", lambda e: e.tensor_copy(out=DAA[0:Q, 0:nd], in_=DTT[0:Q, 0:nd]),
                     reads=["DTT", "M"], writes=["DAA"])
            elif DBG.get("useabc", 1) == 4:
                S.op("pool", daa_fn, reads=["DTT", "ABC", "M"], writes=["DAA"])
            else:
                S.op("dve", daa_fn, reads=["DTT", "ABC", "M"], writes=["DAA"])

            if DBG["ssdcut"] < 1:
                return
            Sb = SSMP if sample else SP_[l]
            sk = "SS" if sample else "SP%d" % l
            SBF = SBFS if sample else SBFP[l]
            bk = "SBFS" if sample else "SBFP%d" % l
            negm = NEGM128 if Q == 128 else NEGM32
            p01 = psum[:, 0:1024]
            p67 = psum[:, 6 * 512: 6 * 512 + 8 * Q]
            psB = b5[:, 256:384].bitcast(BF16)
            MT3 = MT.rearrange("p (h q) -> p h q", h=16)
            X3 = XDTP.rearrange("p (h m) -> p h m", h=16)
            CDEs = [CDE, CDEb]
            xsk = lambda ci: ["XS%d_%d" % (c, ci) for c in range(8)]

            def F1(ci):
                dA = DAA[0:Q, ci * 16:(ci + 1) * 16]
                dtc = DTT[0:Q, ci * 16:(ci + 1) * 16]
                cde = CDEs[ci % 2]

                def cums(e):
                    e.matmul(b5[0:Q, 64:80], lhsT=TRI_I[0:Q, 0:Q], rhs=dA, start=True, stop=True)
                    e.matmul(b5[0:Q, 80:96], lhsT=TRI_G[0:Q, 0:Q], rhs=dA, start=True, stop=True)
                    e.matmul(b5[:, 96:112], lhsT=ONES[0:Q, :], rhs=dA, start=True, stop=True)
                    return e.matmul(b5[0:16, 128:128 + Q], lhsT=dA, rhs=TRI_I[0:Q, 0:Q], start=True, stop=True)

                S.op("pe", cums, reads=["DAA", "CONST"], writes=["ps5c"])
                S.op("act", lambda e: e.activation(out=E32[0:Q, :], in_=b5[0:Q, 64:96], func=AF.Exp),
                     reads=["ps5c", "M"], writes=["E32"])
                S.op("act", lambda e: e.activation(out=cde, in_=b5[:, 96:112], func=AF.Exp),
                     reads=["ps5c", "M"], writes=["CDE%d" % (ci % 2)])
                S.op("dve", lambda e: e.tensor_copy(out=ACS[0:16, 0:Q], in_=b5[0:16, 128:128 + Q]),
                     reads=["ps5c", "M"], writes=["ACS"])
                S.op("dve", lambda e: e.tensor_scalar(out=NAC[0:Q, :], in0=b5[0:Q, 64:80], scalar1=-1.0, scalar2=None,
                                                      op0=ALU.mult), reads=["ps5c", "M"], writes=["NAC"])
                S.op("dve", lambda e: e.tensor_tensor(out=SCXW[0:Q, :], in0=dtc, in1=E32[0:Q, 16:32], op=ALU.mult),
                     reads=["DTT", "E32", "M"], writes=["SCXW"])

            def F2(ci):
                q0 = ci * Q
                dtc = DTT[0:Q, ci * 16:(ci + 1) * 16]

                def xtr(e):
                    i = None
                    for c in range(8):
                        i = e.transpose(out=p01[0:Q, c * 128:(c + 1) * 128], in_=XS3[:, c, q0:q0 + Q], identity=IDENT)
                    return i

                S.op("pe", xtr, reads=xsk(ci) + ["CONST"], writes=["ps0", "ps1"])

                def btr(e):
                    i = None
                    for g in range(2):
                        i = e.transpose(out=psB[0:Q, g * 128:(g + 1) * 128], in_=BT3[:, g, q0:q0 + Q], identity=IDENTB)
                    return i

                S.op("pe", btr, reads=["BT0", "BT1", "IDENTB"], writes=["ps5b"])

                def xdtop(e):
                    i = None
                    for hf in range(2):
                        xdo = bass.AP(XDTP.tensor, XDTP.offset + hf * 1024, [list(XDTP.ap[0]), [256, 4], [192, 2], [1, 64]])
                        i = e.tensor_tensor(
                            out=xdo[0:Q], in0=p01[0:Q, hf * 512:(hf + 1) * 512].rearrange("p (c t m) -> p c t m", c=4, t=2),
                            in1=dtc[:, hf * 8:(hf + 1) * 8].rearrange("p (c t) -> p c t", c=4).unsqueeze(3).to_broadcast([Q, 4, 2, 64]),
                            op=ALU.mult)
                    return i

                S.op("dve", xdtop, reads=["ps0", "ps1", "DTT", "M"], writes=["XDTP"])

                def xwop(e):
                    i = None
                    for hf in range(2):
                        i = e.tensor_tensor(
                            out=XW[0:Q, hf * 512:(hf + 1) * 512].rearrange("p (h m) -> p h m", h=8),
                            in0=p01[0:Q, hf * 512:(hf + 1) * 512].rearrange("p (h m) -> p h m", h=8),
                            in1=SCXW[0:Q, hf * 8:(hf + 1) * 8].unsqueeze(2).to_broadcast([Q, 8, 64]), op=ALU.mult)
                    return i

                S.op("dve", xwop, reads=["ps0", "ps1", "SCXW", "M"], writes=["XW"])
                S.op("act", lambda e: e.activation(out=BTOK[0:Q, :], in_=psB[0:Q, :], func=AF.Copy),
                     reads=["ps5b", "M"], writes=["BTOK"])

                def cbt(e):
                    i = None
                    for g in range(2):
                        i = e.matmul(bank(4)[0:Q, g * 128:g * 128 + Q], lhsT=BT3[:, g, q0:q0 + Q], rhs=CT3[:, g, q0:q0 + Q],
                                     start=True, stop=True)
                    return i

                S.op("pe", cbt, reads=["BT0", "BT1", "CT0", "CT1"], writes=["ps4"])

                def ebc(e):
                    i = None
                    for c in range(8):
                        i = e.matmul(p67[:, c * Q:(c + 1) * Q], lhsT=SEL2[0:16, c * 128:(c + 1) * 128], rhs=ACS[0:16, 0:Q],
                                     start=True, stop=True)
                    return i

                S.op("pe", ebc, reads=["ACS", "CONST"], writes=["ps6", "ps7"])

                def ebexp(e):
                    hw_ = 4 * Q
                    e.activation(out=EB[:, 0:hw_], in_=p67[:, 0:hw_], func=AF.Exp)
                    return e.activation(out=EB[:, hw_:2 * hw_], in_=p67[:, hw_:2 * hw_], func=AF.Exp)

                S.op("act", ebexp, reads=["ps6", "ps7", "M"], writes=["EB"])

            def F5(ci):
                dA = DAA[0:Q, ci * 16:(ci + 1) * 16]
                for r in range(4):
                    bb = 2 + r % 2
                    pl = bank(bb)

                    def lmm(e, r=r, pl=pl):
                        e.matmul(pl[0:Q, 0:4 * Q], lhsT=IDENT[0:Q, 0:Q], rhs=negm[0:Q, 0:4 * Q], start=True, stop=False)
                        i = None
                        for i4 in range(4):
                            h = 4 * r + i4
                            i = e.matmul(pl[0:Q, i4 * Q:(i4 + 1) * Q], lhsT=dA[:, h:h + 1].to_broadcast([Q, Q]),
                                         rhs=TRI_I[0:Q, 0:Q], start=False, stop=(i4 == 3), skip_group_check=True)
                        return i

                    S.op("pe", lmm, reads=["DAA", "CONST"], writes=["ps%d" % bb])
                    lt = LT[r % 2]

                    def lexp(e, r=r, pl=pl, lt=lt):
                        i = None
                        for i4 in range(4):
                            h = 4 * r + i4
                            i = e.activation(out=lt[0:Q, i4 * Q:(i4 + 1) * Q], in_=pl[0:Q, i4 * Q:(i4 + 1) * Q], func=AF.Exp,
                                             bias=NAC[0:Q, h:h + 1])
                        return i

                    S.op("act", lexp, reads=["ps%d" % bb, "NAC", "M"], writes=["LT%d" % (r % 2)])
                    g = r // 2
                    S.op("dve", lambda e, r=r, lt=lt, g=g: e.tensor_tensor(
                        out=MT3[0:Q, 4 * r:4 * r + 4, 0:Q], in0=lt[0:Q, 0:4 * Q].rearrange("p (h q) -> p h q", h=4),
                        in1=bank(4)[0:Q, g * 128:g * 128 + Q].unsqueeze(1).to_broadcast([Q, 4, Q]), op=ALU.mult),
                         reads=["LT%d" % (r % 2), "ps4", "M"], writes=["MT%d" % r])

            def B1(ci):
                q0 = ci * Q

                def ymm(e):
                    i = None
                    for c in range(8):
                        e.matmul(p01[:, c * Q:(c + 1) * Q], lhsT=X3[0:Q, 2 * c, :], rhs=MT3[0:Q, 2 * c, 0:Q],
                                 start=True, stop=False)
                        i = e.matmul(p01[:, c * Q:(c + 1) * Q], lhsT=X3[0:Q, 2 * c + 1, :], rhs=MT3[0:Q, 2 * c + 1, 0:Q],
                                     start=False, stop=True)
                    return i

                S.op("pe", ymm, reads=["XDTP", "MT0", "MT1", "MT2", "MT3"], writes=["ps0", "ps1"])

                def omm(e):
                    i = None
                    for c in range(8):
                        i = e.matmul(p67[:, c * Q:(c + 1) * Q], lhsT=SBF[:, c * 128:(c + 1) * 128],
                                     rhs=CT3[:, c // 4, q0:q0 + Q], start=True, stop=True)
                    return i

                S.op("pe", omm, reads=[bk + "_0", bk + "_1", "CT0", "CT1"], writes=["ps6", "ps7"])
                for g in range(2):
                    S.op("pe", lambda e, g=g: e.matmul(bank(2 + g), lhsT=BTOK[0:Q, g * 128:(g + 1) * 128],
                                                       rhs=XW[0:Q, g * 512:(g + 1) * 512], start=True, stop=True),
                         reads=["BTOK", "XW"], writes=["ps%d" % (2 + g)])

            def B2(ci):
                def t1a(e):
                    hw_ = 4 * Q
                    e.tensor_tensor(out=T1[:, 0:hw_], in0=p67[:, 0:hw_], in1=EB[:, 0:hw_], op=ALU.mult)
                    return e.tensor_tensor(out=T1[:, hw_:2 * hw_], in0=p67[:, hw_:2 * hw_], in1=EB[:, hw_:2 * hw_], op=ALU.mult)

                def t1b(e):
                    hw_ = 4 * Q
                    e.tensor_tensor(out=T1[:, 0:hw_], in0=p01[:, 0:hw_], in1=T1[:, 0:hw_], op=ALU.add)
                    return e.tensor_tensor(out=T1[:, hw_:2 * hw_], in0=p01[:, hw_:2 * hw_], in1=T1[:, hw_:2 * hw_], op=ALU.add)

                S.op("dve", t1a, reads=["ps6", "ps7", "EB", "M"], writes=["T1"])
                S.op("dve", t1b, reads=["ps0", "ps1", "T1", "M"], writes=["T1"])

            def B3(ci):
                q0 = ci * Q

                def dsk(e):
                    i = None
                    for c in range(8):
                        i = e.scalar_tensor_tensor(out=XS3[:, c, q0:q0 + Q], in0=XS3[:, c, q0:q0 + Q],
                                                   scalar=pcol(l, "dsk", c), in1=T1[:, c * Q:(c + 1) * Q],
                                                   op0=ALU.mult, op1=ALU.add)
                    return i

                S.op("dve", dsk, reads=["T1", "PAR", "M"] + xsk(ci), writes=xsk(ci))

            def B4(ci):
                cde = CDEs[ci % 2]
                for g in range(2):
                    Sg = Sb[:, g * 512:(g + 1) * 512]
                    S.op("dve", lambda e, g=g, Sg=Sg: e.tensor_tensor(
                        out=Sg.rearrange("p (h m) -> p h m", h=8), in0=Sg.rearrange("p (h m) -> p h m", h=8),
                        in1=cde[:, g * 8:(g + 1) * 8].unsqueeze(2).to_broadcast([128, 8, 64]), op=ALU.mult),
                         reads=["CDE%d" % (ci % 2), sk + "_%d" % g, "M"], writes=[sk + "_%d" % g])
                    S.op("dve", lambda e, g=g, Sg=Sg: e.tensor_tensor(out=Sg, in0=bank(2 + g), in1=Sg, op=ALU.add),
                         reads=["ps%d" % (2 + g), sk + "_%d" % g], writes=[sk + "_%d" % g])
                    S.op("act", lambda e, g=g, Sg=Sg: e.activation(out=SBF[:, g * 512:(g + 1) * 512], in_=Sg, func=AF.Copy),
                         reads=[sk + "_%d" % g], writes=[bk + "_%d" % g])

            if sample:
                for ci in range(nck):
                    ssm_in(Sb, sk, sts[(l * 4 + ci) * 1024:(l * 4 + ci + 1) * 1024, :], SBF, bk)
                    F1(ci); F2(ci); F5(ci); B1(ci); B2(ci); B3(ci); B4(ci)
                    ssm_out(Sb, sk, o_sss[(l * 4 + ci) * 1024:(l * 4 + ci + 1) * 1024, :])
            else:
                F1(0); F2(0); F5(0)
                for ci in range(nck):
                    nxt = ci + 1 < nck
                    B1(ci)
                    B2(ci)
                    if nxt:
                        F1(ci + 1)
                    B4(ci)
                    if nxt:
                        F2(ci + 1)
                    B3(ci)
                    if nxt:
                        F5(ci + 1)
            if last and not sample and DBG.get("ssmout", 1):
                ssm_out(Sb, sk, o_pss[l * 1024:(l + 1) * 1024, :])

            if DBG['phase'] < 6:
                return
            allxs = lambda c: ["XS%d_%d" % (c, ci) for ci in range(nck)]
            for c in range(8):
                slab, skey, jj = wchunk(l, 32 + c)
                pz, pzk = proj(T, slab, skey, jj)
                zt = ZT[c % 2]
                S.op("act", lambda e, zt=zt, pz=pz: e.activation(out=zt[:, 0:T], in_=pz, func=AF.Silu),
                     reads=[pzk], writes=["ZT%d" % (c % 2)])
                S.op("dve", lambda e, zt=zt, c=c: e.tensor_tensor(out=XS3[:, c, 0:T], in0=XS3[:, c, 0:T], in1=zt[:, 0:T],
                                                                 op=ALU.mult),
                     reads=["ZT%d" % (c % 2)] + allxs(c), writes=allxs(c))
            rmsnorm(T, lambda c: pcol(l, "gssm", c), lambda c: YC3[:, 4 + c, 0:T], lambda c: allxs(c),
                    lambda c: "YC%d" % (4 + c), EPS, src3=XS3)

            if DBG['phase'] < 7:
                return
            for si in range(4):
                slab, skey = get_slab("wout", l, si)
                for m in range(2):
                    po, pok = proj(T, slab, skey, m, width=256, nk=16, rhs3=YC3, rkeys=["YC%d" % k for k in range(16)])
                    cc = si * 2 + m
                    S.op("dve", lambda e, po=po, cc=cc: e.tensor_tensor(out=XT3[:, cc, 0:T], in0=po, in1=XT3[:, cc, 0:T],
                                                                        op=ALU.add),
                         reads=[pok, "XT%d" % cc], writes=["XT%d" % cc])

            if DBG['phase'] < 8:
                return
            rmsnorm(T, lambda c: pcol(l, "g2", c), lambda c: HB3[:, c, 0:T], lambda c: ["XT%d" % c],
                    lambda c: "HB", EPS)
            S.op("dve", lambda e: e.memset(FENCE, 0.0), reads=["M"], writes=["M"])
            pend_f = []

            def flush_f():
                while pend_f:
                    gs_, cf_, j_, pu_, puk_ = pend_f.pop(0)
                    S.op("act", lambda e, gs_=gs_, cf_=cf_: e.activation(out=gs_[:, 0:T], in_=cf_[:, 0:T], func=AF.Silu),
                         reads=["CF%d" % (j_ % 2), "M"], writes=["GS%d" % (j_ % 2)])
                    S.op("dve", lambda e, pu_=pu_, gs_=gs_, j_=j_: e.tensor_tensor(out=FA3[:, j_, 0:T], in0=pu_, in1=gs_[:, 0:T],
                                                                                   op=ALU.mult),
                         reads=[puk_, "GS%d" % (j_ % 2), "M"], writes=["FA%d" % j_])

            for si in range(11):
                slab, skey = get_slab("wup", l, si)
                for jj in range(2):
                    j = si * 2 + jj
                    pg, pgk = proj(T, slab, skey, 2 * jj)
                    uf = UFt[j % 3]
                    U3 = uf[:, 0:nseg * (L + 2)].rearrange("p (s t) -> p s t", s=nseg)
                    uk = "UF%d" % (j % 3)
                    hist_in("dve", hist["f"], j, nseg, 2, U3, uk, hkey["f"])
                    S.op("act", lambda e, U3=U3, pg=pg: e.activation(out=U3[:, :, 2:2 + L], in_=seg3(pg), func=AF.Copy),
                         reads=[pgk, "M"], writes=[uk])
                    flush_f()
                    hist_out("dve", hist["f"], j, nseg, 2, L, U3, uk, hkey["f"])
                    cf = CFt[j % 2]

                    def convF(e, U3=U3, cf=cf, j=j):
                        o = seg3(cf[:, 0:T])
                        e.tensor_scalar(out=o, in0=U3[:, :, 0:L], scalar1=pcol(l, "cfw", j * 3), scalar2=None, op0=ALU.mult)
                        e.scalar_tensor_tensor(out=o, in0=U3[:, :, 1:1 + L], scalar=pcol(l, "cfw", j * 3 + 1), in1=o,
                                               op0=ALU.mult, op1=ALU.add)
                        return e.scalar_tensor_tensor(out=o, in0=U3[:, :, 2:2 + L], scalar=pcol(l, "cfw", j * 3 + 2), in1=o,
                                                      op0=ALU.mult, op1=ALU.add)

                    S.op("dve", convF, reads=[uk, uk + "h", "PAR", "M"], writes=["CF%d" % (j % 2)])
                    gs = GS[j % 2]
                    pu, puk = proj(T, slab, skey, 2 * jj + 1)
                    pend_f.append((gs, cf, j, pu, puk))
            flush_f()
            if sample or last:
                R = nseg * 2
                dst = (o_sf if sample else o_pf)[l * R:(l + 1) * R, :]
                state_out(hist["f"], 22, R, dst, 1.0, hkey["f"])
            for si in range(8):
                slab, skey = get_slab("wdn", l, si)
                pd, pdk = proj(T, slab, skey, 0, width=128, nk=22, rhs3=FA3, rkeys=["FA%d" % k for k in range(22)])
                S.op("dve", lambda e, pd=pd, si=si: e.tensor_tensor(out=XT3[:, si, 0:T], in0=pd, in1=XT3[:, si, 0:T],
                                                                    op=ALU.add),
                     reads=[pdk, "XT%d" % si], writes=["XT%d" % si])

        def tile(xsrc, ydst, T, nseg, sample, last):
            nblk = T // 128
            for blk in range(nblk):
                xi = XIN[blk % 2]
                S.dma("sp", "xin%d" % (blk % 2), lambda e, blk=blk, xi=xi: e.dma_start(out=xi, in_=xsrc[blk * 128:(blk + 1) * 128, :]),
                      writes=["XIN%d" % (blk % 2)])
                for hf in range(2):
                    b = next_bank()

                    def fn(e, hf=hf, b=b, xi=xi):
                        i = None
                        for j in range(4):
                            c = hf * 4 + j
                            i = e.transpose(out=bank(b)[:, j * 128:(j + 1) * 128], in_=xi[:, c * 128:(c + 1) * 128],
                                            identity=IDENT)
                        return i

                    S.op("pe", fn, reads=["XIN%d" % (blk % 2), "CONST"], writes=["ps%d" % b])
                    S.op("act", lambda e, hf=hf, b=b, blk=blk: e.activation(
                        out=XT3[:, hf * 4:(hf + 1) * 4, blk * 128:(blk + 1) * 128],
                        in_=bank(b).rearrange("p (c t) -> p c t", c=4), func=AF.Copy),
                         reads=["ps%d" % b], writes=["XT%d" % c for c in range(hf * 4, hf * 4 + 4)])
            for l in range(DBG["layers"]):
                layer(l, T, nseg, sample, last)
            S.op("dve", lambda e: e.memset(FENCE, 0.0), reads=["M"], writes=["M"])
            for c in range(8):
                stat_chunk(T, c, XT3, ["XT%d" % c])
            S.op("dve", lambda e: e.tensor_scalar(out=STAT[:, 0:T], in0=bank(4)[:, 0:T], scalar1=EPS, scalar2=None,
                                                  op0=ALU.add), reads=["ps4"], writes=["STAT"])
            def rs_fn(e):
                e.activation(out=RSTD[:, 0:T], in_=STAT[:, 0:T], func=AF.Ln)
                return e.activation(out=RSTD[:, 0:T], in_=RSTD[:, 0:T], func=AF.Exp, scale=-0.5)

            S.op("act", rs_fn, reads=["STAT"], writes=["RSTD"])
            gf = PAR[:, NL * NPAR: NL * NPAR + 8]
            for blk in range(nblk):
                yf = YFt[blk % 2]
                yf3 = v3(yf, 8)

                def fin(e, blk=blk, yf3=yf3):
                    i = None
                    for c in range(8):
                        i = e.scalar_tensor_tensor(out=yf3[:, c, :], in0=XT3[:, c, blk * 128:(blk + 1) * 128],
                                                   scalar=gf[:, c:c + 1], in1=RSTD[:, blk * 128:(blk + 1) * 128],
                                                   op0=ALU.mult, op1=ALU.mult)
                    return i

                S.op("dve", fin, reads=["XT%d" % c for c in range(8)] + ["RSTD", "PAR", "M"], writes=["YF%d" % (blk % 2)])
                xi = XIN[blk % 2]
                for hf in range(2):
                    b = next_bank()

                    def fn(e, hf=hf, b=b, yf3=yf3):
                        i = None
                        for j in range(4):
                            c = hf * 4 + j
                            i = e.transpose(out=bank(b)[:, j * 128:(j + 1) * 128], in_=yf3[:, c, :], identity=IDENT)
                        return i

                    S.op("pe", fn, reads=["YF%d" % (blk % 2), "CONST"], writes=["ps%d" % b])
                    S.op("act", lambda e, hf=hf, b=b, xi=xi: e.activation(out=xi[:, hf * 512:(hf + 1) * 512], in_=bank(b),
                                                                         func=AF.Copy),
                         reads=["ps%d" % b], writes=["XIN%d" % (blk % 2)])
                S.dma("act", "yout%d" % (blk % 2), lambda e, blk=blk, xi=xi: e.dma_start(out=ydst[blk * 128:(blk + 1) * 128, :], in_=xi),
                      reads=["XIN%d" % (blk % 2)])

        for t in range(NT):
            tile(xp[t * TP:(t + 1) * TP, :], yp[t * TP:(t + 1) * TP, :], TP, 1, False, t == NT - 1)
        if with_sample:
            tile(xs, ys, 128, 4, True, False)

        for e_ in ("sp", "act", "pool"):
            pass
        S.final_wait("sp")
        import os as _os2
        if _os2.environ.get("DUMPLOG"):
            with open(_os2.environ["DUMPLOG"], "w") as f:
                for t in S.log:
                    f.write(repr(t) + "\n")
            print("counts", S.cnt, {k: v[1] for k, v in S.dsem.items()})
        block = es.enter_context(nc.Block())
        S.emit(block)
    return nc


def _pack_params(inp):
    out = np.zeros((128, NPARTOT), np.float32)

    def fm(v, nch):
        return np.ascontiguousarray(v.reshape(nch, 128).T)

    def put(l, name, arr):
        o, w = _PC[name]
        assert arr.shape == (128, w), (name, arr.shape, w)
        out[:, l * NPAR + o: l * NPAR + o + w] = arr

    for l in range(NL):
        put(l, "g1", fm(inp["norm_mix_g"][l], 8))
        put(l, "g2", fm(inp["norm_ffn_g"][l], 8))
        put(l, "gssm", fm(inp["ssm_norm_g"][l], 8))
        put(l, "dsk", fm(np.repeat(inp["d_skip"][l], 64), 8))
        def cw(w, nch):
            K = w.shape[0]
            return np.ascontiguousarray(w.reshape(K, nch, 128).transpose(2, 1, 0).reshape(128, nch * K))
        put(l, "caw", cw(inp["conv_a_w"][l], 4))
        put(l, "cbw", cw(inp["conv_b_w"][l], 12))
        put(l, "cbb", fm(inp["conv_b_bias"][l], 12))
        put(l, "ccw", cw(inp["conv_c_w"][l], 4))
        put(l, "ccb", fm(inp["conv_c_bias"][l], 4))
        put(l, "lng", fm(inp["ln_c_g"][l], 4))
        put(l, "lnb", fm(inp["ln_c_b"][l], 4))
        put(l, "cfw", cw(inp["conv_ffn_w"][l], 22))
        put(l, "dtb", np.broadcast_to(inp["dt_bias"][l][None, :], (128, 16)))
        put(l, "alog", np.broadcast_to(inp["a_log"][l][None, :], (128, 16)))
    out[:, NL * NPAR: NL * NPAR + 8] = fm(inp["final_norm_g"], 8)
    return out


_NC_CACHE = {}


def kernel(**inp):
    inp = {k: np.asarray(v) for k, v in inp.items()}
    xp = inp["x_prompt"]
    seq = xp.shape[1]
    nt = seq // TP
    key = (nt,)
    if key not in _NC_CACHE:
        _NC_CACHE[key] = build_program(nt, True)
    nc = _NC_CACHE[key]
    par = _pack_params(inp)
    win = np.ascontiguousarray(inp["w_in"].reshape(NL * D, DIN))
    wout = np.ascontiguousarray(inp["w_out"].reshape(NL * 2048, D))
    wup = np.ascontiguousarray(inp["w_up"].reshape(NL * D, 2 * DFF))
    wdn = np.ascontiguousarray(inp["w_down"].reshape(NL * DFF, D))
    in_maps = []
    for c in range(NCORES):
        b0 = c * 4
        in_maps.append({
            "xp": np.ascontiguousarray(xp[c]),
            "xs": np.ascontiguousarray(inp["x_sample"][b0:b0 + 4].reshape(128, D)),
            "sta": np.ascontiguousarray(inp["state_conv_a"][:, b0:b0 + 4].reshape(NL * 8, 512)),
            "sts": np.ascontiguousarray(inp["state_ssm"][:, b0:b0 + 4].reshape(NL * 4 * 1024, 128)),
            "stb": np.ascontiguousarray(inp["state_conv_b"][:, b0:b0 + 4].reshape(NL * 12, 1536)),
            "stc": np.ascontiguousarray(inp["state_conv_c"][:, b0:b0 + 4].reshape(NL * 120, 512)),
            "stf": np.ascontiguousarray(inp["state_conv_ffn"][:, b0:b0 + 4].reshape(NL * 8, DFF)),
            "win": win, "wout": wout, "wup": wup, "wdn": wdn, "par": par,
        })
    res = run_bass_kernel_spmd(nc, in_maps, core_ids=list(range(NCORES)))
    R = res.results
    y_prompt = np.stack([R[c]["yp"] for c in range(NCORES)], 0)
    y_sample = np.concatenate([R[c]["ys"].reshape(4, 32, D) for c in range(NCORES)], 0)

    def pst(name, shp):
        return np.stack([R[c][name].reshape((NL,) + shp) for c in range(NCORES)], 1)

    def sst(name, shp):
        return np.concatenate([R[c][name].reshape((NL, 4) + shp) for c in range(NCORES)], 1)

    return (y_prompt.astype(np.float32), y_sample.astype(np.float32),
            pst("pa", (2, 512)), pst("pss", (16, 64, 128)), pst("pb", (3, 1536)), pst("pc", (30, 512)),
            pst("pf", (2, DFF)),
            sst("sa", (2, 512)), sst("sss", (16, 64, 128)), sst("sb", (3, 1536)), sst("sc", (30, 512)),
            sst("sf", (2, DFF)))
```

```python
import numpy as np
from contextlib import ExitStack
import concourse.bass as bass
import concourse.mybir as mybir
from concourse.bass_utils import run_bass_kernel_spmd

F32 = mybir.dt.float32
BF16 = mybir.dt.bfloat16
AF = mybir.ActivationFunctionType
ALU = mybir.AluOpType

NCORES = 8
D = 1024
DIN = 5136
DFF = 2816
NL = 2
TP = 512
EPS = 1e-5
NSLOT = 4
SLOTW = 2048

_PC = {}
_off = 0
for _n, _w in (("g1", 8), ("g2", 8), ("gssm", 8), ("dsk", 8), ("caw", 12), ("cbw", 48), ("cbb", 12),
               ("ccw", 124), ("ccb", 4), ("lng", 4), ("lnb", 4), ("cfw", 66), ("dtb", 16), ("alog", 16)):
    _PC[_n] = (_off, _w)
    _off += _w
NPAR = _off
NPARTOT = NL * NPAR + 8


class Sched:
    ENGS = ("pe", "act", "dve", "pool", "sp")

    def __init__(self, nc, es):
        self.nc = nc
        self.es = es
        self.sem = {e: es.enter_context(nc.semaphore("s_" + e)) for e in self.ENGS}
        self.cnt = {e: 0 for e in self.ENGS}
        self.prog = {e: [] for e in self.ENGS}
        self.waited = {e: {} for e in self.ENGS}
        self.last_w = {}
        self.readers = {}
        self.dsem = {}

    def _handle(self, key):
        return self.sem[key] if key in self.sem else self.dsem[key][0]

    def _deps(self, eng, reads, writes, extra):
        deps = {}

        def add(tok, kind):
            if tok is None:
                return
            k, v = tok
            if k == eng and kind == "war":
                return
            if v > deps.get(k, 0):
                deps[k] = v

        for r in reads:
            add(self.last_w.get(r), "raw")
        for w in writes:
            add(self.last_w.get(w), "waw")
            for t in self.readers.get(w, ()):
                add(t, "war")
        for t in extra:
            add(t, "raw")
        out = []
        for k, v in deps.items():
            if v > self.waited[eng].get(k, 0):
                self.waited[eng][k] = v
                out.append((k, v))
        return out

    def _record(self, tok, reads, writes):
        for r in reads:
            lst = self.readers.setdefault(r, [])
            lst[:] = [t for t in lst if t[0] != tok[0]]
            lst.append(tok)
        for w in writes:
            self.last_w[w] = tok
            self.readers[w] = []

    MPREF = ("AV", "UA", "CA", "TG", "UC", "CC", "PTMP", "MEAN", "VAR", "RSTC", "TN", "DTP", "DTE", "DTT", "DAA",
             "E32", "CDE", "NAC", "SCXW", "ACS", "XW", "BTOK", "LT", "MT", "EB", "T1", "ZT", "FA", "UF", "CF", "GS", "YF")

    def op(self, eng, fn, reads=(), writes=(), extra=()):
        reads = list(reads)
        writes = list(writes)
        if "M" not in writes and "M" not in reads:
            if any(k.startswith(p) for k in reads + writes for p in self.MPREF):
                reads.append("M")
        for b in sorted({k[2] for k in reads + writes if k.startswith("ps")}):
            writes.append("bk" + b)
        waits = self._deps(eng, reads, writes, extra)
        self.cnt[eng] += 1
        tok = (eng, self.cnt[eng])
        sem = self.sem[eng]
        hw = [(self._handle(k), v) for k, v in waits]

        def thunk(e):
            for h, v in hw:
                e.wait_ge(h, v)
            fn(e).then_inc(sem, 1)

        self.prog[eng].append(thunk)
        self._record(tok, reads, writes)
        if not hasattr(self, "log"):
            self.log = []
        self.log.append((tok, waits, list(reads), list(writes)))
        return tok

    def dma(self, eng, semname, fn, reads=(), writes=(), extra=()):
        if semname not in self.dsem:
            self.dsem[semname] = [self.es.enter_context(self.nc.semaphore("d_" + semname)), 0]
        ent = self.dsem[semname]
        prev = (semname, ent[1] * 16) if ent[1] > 0 else None
        ex = list(extra) + ([prev] if prev else [])
        waits = self._deps(eng, reads, writes, ex)
        ent[1] += 1
        tok = (semname, ent[1] * 16)
        hw = [(self._handle(k), v) for k, v in waits]
        h = ent[0]

        def thunk(e):
            for hh, v in hw:
                e.wait_ge(hh, v)
            fn(e).then_inc(h, 16)

        self.prog[eng].append(thunk)
        self._record(tok, reads, writes)
        return tok

    def final_wait(self, eng):
        hw = [(v[0], v[1] * 16) for v in self.dsem.values()]

        def thunk(e):
            for hh, v in hw:
                e.wait_ge(hh, v)

        self.prog[eng].append(thunk)

    def emit(self, block):
        prog = self.prog

        @block.tensor
        def _(e):
            for t in prog["pe"]:
                t(e)

        @block.scalar
        def _(e):
            for t in prog["act"]:
                t(e)

        @block.vector
        def _(e):
            for t in prog["dve"]:
                t(e)

        @block.gpsimd
        def _(e):
            for t in prog["pool"]:
                t(e)

        @block.sync
        def _(e):
            for t in prog["sp"]:
                t(e)


class Arena:
    def __init__(self, ap, total):
        self.ap = ap
        self.total = total
        self.off = 0

    def f32(self, n):
        a = self.ap[:, self.off:self.off + n]
        self.off += n
        assert self.off <= self.total, ("arena overflow", self.off, self.total)
        return a

    def bf16(self, n):
        assert n % 2 == 0
        return self.f32(n // 2).bitcast(BF16)


def v3(ap, a):
    return ap.rearrange("p (a b) -> p a b", a=a)


DBG = {"layers": NL, "phase": 99, "casts": True, "ssdcut": 99}


def build_program(n_ptiles=8, with_sample=True):
    nc = bass.Bass("TRN2", target_bir_lowering=False)
    NT = n_ptiles
    SEQ = NT * TP

    def din(name, shape, dt=F32):
        return nc.dram_tensor(name, shape, dt, kind="ExternalInput").ap()

    def dout(name, shape):
        return nc.dram_tensor(name, shape, F32, kind="ExternalOutput").ap()

    def dint(name, shape, dt):
        return nc.dram_tensor(name, shape, dt, kind="Internal").ap()

    xp = din("xp", [SEQ, D])
    xs = din("xs", [128, D])
    sta = din("sta", [NL * 4 * 2, 512])
    sts = din("sts", [NL * 4 * 1024, 128])
    stb = din("stb", [NL * 4 * 3, 1536])
    stc = din("stc", [NL * 4 * 30, 512])
    stf = din("stf", [NL * 4 * 2, DFF])
    win = din("win", [NL * D, DIN])
    wout = din("wout", [NL * 2048, D])
    wup = din("wup", [NL * D, 2 * DFF])
    wdn = din("wdn", [NL * DFF, D])
    par = din("par", [128, NPARTOT])

    yp = dout("yp", [SEQ, D])
    ys = dout("ys", [128, D])
    o_pa = dout("pa", [NL * 2, 512])
    o_pss = dout("pss", [NL * 1024, 128])
    o_pb = dout("pb", [NL * 3, 1536])
    o_pc = dout("pc", [NL * 30, 512])
    o_pf = dout("pf", [NL * 2, DFF])
    o_sa = dout("sa", [NL * 4 * 2, 512])
    o_sss = dout("sss", [NL * 4 * 1024, 128])
    o_sb = dout("sb", [NL * 4 * 3, 1536])
    o_sc = dout("sc", [NL * 4 * 30, 512])
    o_sf = dout("sf", [NL * 4 * 2, DFF])

    s_win = dint("s_win", [NL * 10, 128, 8 * 512], BF16)
    s_wdt = dint("s_wdt", [NL, 128, 8 * 16], BF16)
    s_wout = dint("s_wout", [NL * 4, 128, 16 * 256], BF16)
    s_wup = dint("s_wup", [NL * 11, 128, 8 * 512], BF16)
    s_wdn = dint("s_wdn", [NL * 8, 128, 22 * 128], BF16)

    es = ExitStack()
    with es:
        S = Sched(nc, es)
        ARENA_WORDS = 53200
        arena_t = es.enter_context(nc.sbuf_tensor("arena", [128, ARENA_WORDS], F32))
        psum = es.enter_context(nc.psum_tensor("psum", [128, 4096], F32))
        A = Arena(arena_t, ARENA_WORDS)

        def bank(b):
            return psum[:, b * 512:(b + 1) * 512]

        XT = A.f32(8 * TP)
        XT3 = v3(XT, 8)
        HB = A.bf16(8 * TP)
        HB3 = v3(HB, 8)
        WSLOT = [A.f32(SLOTW).bitcast(BF16) for _ in range(NSLOT)]
        XIN = [A.f32(1024) for _ in range(2)]
        PAR = A.f32(NPARTOT)
        IDENT = A.f32(128)
        IDENTB = A.bf16(128)
        TRI_I = A.f32(128)
        TRI_G = A.f32(128)
        ONES = A.f32(128)
        ONESN = A.f32(128)
        ONES5 = A.f32(128)
        NEGM128 = A.f32(512)
        NEGM32 = A.f32(128)
        SEL2 = A.f32(1024)
        NEGHALF = A.f32(TP)
        ABC = A.f32(NL * 16)
        CCWH = A.f32(NL * 124)
        WDT = [A.bf16(128) for _ in range(NL)]
        SP_ = [A.f32(1024) for _ in range(NL)]
        SSMP = A.f32(1024)
        SBFP = [A.bf16(1024) for _ in range(NL)]
        SBFS = A.bf16(1024)
        XDTP = A.bf16(16 * 128)
        HIST_W = {"a": (4, 2), "b": (12, 3), "c": (4, 30), "f": (22, 2)}
        HISTP = {k: [A.f32(nch * H) for _ in range(NL)] for k, (nch, H) in HIST_W.items()}
        HISTS = {k: A.f32(nch * 4 * H) for k, (nch, H) in HIST_W.items()}
        YCAT = A.bf16(16 * TP)
        YC3 = v3(YCAT, 16)
        SQ = [A.f32(TP) for _ in range(2)]
        STAT = A.f32(TP)
        RSTD = A.f32(TP)
        XS = A.f32(8 * TP)
        XS3 = v3(XS, 8)
        BT = A.bf16(2 * TP)
        BT3 = v3(BT, 2)
        CT = A.bf16(2 * TP)
        CT3 = v3(CT, 2)
        UBt = [A.f32(TP + 4 * 3) for _ in range(3)]
        CBo = [A.f32(TP) for _ in range(2)]
        STG = XIN[0][:, 0:512]
        STG2 = XIN[1]
        m_base = A.off
        print("m_base", m_base)
        AVt = [A.f32(TP) for _ in range(2)]
        UAt = [A.f32(TP + 4 * 2) for _ in range(2)]
        CAt = [A.f32(TP) for _ in range(2)]
        TG = [A.f32(TP) for _ in range(2)]
        UCt = [A.f32(TP + 4 * 30) for _ in range(2)]
        CC = A.f32(4 * TP)
        CC3 = v3(CC, 4)
        PTMP = A.f32(TP)
        MEAN = A.f32(TP)
        VAR = A.f32(TP)
        RSTC = A.f32(TP)
        TN = [A.f32(TP) for _ in range(2)]
        m_end_ac = A.off
        A.off = m_base
        DTP = A.f32(64)
        DTE = A.f32(64)
        DTT = A.f32(64)
        DAA = A.f32(64)
        E32 = A.f32(32)
        CDE = A.f32(16)
        CDEb = A.f32(16)
        NAC = A.f32(16)
        SCXW = A.f32(16)
        ACS = A.f32(128)
        XW = A.bf16(1024)
        BTOK = A.bf16(256)
        LT = [A.f32(4 * 128) for _ in range(2)]
        MT = A.bf16(16 * 128)
        EB = A.f32(1024)
        T1 = A.f32(1024)
        ZT = [A.f32(TP) for _ in range(2)]
        m_end_ssd = A.off
        A.off = m_base
        FACT = A.bf16(22 * TP)
        FA3 = v3(FACT, 22)
        UFt = [A.f32(TP + 4 * 2) for _ in range(3)]
        CF_base = A.off
        CFt = [A.f32(TP) for _ in range(2)]
        GS = [A.f32(TP) for _ in range(2)]
        YFt = [arena_t[:, CF_base:CF_base + 1024], arena_t[:, CF_base + 1024:CF_base + 2048]]
        m_end_ffn = A.off
        A.off = max(m_end_ac, m_end_ssd, m_end_ffn)
        print("arena words used", A.off, "of", ARENA_WORDS, "(AC %d SSD %d FFN %d)" % (
            m_end_ac - m_base, m_end_ssd - m_base, m_end_ffn - m_base))

        def pcol(l, name, j=0, w=1):
            o, _ = _PC[name]
            return PAR[:, l * NPAR + o + j: l * NPAR + o + j + w]

        S.dma("sp", "par", lambda e: e.dma_start(out=PAR, in_=par), writes=["PAR"])

        def consts(e):
            e.memset(arena_t[:, m_base:A.off], 0.0)
            e.memset(TRI_I, 1.0)
            e.affine_select(out=TRI_I, in_=TRI_I, compare_op=ALU.is_ge, fill=0.0, base=0,
                            pattern=[[1, 128]], channel_multiplier=-1)
            e.memset(TRI_G, 1.0)
            e.affine_select(out=TRI_G, in_=TRI_G, compare_op=ALU.is_gt, fill=0.0, base=0,
                            pattern=[[-1, 128]], channel_multiplier=1)
            e.memset(IDENT, 0.0)
            e.affine_select(out=IDENT, in_=IDENT, compare_op=ALU.not_equal, fill=1.0, base=0,
                            pattern=[[-1, 128]], channel_multiplier=1)
            e.memset(ONES, 1.0)
            e.memset(ONESN, 1.0 / 1024.0)
            e.memset(ONES5, 1.0 / 512.0)
            e.memset(NEGHALF, -0.5)
            e.memset(NEGM128, 0.0)
            for i in range(4):
                e.affine_select(out=NEGM128[:, i * 128:(i + 1) * 128], in_=NEGM128[:, i * 128:(i + 1) * 128],
                                compare_op=ALU.is_ge, fill=-30000.0, base=0, pattern=[[1, 128]],
                                channel_multiplier=-1)
            e.memset(NEGM32, 0.0)
            for i in range(4):
                e.affine_select(out=NEGM32[:, i * 32:(i + 1) * 32], in_=NEGM32[:, i * 32:(i + 1) * 32],
                                compare_op=ALU.is_ge, fill=-30000.0, base=0, pattern=[[1, 32]],
                                channel_multiplier=-1)
            e.memset(SEL2, 1.0)
            s4 = SEL2.rearrange("p (c t m) -> p c t m", c=8, t=2)
            e.affine_select(out=s4, in_=s4, compare_op=ALU.is_equal, fill=0.0, base=0,
                            pattern=[[2, 8], [1, 2], [0, 64]], channel_multiplier=-1)
            for k in HISTP:
                for l in range(NL):
                    e.memset(HISTP[k][l], 0.0)
            for l in range(NL):
                e.memset(SP_[l], 0.0)
            e.memset(XDTP, 0.0)
            for l in range(NL):
                e.memset(SBFP[l], 0.0)
            return e.memset(SBFS, 0.0)

        S.op("pool", consts, writes=["CONST", "M", "XDTP", "SBFS_0", "SBFS_1"] + ["HP%s%d" % (k, l) for k in "abcf" for l in range(NL)]
             + ["SP%d_%d" % (l, g) for l in range(NL) for g in range(2)] + ["SBFP%d_%d" % (l, g) for l in range(NL) for g in range(2)])
        S.op("dve", lambda e: e.tensor_copy(out=IDENTB, in_=IDENT), reads=["CONST"], writes=["IDENTB"])

        def abc_fn(e):
            for l in range(NL):
                e.activation(out=ABC[:, l * 16:(l + 1) * 16], in_=pcol(l, "alog", 0, 16), func=AF.Exp)
            return e.mul(ABC, ABC, -1.0)

        S.op("act", abc_fn, reads=["PAR"], writes=["ABC"])

        def ccwh_fn(e):
            i = None
            for l in range(NL):
                i = e.tensor_scalar(out=CCWH[:, l * 124:(l + 1) * 124], in0=pcol(l, "ccw", 0, 124),
                                    scalar1=0.5, scalar2=None, op0=ALU.mult)
            return i

        S.op("dve", ccwh_fn, reads=["PAR"], writes=["CCWH"])

        cast_i = [0]

        scr_keys = {}

        import os as _os
        _sel = _os.environ.get("CASTSEL", "win,wdt,wout,wup,wdn").split(",")

        def cast(dst, src, key):
            if not any(key.startswith("scr_" + q) for q in _sel):
                scr_keys.setdefault(key, [])
                return
            n = cast_i[0] % 6
            cast_i[0] += 1
            sub = key + "#%d" % len(scr_keys.setdefault(key, []))
            scr_keys[key].append(sub)
            S.dma("pool", "cast%d" % n, lambda e: e.dma_start(out=dst, in_=src), writes=[sub])

        def win_src(l, c0, w):
            return win[l * D:(l + 1) * D, c0:c0 + w].rearrange("(kc p) e -> p kc e", p=128)

        ordA = []
        for c in range(4):
            ordA += [0 + c * 128, 1024 + c * 128, 512 + c * 128]
        ordC = []
        for c in range(4):
            ordC += [4112 + 512 + c * 128, 4112 + c * 128]
        ordB = [2560 + j * 128 for j in range(12)]
        ordZ = [1536 + j * 128 for j in range(8)]
        win_order = ordA + ordC + ordB + ordZ

        def emit_casts(l):
            for si in range(10):
                cols = win_order[si * 4:(si + 1) * 4]
                dst = s_win[l * 10 + si].rearrange("p (kc e) -> p kc e", kc=8)
                if cols[3] - cols[0] == 384 and cols[1] - cols[0] == 128:
                    cast(dst, win_src(l, cols[0], 512), "scr_win%d_%d" % (l, si))
                else:
                    for j, c0 in enumerate(cols):
                        cast(dst[:, :, j * 128:(j + 1) * 128], win_src(l, c0, 128), "scr_win%d_%d" % (l, si))
            cast(s_wdt[l].rearrange("p (kc e) -> p kc e", kc=8), win_src(l, 4096, 16), "scr_wdt%d" % l)
            for si in range(4):
                cast(s_wout[l * 4 + si].rearrange("p (kc e) -> p kc e", kc=16),
                     wout[l * 2048:(l + 1) * 2048, si * 256:(si + 1) * 256].rearrange("(kc p) e -> p kc e", p=128),
                     "scr_wout%d_%d" % (l, si))
            for si in range(11):
                dst = s_wup[l * 11 + si].rearrange("p (kc j t e) -> p kc j t e", kc=8, j=2, t=2)
                for t, base in ((0, DFF), (1, 0)):
                    for j in range(2):
                        c0 = base + si * 256 + j * 128
                        src = wup[l * D:(l + 1) * D, c0:c0 + 128].rearrange("(kc p) e -> p kc e", p=128)
                        cast(dst[:, :, j, t, :], src, "scr_wup%d_%d" % (l, si))
            for si in range(8):
                cast(s_wdn[l * 8 + si].rearrange("p (kc e) -> p kc e", kc=22),
                     wdn[l * DFF:(l + 1) * DFF, si * 128:(si + 1) * 128].rearrange("(kc p) e -> p kc e", p=128),
                     "scr_wdn%d_%d" % (l, si))

        for l in range(NL if DBG["casts"] else 0):
            emit_casts(l)
            S.dma("pool", "wdt", lambda e, l=l: e.dma_start(out=WDT[l], in_=s_wdt[l]),
                  reads=scr_keys["scr_wdt%d" % l], writes=["WDT%d" % l])

        def layer_slabs(l):
            seq = []
            for si in range(10):
                seq.append(("win", l, si))
            for si in range(4):
                seq.append(("wout", l, si))
            for si in range(11):
                seq.append(("wup", l, si))
            for si in range(8):
                seq.append(("wdn", l, si))
            return seq

        ntiles_total = NT + (1 if with_sample else 0)
        wseq = []
        for _t in range(ntiles_total):
            for l in range(NL):
                wseq += layer_slabs(l)
        wstate = {"next": 0, "use": 0}

        def slab_src(kind, l, si):
            if kind == "win":
                return s_win[l * 10 + si], 4096, "scr_win%d_%d" % (l, si)
            if kind == "wout":
                return s_wout[l * 4 + si], 4096, "scr_wout%d_%d" % (l, si)
            if kind == "wup":
                return s_wup[l * 11 + si], 4096, "scr_wup%d_%d" % (l, si)
            return s_wdn[l * 8 + si], 22 * 128, "scr_wdn%d_%d" % (l, si)

        def issue_load(i):
            kind, l, si = wseq[i]
            src, n, key = slab_src(kind, l, si)
            slot = i % NSLOT
            S.dma("sp", "w%d" % slot, lambda e: e.dma_start(out=WSLOT[slot][:, 0:n], in_=src),
                  reads=scr_keys[key], writes=["W%d" % slot])

        def get_slab(kind, l, si):
            i = wstate["use"]
            while wseq[i] != (kind, l, si):
                assert DBG["phase"] < 99 or DBG["layers"] < NL
                i += 1
            wstate["use"] = i + 1
            while wstate["next"] < min(len(wseq), i + NSLOT):
                issue_load(wstate["next"])
                wstate["next"] += 1
            slot = i % NSLOT
            return WSLOT[slot], "W%d" % slot

        mmrr = [0]

        def next_bank():
            b = mmrr[0] % 4
            mmrr[0] += 1
            return b

        FENCE = A.f32(2)
        wcur = {}

        def wchunk(l, gi):
            si, jj = divmod(gi, 4)
            if jj == 0:
                wcur["s"] = get_slab("win", l, si)
            return wcur["s"][0], wcur["s"][1], jj

        def stat_chunk(T, c, src3, keys):
            sq = SQ[c % 2]
            S.op("act", lambda e: e.activation(out=sq[:, 0:T], in_=src3[:, c, 0:T], func=AF.Square),
                 reads=keys, writes=["SQ%d" % (c % 2)])
            S.op("pe", lambda e: e.matmul(bank(4)[:, 0:T], lhsT=ONESN, rhs=sq[:, 0:T], start=(c == 0), stop=(c == 7)),
                 reads=["SQ%d" % (c % 2), "CONST"], writes=["ps4"])

        def rmsnorm(T, gcol, out_fn, xkeys_r, outkeys, eps, src3=None, stats_done=False):
            src3 = XT3 if src3 is None else src3
            for c in range(0 if stats_done else 8):
                stat_chunk(T, c, src3, xkeys_r(c))
            S.op("dve", lambda e: e.tensor_scalar(out=STAT[:, 0:T], in0=bank(4)[:, 0:T], scalar1=eps, scalar2=None,
                                                  op0=ALU.add), reads=["ps4"], writes=["STAT"])
            def rs_fn(e):
                e.activation(out=RSTD[:, 0:T], in_=STAT[:, 0:T], func=AF.Ln)
                return e.activation(out=RSTD[:, 0:T], in_=RSTD[:, 0:T], func=AF.Exp, scale=-0.5)

            S.op("act", rs_fn, reads=["STAT"], writes=["RSTD"])
            for c in range(8):
                S.op("dve", lambda e, c=c: e.scalar_tensor_tensor(out=out_fn(c), in0=src3[:, c, 0:T], scalar=gcol(c),
                                                                   in1=RSTD[:, 0:T], op0=ALU.mult, op1=ALU.mult),
                     reads=xkeys_r(c) + ["RSTD", "PAR"], writes=[outkeys(c)])

        def proj(T, slab, slabkey, j, width=512, nk=8, rhs3=None, rkeys=None):
            rhs3 = HB3 if rhs3 is None else rhs3
            rkeys = ["HB"] if rkeys is None else rkeys
            b = next_bank()
            sl3 = slab[:, 0:nk * width].rearrange("p (k e) -> p k e", k=nk)

            def fn(e):
                i = None
                for k in range(nk):
                    i = e.matmul(bank(b)[:, 0:T], lhsT=sl3[:, k, j * 128:(j + 1) * 128], rhs=rhs3[:, k, 0:T],
                                 start=(k == 0), stop=(k == nk - 1))
                return i

            S.op("pe", fn, reads=[slabkey] + rkeys, writes=["ps%d" % b])
            return bank(b)[:, 0:T], "ps%d" % b

        def hist_in(eng, hist, c, nseg, H, U3, ukey, hkey):
            S.op(eng, lambda e: e.tensor_copy(out=U3[:, :, 0:H],
                                              in_=hist[:, c * nseg * H:(c + 1) * nseg * H].rearrange("p (s h) -> p s h", s=nseg)),
                 reads=[hkey], writes=[ukey + "h"])

        def hist_out(eng, hist, c, nseg, H, L, U3, ukey, hkey):
            S.op(eng, lambda e: e.tensor_copy(out=hist[:, c * nseg * H:(c + 1) * nseg * H].rearrange("p (s h) -> p s h", s=nseg),
                                              in_=U3[:, :, L:L + H]),
                 reads=[ukey, ukey + "h"], writes=[hkey])

        def state_out(hist, nch, R, dst2d, scale, hkey):
            for c0 in range(0, nch, 4):
                n = min(4, nch - c0)
                b = next_bank()

                def fn(e, c0=c0, n=n, b=b):
                    i = None
                    for j in range(n):
                        i = e.transpose(out=bank(b)[0:R, j * 128:(j + 1) * 128],
                                        in_=hist[:, (c0 + j) * R:(c0 + j + 1) * R], identity=IDENT)
                    return i

                S.op("pe", fn, reads=[hkey, "CONST"], writes=["ps%d" % b])
                S.op("act", lambda e, n=n, b=b: e.activation(out=STG[0:R, 0:n * 128], in_=bank(b)[0:R, 0:n * 128],
                                                             func=AF.Copy, scale=scale),
                     reads=["ps%d" % b], writes=["XIN0"])
                S.dma("act", "stout", lambda e, c0=c0, n=n: e.dma_start(out=dst2d[:, c0 * 128:(c0 + n) * 128],
                                                                         in_=STG[0:R, 0:n * 128]),
                      reads=["XIN0"])

        def state_in(hist, nch, R, src2d, scale, hkey):
            for c0 in range(0, nch, 4):
                n = min(4, nch - c0)
                S.dma("sp", "stin", lambda e, c0=c0, n=n: e.dma_start(out=STG[0:R, 0:n * 128],
                                                                       in_=src2d[:, c0 * 128:(c0 + n) * 128]),
                      writes=["XIN0"])
                b = next_bank()

                def fn(e, n=n, b=b):
                    i = None
                    for j in range(n):
                        i = e.transpose(out=bank(b)[:, j * R:(j + 1) * R], in_=STG[0:R, j * 128:(j + 1) * 128],
                                        identity=IDENT[0:R, 0:R])
                    return i

                S.op("pe", fn, reads=["XIN0", "CONST"], writes=["ps%d" % b])
                S.op("act", lambda e, c0=c0, n=n, b=b: e.activation(out=hist[:, c0 * R:(c0 + n) * R],
                                                                     in_=bank(b)[:, 0:n * R], func=AF.Copy, scale=scale),
                     reads=["ps%d" % b], writes=[hkey])

        def ssm_in(Sbuf, skey, src2d, SBF, bk):
            S.dma("sp", "ssin", lambda e: e.dma_start(out=v3(STG2, 8), in_=src2d.rearrange("(c q) n -> q c n", q=128)),
                  writes=["XIN1"])
            for hf in range(2):
                b = next_bank()

                def fn(e, hf=hf, b=b):
                    i = None
                    for j in range(4):
                        c = hf * 4 + j
                        i = e.transpose(out=bank(b)[:, j * 128:(j + 1) * 128], in_=STG2[:, c * 128:(c + 1) * 128],
                                        identity=IDENT)
                    return i

                S.op("pe", fn, reads=["XIN1", "CONST"], writes=["ps%d" % b])
                S.op("act", lambda e, hf=hf, b=b: e.activation(out=Sbuf[:, hf * 512:(hf + 1) * 512], in_=bank(b),
                                                               func=AF.Copy), reads=["ps%d" % b], writes=[skey + "_%d" % hf])
                S.op("dve", lambda e, hf=hf, b=b: e.tensor_copy(out=SBF[:, hf * 512:(hf + 1) * 512], in_=bank(b)),
                     reads=["ps%d" % b], writes=[bk + "_%d" % hf])

        def ssm_out(Sbuf, skey, dst2d):
            for hf in range(2):
                b = next_bank()

                def fn(e, hf=hf, b=b):
                    i = None
                    for j in range(4):
                        c = hf * 4 + j
                        i = e.transpose(out=bank(b)[:, j * 128:(j + 1) * 128], in_=Sbuf[:, c * 128:(c + 1) * 128],
                                        identity=IDENT)
                    return i

                S.op("pe", fn, reads=[skey + "_%d" % hf, "CONST"], writes=["ps%d" % b])
                S.op("act", lambda e, hf=hf, b=b: e.activation(out=STG2[:, hf * 512:(hf + 1) * 512], in_=bank(b),
                                                               func=AF.Copy), reads=["ps%d" % b], writes=["XIN1"])
                S.dma("act", "ssout", lambda e, hf=hf: e.dma_start(
                    out=dst2d.rearrange("(c q) n -> q c n", q=128)[:, hf * 4:(hf + 1) * 4, :],
                    in_=STG2[:, hf * 512:(hf + 1) * 512].rearrange("q (c n) -> q c n", c=4)), reads=["XIN1"])

        def layer(l, T, nseg, sample, last):
            L = T // nseg
            Q = min(128, L)
            nck = T // Q
            hist = {k: (HISTS[k] if sample else HISTP[k][l]) for k in HIST_W}
            hkey = {k: ("HS" + k if sample else "HP%s%d" % (k, l)) for k in HIST_W}

            def seg3(ap):
                return ap.rearrange("p (s t) -> p s t", s=nseg)

            if sample:
                state_in(hist["a"], 4, 8, sta[l * 8:(l + 1) * 8, :], 1.0, hkey["a"])
                state_in(hist["b"], 12, 12, stb[l * 12:(l + 1) * 12, :], 1.0, hkey["b"])
                state_in(hist["c"], 4, 120, stc[l * 120:(l + 1) * 120, :], 2.0, hkey["c"])
                state_in(hist["f"], 22, 8, stf[l * 8:(l + 1) * 8, :], 1.0, hkey["f"])

            S.op("dve", lambda e: e.memset(FENCE, 0.0), reads=["M"], writes=["M"])
            rmsnorm(T, lambda c: pcol(l, "g1", c), lambda c: HB3[:, c, 0:T], lambda c: ["XT%d" % c],
                    lambda c: "HB", EPS)

            if DBG['phase'] < 2:
                return
            for c in range(4):
                slab, skey, jj = wchunk(l, c * 3)
                pv, pvk = proj(T, slab, skey, jj)
                av = AVt[c % 2]
                S.op("act", lambda e, av=av, pv=pv: e.activation(out=av[:, 0:T], in_=pv, func=AF.Copy),
                     reads=[pvk, "M"], writes=["AV%d" % (c % 2)])
                slab, skey, jj = wchunk(l, c * 3 + 1)
                pc_, pck = proj(T, slab, skey, jj)
                ua = UAt[c % 2]
                U3 = ua[:, 0:nseg * (L + 2)].rearrange("p (s t) -> p s t", s=nseg)
                uk = "UA%d" % (c % 2)
                hist_in("dve", hist["a"], c, nseg, 2, U3, uk, hkey["a"])
                S.op("dve", lambda e, U3=U3, pc_=pc_, av=av: e.tensor_tensor(out=U3[:, :, 2:2 + L], in0=seg3(pc_),
                                                                              in1=seg3(av[:, 0:T]), op=ALU.mult),
                     reads=[pck, "AV%d" % (c % 2), "M"], writes=[uk])
                hist_out("dve", hist["a"], c, nseg, 2, L, U3, uk, hkey["a"])
                ca = CAt[c % 2]

                def convA(e, U3=U3, ca=ca, c=c):
                    o = seg3(ca[:, 0:T])
                    e.tensor_scalar(out=o, in0=U3[:, :, 0:L], scalar1=pcol(l, "caw", c * 3 + 0), scalar2=None, op0=ALU.mult)
                    e.scalar_tensor_tensor(out=o, in0=U3[:, :, 1:1 + L], scalar=pcol(l, "caw", c * 3 + 1), in1=o,
                                           op0=ALU.mult, op1=ALU.add)
                    return e.scalar_tensor_tensor(out=o, in0=U3[:, :, 2:2 + L], scalar=pcol(l, "caw", c * 3 + 2), in1=o,
                                                  op0=ALU.mult, op1=ALU.add)

                S.op("dve", convA, reads=[uk, uk + "h", "PAR", "M"], writes=["CA%d" % (c % 2)])
                slab, skey, jj = wchunk(l, c * 3 + 2)
                pb_, pbk = proj(T, slab, skey, jj)
                S.op("dve", lambda e, pb_=pb_, ca=ca, c=c: e.tensor_tensor(out=YC3[:, c, 0:T], in0=pb_, in1=ca[:, 0:T],
                                                                           op=ALU.mult),
                     reads=[pbk, "CA%d" % (c % 2)], writes=["YC%d" % c])
            if sample or last:
                R = nseg * 2
                dst = (o_sa if sample else o_pa)[l * R:(l + 1) * R, :]
                state_out(hist["a"], 4, R, dst, 1.0, hkey["a"])

            if DBG['phase'] < 3:
                return
            for c in range(4):
                slab, skey, jj = wchunk(l, 12 + c * 2)
                pg, pgk = proj(T, slab, skey, jj)
                tg = TG[c % 2]
                S.op("act", lambda e, tg=tg, pg=pg: e.activation(out=tg[:, 0:T], in_=pg, func=AF.Tanh, scale=0.5),
                     reads=[pgk, "M"], writes=["TG%d" % (c % 2)])
                slab, skey, jj = wchunk(l, 12 + c * 2 + 1)
                pa_, pak = proj(T, slab, skey, jj)
                uc = UCt[c % 2]
                U3 = uc[:, 0:nseg * (L + 30)].rearrange("p (s t) -> p s t", s=nseg)
                uk = "UC%d" % (c % 2)
                hist_in("dve", hist["c"], c, nseg, 30, U3, uk, hkey["c"])
                S.op("dve", lambda e, U3=U3, tg=tg, pa_=pa_: e.scalar_tensor_tensor(
                    out=U3[:, :, 30:30 + L], in0=seg3(tg[:, 0:T]), scalar=1.0, in1=seg3(pa_), op0=ALU.add, op1=ALU.mult),
                     reads=[pak, "TG%d" % (c % 2), "M"], writes=[uk])
                hist_out("dve", hist["c"], c, nseg, 30, L, U3, uk, hkey["c"])

                def convC(e, U3=U3, c=c):
                    o = seg3(CC3[:, c, 0:T])
                    tmp = seg3(PTMP[:, 0:T])
                    wh = CCWH[:, l * 124 + c * 31: l * 124 + (c + 1) * 31]
                    i = e.tensor_scalar(out=o, in0=U3[:, :, 0:L], scalar1=wh[:, 0:1], scalar2=pcol(l, "ccb", c),
                                        op0=ALU.mult, op1=ALU.add)
                    for k in range(1, 31):
                        i = e.scalar_tensor_tensor(out=o, in0=U3[:, :, k:k + L], scalar=wh[:, k:k + 1], in1=o,
                                                   op0=ALU.mult, op1=ALU.add)
                    return i

                S.op("dve", convC, reads=[uk, uk + "h", "CCWH", "PAR", "M"], writes=["CC%d" % c])
            if sample or last:
                R = nseg * 30
                dst = (o_sc if sample else o_pc)[l * R:(l + 1) * R, :]
                state_out(hist["c"], 4, R, dst, 0.5, hkey["c"])
            for c in range(4):
                sq = SQ[c % 2]
                S.op("act", lambda e, c=c, sq=sq: e.activation(out=sq[:, 0:T], in_=CC3[:, c, 0:T], func=AF.Square),
                     reads=["CC%d" % c], writes=["SQ%d" % (c % 2)])
                S.op("pe", lambda e, c=c: e.matmul(bank(4)[:, 0:T], lhsT=ONES5, rhs=CC3[:, c, 0:T], start=(c == 0),
                                                   stop=(c == 3)), reads=["CC%d" % c, "CONST"], writes=["ps4"])
                S.op("pe", lambda e, c=c, sq=sq: e.matmul(bank(5)[:, 0:T], lhsT=ONES5, rhs=sq[:, 0:T], start=(c == 0),
                                                          stop=(c == 3)), reads=["SQ%d" % (c % 2), "CONST"], writes=["ps5", "ps5c", "ps5b"])
            S.op("act", lambda e: e.activation(out=MEAN[:, 0:T], in_=bank(4)[:, 0:T], func=AF.Copy),
                 reads=["ps4", "M"], writes=["MEAN"])
            S.op("dve", lambda e: e.tensor_tensor(out=VAR[:, 0:T], in0=MEAN[:, 0:T], in1=MEAN[:, 0:T], op=ALU.mult),
                 reads=["MEAN", "M"], writes=["VAR"])
            S.op("dve", lambda e: e.scalar_tensor_tensor(out=VAR[:, 0:T], in0=bank(5)[:, 0:T], scalar=EPS, in1=VAR[:, 0:T],
                                                         op0=ALU.add, op1=ALU.subtract),
                 reads=["ps5", "VAR"], writes=["VAR"])
            def rsc_fn(e):
                e.activation(out=RSTC[:, 0:T], in_=VAR[:, 0:T], func=AF.Ln)
                return e.activation(out=RSTC[:, 0:T], in_=RSTC[:, 0:T], func=AF.Exp, scale=-0.5)

            S.op("act", rsc_fn, reads=["VAR", "M"], writes=["RSTC"])
            for c in range(4):
                tn = TN[c % 2]

                def lnn(e, c=c, tn=tn):
                    e.tensor_tensor(out=tn[:, 0:T], in0=CC3[:, c, 0:T], in1=MEAN[:, 0:T], op=ALU.subtract)
                    return e.tensor_tensor(out=tn[:, 0:T], in0=tn[:, 0:T], in1=RSTC[:, 0:T], op=ALU.mult)

                S.op("dve", lnn, reads=["CC%d" % c, "MEAN", "RSTC", "M"], writes=["TN%d" % (c % 2)])
                S.op("act", lambda e, c=c, tn=tn: e.activation(out=YC3[:, 12 + c, 0:T], in_=tn[:, 0:T], func=AF.Silu,
                                                               scale=pcol(l, "lng", c), bias=pcol(l, "lnb", c)),
                     reads=["TN%d" % (c % 2), "PAR"], writes=["YC%d" % (12 + c)])

            if DBG['phase'] < 4:
                return
            pend_b = []

            def flush_b():
                while pend_b:
                    dst_, co_, j_, dk_ = pend_b.pop(0)
                    S.op("act", lambda e, dst_=dst_, co_=co_: e.activation(out=dst_, in_=co_[:, 0:T], func=AF.Silu),
                         reads=["CBo%d" % (j_ % 2)], writes=dk_)

            for j in range(12):
                slab, skey, jj = wchunk(l, 20 + j)
                px, pxk = proj(T, slab, skey, jj)
                ub = UBt[j % 3]
                U3 = ub[:, 0:nseg * (L + 3)].rearrange("p (s t) -> p s t", s=nseg)
                uk = "UB%d" % (j % 3)
                hist_in("dve", hist["b"], j, nseg, 3, U3, uk, hkey["b"])
                S.op("act", lambda e, U3=U3, px=px: e.activation(out=U3[:, :, 3:3 + L], in_=seg3(px), func=AF.Copy),
                     reads=[pxk], writes=[uk])
                flush_b()
                hist_out("dve", hist["b"], j, nseg, 3, L, U3, uk, hkey["b"])
                co = CBo[j % 2]

                def convB(e, U3=U3, co=co, j=j):
                    o = seg3(co[:, 0:T])
                    e.tensor_scalar(out=o, in0=U3[:, :, 0:L], scalar1=pcol(l, "cbw", j * 4), scalar2=pcol(l, "cbb", j),
                                    op0=ALU.mult, op1=ALU.add)
                    i = None
                    for k in range(1, 4):
                        i = e.scalar_tensor_tensor(out=o, in0=U3[:, :, k:k + L], scalar=pcol(l, "cbw", j * 4 + k), in1=o,
                                                   op0=ALU.mult, op1=ALU.add)
                    return i

                S.op("dve", convB, reads=[uk, uk + "h", "PAR"], writes=["CBo%d" % (j % 2)])
                if j < 8:
                    dst, dk = XS3[:, j, 0:T], ["XS%d_%d" % (j, ci) for ci in range(nck)]
                elif j < 10:
                    dst, dk = BT3[:, j - 8, 0:T], ["BT%d" % (j - 8)]
                else:
                    dst, dk = CT3[:, j - 10, 0:T], ["CT%d" % (j - 10)]
                pend_b.append((dst, co, j, dk))
            flush_b()
            if sample or last:
                R = nseg * 3
                dst = (o_sb if sample else o_pb)[l * R:(l + 1) * R, :]
                state_out(hist["b"], 12, R, dst, 1.0, hkey["b"])

            if DBG['phase'] < 5:
                return
            S.op("dve", lambda e: e.memset(FENCE, 0.0), reads=["M"], writes=["M"])
            b5 = bank(5)
            for ci in range(nck if DBG.get("dtv", 9) >= 1 else 0):
                def dtmm(e, ci=ci):
                    i = None
                    w3 = WDT[l].rearrange("p (k e) -> p k e", k=8)
                    for k in range(8):
                        i = e.matmul(b5[0:Q, ci * 16:(ci + 1) * 16], lhsT=HB3[:, k, ci * Q:(ci + 1) * Q], rhs=w3[:, k, :],
                                     start=(k == 0), stop=(k == 7))
                    return i
                S.op("pe", dtmm, reads=["HB", "WDT%d" % l], writes=["ps5"])
            nd = nck * 16
            if DBG.get("dtv", 9) < 2:
                return
            S.op("dve", lambda e: e.tensor_tensor(out=DTP[0:Q, 0:nd].rearrange("p (c h) -> p c h", c=nck),
                                                  in0=b5[0:Q, 0:nd].rearrange("p (c h) -> p c h", c=nck),
                                                  in1=pcol(l, "dtb", 0, 16)[0:Q].unsqueeze(1).to_broadcast([Q, nck, 16]),
                                                  op=ALU.add), reads=["ps5", "PAR", "M"], writes=["DTP"])
            if DBG.get("dtv", 9) < 3:
                return
            if DBG.get("dtact", 3) & 1:
                S.op("act", lambda e: e.activation(out=DTE[0:Q, 0:nd], in_=DTP[0:Q, 0:nd], func=AF.Exp),
                     reads=["DTP", "M"], writes=["DTE"])
            else:
                S.op("dve", lambda e: e.tensor_copy(out=DTE[0:Q, 0:nd], in_=DTP[0:Q, 0:nd]), reads=["DTP", "M"], writes=["DTE"])
            if DBG.get("dtact", 3) & 2:
                S.op("act", lambda e: e.activation(out=DTT[0:Q, 0:nd], in_=DTE[0:Q, 0:nd], func=AF.Ln, bias=ONES[0:Q, 0:1]),
                     reads=["DTE", "CONST", "M"], writes=["DTT"])
            else:
                S.op("dve", lambda e: e.tensor_copy(out=DTT[0:Q, 0:nd], in_=DTE[0:Q, 0:nd]), reads=["DTE", "M"], writes=["DTT"])
            for _ in range(DBG.get("xfence", 0)):
                S.op(DBG.get("xeng", "dve"), lambda e: e.memset(FENCE, 0.0), reads=["M"], writes=["M"])
            if DBG.get("dtv", 9) < 4:
                return
            def daa_fn(e):
                i = None
                for ci_ in range(nck):
                    i = e.tensor_tensor(out=DAA[0:Q, ci_ * 16:(ci_ + 1) * 16], in0=DTT[0:Q, ci_ * 16:(ci_ + 1) * 16],
                                        in1=(ABC[0:Q, l * 16:(l + 1) * 16] if DBG.get("useabc", 1) else pcol(l, "alog", 0, 16)[0:Q]), op=ALU.mult)
                return i

            if DBG.get("useabc", 1) == 2:
                S.op("dve", lambda e: e.tensor_tensor(out=DAA[0:Q, 0:nd], in0=DTT[0:Q, 0:nd], in1=DTE[0:Q, 0:nd], op=ALU.mult),
                     reads=["DTT", "DTE", "M"], writes=["DAA"])
            elif DBG.get("useabc", 1) == 3:
                S.op("dve", lambda e: e.tensor_copy(out=DAA[0:Q, 0:nd], in_=DTT[0:Q, 0:nd]),
                     reads=["DTT", "M"], writes=["DAA"])
            elif DBG.get("useabc", 1) == 4:
                S.op("pool", daa_fn, reads=["DTT", "ABC", "M"], writes=["DAA"])
            else:
                S.op("dve", daa_fn, reads=["DTT", "ABC", "M"], writes=["DAA"])

            if DBG["ssdcut"] < 1:
                return
            Sb = SSMP if sample else SP_[l]
            sk = "SS" if sample else "SP%d" % l
            SBF = SBFS if sample else SBFP[l]
            bk = "SBFS" if sample else "SBFP%d" % l
            negm = NEGM128 if Q == 128 else NEGM32
            p01 = psum[:, 0:1024]
            p67 = psum[:, 6 * 512: 6 * 512 + 8 * Q]
            psB = b5[:, 256:384].bitcast(BF16)
            MT3 = MT.rearrange("p (h q) -> p h q", h=16)
            X3 = XDTP.rearrange("p (h m) -> p h m", h=16)
            CDEs = [CDE, CDEb]
            xsk = lambda ci: ["XS%d_%d" % (c, ci) for c in range(8)]

            def F1(ci):
                dA = DAA[0:Q, ci * 16:(ci + 1) * 16]
                dtc = DTT[0:Q, ci * 16:(ci + 1) * 16]
                cde = CDEs[ci % 2]

                def cums(e):
                    e.matmul(b5[0:Q, 64:80], lhsT=TRI_I[0:Q, 0:Q], rhs=dA, start=True, stop=True)
                    e.matmul(b5[0:Q, 80:96], lhsT=TRI_G[0:Q, 0:Q], rhs=dA, start=True, stop=True)
                    e.matmul(b5[:, 96:112], lhsT=ONES[0:Q, :], rhs=dA, start=True, stop=True)
                    return e.matmul(b5[0:16, 128:128 + Q], lhsT=dA, rhs=TRI_I[0:Q, 0:Q], start=True, stop=True)

                S.op("pe", cums, reads=["DAA", "CONST"], writes=["ps5c"])
                S.op("act", lambda e: e.activation(out=E32[0:Q, :], in_=b5[0:Q, 64:96], func=AF.Exp),
                     reads=["ps5c", "M"], writes=["E32"])
                S.op("act", lambda e: e.activation(out=cde, in_=b5[:, 96:112], func=AF.Exp),
                     reads=["ps5c", "M"], writes=["CDE%d" % (ci % 2)])
                S.op("dve", lambda e: e.tensor_copy(out=ACS[0:16, 0:Q], in_=b5[0:16, 128:128 + Q]),
                     reads=["ps5c", "M"], writes=["ACS"])
                S.op("dve", lambda e: e.tensor_scalar(out=NAC[0:Q, :], in0=b5[0:Q, 64:80], scalar1=-1.0, scalar2=None,
                                                      op0=ALU.mult), reads=["ps5c", "M"], writes=["NAC"])
                S.op("dve", lambda e: e.tensor_tensor(out=SCXW[0:Q, :], in0=dtc, in1=E32[0:Q, 16:32], op=ALU.mult),
                     reads=["DTT", "E32", "M"], writes=["SCXW"])

            def F2(ci):
                q0 = ci * Q
                dtc = DTT[0:Q, ci * 16:(ci + 1) * 16]

                def xtr(e):
                    i = None
                    for c in range(8):
                        i = e.transpose(out=p01[0:Q, c * 128:(c + 1) * 128], in_=XS3[:, c, q0:q0 + Q], identity=IDENT)
                    return i

                S.op("pe", xtr, reads=xsk(ci) + ["CONST"], writes=["ps0", "ps1"])

                def btr(e):
                    i = None
                    for g in range(2):
                        i = e.transpose(out=psB[0:Q, g * 128:(g + 1) * 128], in_=BT3[:, g, q0:q0 + Q], identity=IDENTB)
                    return i

                S.op("pe", btr, reads=["BT0", "BT1", "IDENTB"], writes=["ps5b"])

                def xdtop(e):
                    i = None
                    for hf in range(2):
                        xdo = bass.AP(XDTP.tensor, XDTP.offset + hf * 1024, [list(XDTP.ap[0]), [256, 4], [192, 2], [1, 64]])
                        i = e.tensor_tensor(
                            out=xdo[0:Q], in0=p01[0:Q, hf * 512:(hf + 1) * 512].rearrange("p (c t m) -> p c t m", c=4, t=2),
                            in1=dtc[:, hf * 8:(hf + 1) * 8].rearrange("p (c t) -> p c t", c=4).unsqueeze(3).to_broadcast([Q, 4, 2, 64]),
                            op=ALU.mult)
                    return i

                S.op("dve", xdtop, reads=["ps0", "ps1", "DTT", "M"], writes=["XDTP"])

                def xwop(e):
                    i = None
                    for hf in range(2):
                        i = e.tensor_tensor(
                            out=XW[0:Q, hf * 512:(hf + 1) * 512].rearrange("p (h m) -> p h m", h=8),
                            in0=p01[0:Q, hf * 512:(hf + 1) * 512].rearrange("p (h m) -> p h m", h=8),
                            in1=SCXW[0:Q, hf * 8:(hf + 1) * 8].unsqueeze(2).to_broadcast([Q, 8, 64]), op=ALU.mult)
                    return i

                S.op("dve", xwop, reads=["ps0", "ps1", "SCXW", "M"], writes=["XW"])
                S.op("act", lambda e: e.activation(out=BTOK[0:Q, :], in_=psB[0:Q, :], func=AF.Copy),
                     reads=["ps5b", "M"], writes=["BTOK"])

                def cbt(e):
                    i = None
                    for g in range(2):
                        i = e.matmul(bank(4)[0:Q, g * 128:g * 128 + Q], lhsT=BT3[:, g, q0:q0 + Q], rhs=CT3[:, g, q0:q0 + Q],
                                     start=True, stop=True)
                    return i

                S.op("pe", cbt, reads=["BT0", "BT1", "CT0", "CT1"], writes=["ps4"])

                def ebc(e):
                    i = None
                    for c in range(8):
                        i = e.matmul(p67[:, c * Q:(c + 1) * Q], lhsT=SEL2[0:16, c * 128:(c + 1) * 128], rhs=ACS[0:16, 0:Q],
                                     start=True, stop=True)
                    return i

                S.op("pe", ebc, reads=["ACS", "CONST"], writes=["ps6", "ps7"])

                def ebexp(e):
                    hw_ = 4 * Q
                    e.activation(out=EB[:, 0:hw_], in_=p67[:, 0:hw_], func=AF.Exp)
                    return e.activation(out=EB[:, hw_:2 * hw_], in_=p67[:, hw_:2 * hw_], func=AF.Exp)

                S.op("act", ebexp, reads=["ps6", "ps7", "M"], writes=["EB"])

            def F5(ci):
                dA = DAA[0:Q, ci * 16:(ci + 1) * 16]
                for r in range(4):
                    bb = 2 + r % 2
                    pl = bank(bb)

                    def lmm(e, r=r, pl=pl):
                        e.matmul(pl[0:Q, 0:4 * Q], lhsT=IDENT[0:Q, 0:Q], rhs=negm[0:Q, 0:4 * Q], start=True, stop=False)
                        i = None
                        for i4 in range(4):
                            h = 4 * r + i4
                            i = e.matmul(pl[0:Q, i4 * Q:(i4 + 1) * Q], lhsT=dA[:, h:h + 1].to_broadcast([Q, Q]),
                                         rhs=TRI_I[0:Q, 0:Q], start=False, stop=(i4 == 3), skip_group_check=True)
                        return i

                    S.op("pe", lmm, reads=["DAA", "CONST"], writes=["ps%d" % bb])
                    lt = LT[r % 2]

                    def lexp(e, r=r, pl=pl, lt=lt):
                        i = None
                        for i4 in range(4):
                            h = 4 * r + i4
                            i = e.activation(out=lt[0:Q, i4 * Q:(i4 + 1) * Q], in_=pl[0:Q, i4 * Q:(i4 + 1) * Q], func=AF.Exp,
                                             bias=NAC[0:Q, h:h + 1])
                        return i

                    S.op("act", lexp, reads=["ps%d" % bb, "NAC", "M"], writes=["LT%d" % (r % 2)])
                    g = r // 2
                    S.op("dve", lambda e, r=r, lt=lt, g=g: e.tensor_tensor(
                        out=MT3[0:Q, 4 * r:4 * r + 4, 0:Q], in0=lt[0:Q, 0:4 * Q].rearrange("p (h q) -> p h q", h=4),
                        in1=bank(4)[0:Q, g * 128:g * 128 + Q].unsqueeze(1).to_broadcast([Q, 4, Q]), op=ALU.mult),
                         reads=["LT%d" % (r % 2), "ps4", "M"], writes=["MT%d" % r])

            def B1(ci):
                q0 = ci * Q

                def ymm(e):
                    i = None
                    for c in range(8):
                        e.matmul(p01[:, c * Q:(c + 1) * Q], lhsT=X3[0:Q, 2 * c, :], rhs=MT3[0:Q, 2 * c, 0:Q],
                                 start=True, stop=False)
                        i = e.matmul(p01[:, c * Q:(c + 1) * Q], lhsT=X3[0:Q, 2 * c + 1, :], rhs=MT3[0:Q, 2 * c + 1, 0:Q],
                                     start=False, stop=True)
                    return i

                S.op("pe", ymm, reads=["XDTP", "MT0", "MT1", "MT2", "MT3"], writes=["ps0", "ps1"])

                def omm(e):
                    i = None
                    for c in range(8):
                        i = e.matmul(p67[:, c * Q:(c + 1) * Q], lhsT=SBF[:, c * 128:(c + 1) * 128],
                                     rhs=CT3[:, c // 4, q0:q0 + Q], start=True, stop=True)
                    return i

                S.op("pe", omm, reads=[bk + "_0", bk + "_1", "CT0", "CT1"], writes=["ps6", "ps7"])
                for g in range(2):
                    S.op("pe", lambda e, g=g: e.matmul(bank(2 + g), lhsT=BTOK[0:Q, g * 128:(g + 1) * 128],
                                                       rhs=XW[0:Q, g * 512:(g + 1) * 512], start=True, stop=True),
                         reads=["BTOK", "XW"], writes=["ps%d" % (2 + g)])

            def B2(ci):
                def t1a(e):
                    hw_ = 4 * Q
                    e.tensor_tensor(out=T1[:, 0:hw_], in0=p67[:, 0:hw_], in1=EB[:, 0:hw_], op=ALU.mult)
                    return e.tensor_tensor(out=T1[:, hw_:2 * hw_], in0=p67[:, hw_:2 * hw_], in1=EB[:, hw_:2 * hw_], op=ALU.mult)

                def t1b(e):
                    hw_ = 4 * Q
                    e.tensor_tensor(out=T1[:, 0:hw_], in0=p01[:, 0:hw_], in1=T1[:, 0:hw_], op=ALU.add)
                    return e.tensor_tensor(out=T1[:, hw_:2 * hw_], in0=p01[:, hw_:2 * hw_], in1=T1[:, hw_:2 * hw_], op=ALU.add)

                S.op("dve", t1a, reads=["ps6", "ps7", "EB", "M"], writes=["T1"])
                S.op("dve", t1b, reads=["ps0", "ps1", "T1", "M"], writes=["T1"])

            def B3(ci):
                q0 = ci * Q

                def dsk(e):
                    i = None
                    for c in range(8):
                        i = e.scalar_tensor_tensor(out=XS3[:, c, q0:q0 + Q], in0=XS3[:, c, q0:q0 + Q],
                                                   scalar=pcol(l, "dsk", c), in1=T1[:, c * Q:(c + 1) * Q],
                                                   op0=ALU.mult, op1=ALU.add)
                    return i

                S.op("dve", dsk, reads=["T1", "PAR", "M"] + xsk(ci), writes=xsk(ci))

            def B4(ci):
                cde = CDEs[ci % 2]
                for g in range(2):
                    Sg = Sb[:, g * 512:(g + 1) * 512]
                    S.op("dve", lambda e, g=g, Sg=Sg: e.tensor_tensor(
                        out=Sg.rearrange("p (h m) -> p h m", h=8), in0=Sg.rearrange("p (h m) -> p h m", h=8),
                        in1=cde[:, g * 8:(g + 1) * 8].unsqueeze(2).to_broadcast([128, 8, 64]), op=ALU.mult),
                         reads=["CDE%d" % (ci % 2), sk + "_%d" % g, "M"], writes=[sk + "_%d" % g])
                    S.op("dve", lambda e, g=g, Sg=Sg: e.tensor_tensor(out=Sg, in0=bank(2 + g), in1=Sg, op=ALU.add),
                         reads=["ps%d" % (2 + g), sk + "_%d" % g], writes=[sk + "_%d" % g])
                    S.op("act", lambda e, g=g, Sg=Sg: e.activation(out=SBF[:, g * 512:(g + 1) * 512], in_=Sg, func=AF.Copy),
                         reads=[sk + "_%d" % g], writes=[bk + "_%d" % g])

            if sample:
                for ci in range(nck):
                    ssm_in(Sb, sk, sts[(l * 4 + ci) * 1024:(l * 4 + ci + 1) * 1024, :], SBF, bk)
                    F1(ci); F2(ci); F5(ci); B1(ci); B2(ci); B3(ci); B4(ci)
                    ssm_out(Sb, sk, o_sss[(l * 4 + ci) * 1024:(l * 4 + ci + 1) * 1024, :])
            else:
                F1(0); F2(0); F5(0)
                for ci in range(nck):
                    nxt = ci + 1 < nck
                    B1(ci)
                    B2(ci)
                    if nxt:
                        F1(ci + 1)
                    B4(ci)
                    if nxt:
                        F2(ci + 1)
                    B3(ci)
                    if nxt:
                        F5(ci + 1)
            if last and not sample and DBG.get("ssmout", 1):
                ssm_out(Sb, sk, o_pss[l * 1024:(l + 1) * 1024, :])

            if DBG['phase'] < 6:
                return
            allxs = lambda c: ["XS%d_%d" % (c, ci) for ci in range(nck)]
            for c in range(8):
                slab, skey, jj = wchunk(l, 32 + c)
                pz, pzk = proj(T, slab, skey, jj)
                zt = ZT[c % 2]
                S.op("act", lambda e, zt=zt, pz=pz: e.activation(out=zt[:, 0:T], in_=pz, func=AF.Silu),
                     reads=[pzk], writes=["ZT%d" % (c % 2)])
                S.op("dve", lambda e, zt=zt, c=c: e.tensor_tensor(out=XS3[:, c, 0:T], in0=XS3[:, c, 0:T], in1=zt[:, 0:T],
                                                                 op=ALU.mult),
                     reads=["ZT%d" % (c % 2)] + allxs(c), writes=allxs(c))
            rmsnorm(T, lambda c: pcol(l, "gssm", c), lambda c: YC3[:, 4 + c, 0:T], lambda c: allxs(c),
                    lambda c: "YC%d" % (4 + c), EPS, src3=XS3)

            if DBG['phase'] < 7:
                return
            for si in range(4):
                slab, skey = get_slab("wout", l, si)
                for m in range(2):
                    po, pok = proj(T, slab, skey, m, width=256, nk=16, rhs3=YC3, rkeys=["YC%d" % k for k in range(16)])
                    cc = si * 2 + m
                    S.op("dve", lambda e, po=po, cc=cc: e.tensor_tensor(out=XT3[:, cc, 0:T], in0=po, in1=XT3[:, cc, 0:T],
                                                                        op=ALU.add),
                         reads=[pok, "XT%d" % cc], writes=["XT%d" % cc])

            if DBG['phase'] < 8:
                return
            rmsnorm(T, lambda c: pcol(l, "g2", c), lambda c: HB3[:, c, 0:T], lambda c: ["XT%d" % c],
                    lambda c: "HB", EPS)
            S.op("dve", lambda e: e.memset(FENCE, 0.0), reads=["M"], writes=["M"])
            pend_f = []

            def flush_f():
                while pend_f:
                    gs_, cf_, j_, pu_, puk_ = pend_f.pop(0)
                    S.op("act", lambda e, gs_=gs_, cf_=cf_: e.activation(out=gs_[:, 0:T], in_=cf_[:, 0:T], func=AF.Silu),
                         reads=["CF%d" % (j_ % 2), "M"], writes=["GS%d" % (j_ % 2)])
                    S.op("dve", lambda e, pu_=pu_, gs_=gs_, j_=j_: e.tensor_tensor(out=FA3[:, j_, 0:T], in0=pu_, in1=gs_[:, 0:T],
                                                                                   op=ALU.mult),
                         reads=[puk_, "GS%d" % (j_ % 2), "M"], writes=["FA%d" % j_])

            for si in range(11):
                slab, skey = get_slab("wup", l, si)
                for jj in range(2):
                    j = si * 2 + jj
                    pg, pgk = proj(T, slab, skey, 2 * jj)
                    uf = UFt[j % 3]
                    U3 = uf[:, 0:nseg * (L + 2)].rearrange("p (s t) -> p s t", s=nseg)
                    uk = "UF%d" % (j % 3)
                    hist_in("dve", hist["f"], j, nseg, 2, U3, uk, hkey["f"])
                    S.op("act", lambda e, U3=U3, pg=pg: e.activation(out=U3[:, :, 2:2 + L], in_=seg3(pg), func=AF.Copy),
                         reads=[pgk, "M"], writes=[uk])
                    flush_f()
                    hist_out("dve", hist["f"], j, nseg, 2, L, U3, uk, hkey["f"])
                    cf = CFt[j % 2]

                    def convF(e, U3=U3, cf=cf, j=j):
                        o = seg3(cf[:, 0:T])
                        e.tensor_scalar(out=o, in0=U3[:, :, 0:L], scalar1=pcol(l, "cfw", j * 3), scalar2=None, op0=ALU.mult)
                        e.scalar_tensor_tensor(out=o, in0=U3[:, :, 1:1 + L], scalar=pcol(l, "cfw", j * 3 + 1), in1=o,
                                               op0=ALU.mult, op1=ALU.add)
                        return e.scalar_tensor_tensor(out=o, in0=U3[:, :, 2:2 + L], scalar=pcol(l, "cfw", j * 3 + 2), in1=o,
                                                      op0=ALU.mult, op1=ALU.add)

                    S.op("dve", convF, reads=[uk, uk + "h", "PAR", "M"], writes=["CF%d" % (j % 2)])
                    gs = GS[j % 2]
                    pu, puk = proj(T, slab, skey, 2 * jj + 1)
                    pend_f.append((gs, cf, j, pu, puk))
            flush_f()
            if sample or last:
                R = nseg * 2
                dst = (o_sf if sample else o_pf)[l * R:(l + 1) * R, :]
                state_out(hist["f"], 22, R, dst, 1.0, hkey["f"])
            for si in range(8):
                slab, skey = get_slab("wdn", l, si)
                pd, pdk = proj(T, slab, skey, 0, width=128, nk=22, rhs3=FA3, rkeys=["FA%d" % k for k in range(22)])
                S.op("dve", lambda e, pd=pd, si=si: e.tensor_tensor(out=XT3[:, si, 0:T], in0=pd, in1=XT3[:, si, 0:T],
                                                                    op=ALU.add),
                     reads=[pdk, "XT%d" % si], writes=["XT%d" % si])

        def tile(xsrc, ydst, T, nseg, sample, last):
            nblk = T // 128
            for blk in range(nblk):
                xi = XIN[blk % 2]
                S.dma("sp", "xin%d" % (blk % 2), lambda e, blk=blk, xi=xi: e.dma_start(out=xi, in_=xsrc[blk * 128:(blk + 1) * 128, :]),
                      writes=["XIN%d" % (blk % 2)])
                for hf in range(2):
                    b = next_bank()

                    def fn(e, hf=hf, b=b, xi=xi):
                        i = None
                        for j in range(4):
                            c = hf * 4 + j
                            i = e.transpose(out=bank(b)[:, j * 128:(j + 1) * 128], in_=xi[:, c * 128:(c + 1) * 128],
                                            identity=IDENT)
                        return i

                    S.op("pe", fn, reads=["XIN%d" % (blk % 2), "CONST"], writes=["ps%d" % b])
                    S.op("act", lambda e, hf=hf, b=b, blk=blk: e.activation(
                        out=XT3[:, hf * 4:(hf + 1) * 4, blk * 128:(blk + 1) * 128],
                        in_=bank(b).rearrange("p (c t) -> p c t", c=4), func=AF.Copy),
                         reads=["ps%d" % b], writes=["XT%d" % c for c in range(hf * 4, hf * 4 + 4)])
            for l in range(DBG["layers"]):
                layer(l, T, nseg, sample, last)
            S.op("dve", lambda e: e.memset(FENCE, 0.0), reads=["M"], writes=["M"])
            for c in range(8):
                stat_chunk(T, c, XT3, ["XT%d" % c])
            S.op("dve", lambda e: e.tensor_scalar(out=STAT[:, 0:T], in0=bank(4)[:, 0:T], scalar1=EPS, scalar2=None,
                                                  op0=ALU.add), reads=["ps4"], writes=["STAT"])
            def rs_fn(e):
                e.activation(out=RSTD[:, 0:T], in_=STAT[:, 0:T], func=AF.Ln)
                return e.activation(out=RSTD[:, 0:T], in_=RSTD[:, 0:T], func=AF.Exp, scale=-0.5)

            S.op("act", rs_fn, reads=["STAT"], writes=["RSTD"])
            gf = PAR[:, NL * NPAR: NL * NPAR + 8]
            for blk in range(nblk):
                yf = YFt[blk % 2]
                yf3 = v3(yf, 8)

                def fin(e, blk=blk, yf3=yf3):
                    i = None
                    for c in range(8):
                        i = e.scalar_tensor_tensor(out=yf3[:, c, :], in0=XT3[:, c, blk * 128:(blk + 1) * 128],
                                                   scalar=gf[:, c:c + 1], in1=RSTD[:, blk * 128:(blk + 1) * 128],
                                                   op0=ALU.mult, op1=ALU.mult)
                    return i

                S.op("dve", fin, reads=["XT%d" % c for c in range(8)] + ["RSTD", "PAR", "M"], writes=["YF%d" % (blk % 2)])
                xi = XIN[blk % 2]
                for hf in range(2):
                    b = next_bank()

                    def fn(e, hf=hf, b=b, yf3=yf3):
                        i = None
                        for j in range(4):
                            c = hf * 4 + j
                            i = e.transpose(out=bank(b)[:, j * 128:(j + 1) * 128], in_=yf3[:, c, :], identity=IDENT)
                        return i

                    S.op("pe", fn, reads=["YF%d" % (blk % 2), "CONST"], writes=["ps%d" % b])
                    S.op("act", lambda e, hf=hf, b=b, xi=xi: e.activation(out=xi[:, hf * 512:(hf + 1) * 512], in_=bank(b),
                                                                         func=AF.Copy),
                         reads=["ps%d" % b], writes=["XIN%d" % (blk % 2)])
                S.dma("act", "yout%d" % (blk % 2), lambda e, blk=blk, xi=xi: e.dma_start(out=ydst[blk * 128:(blk + 1) * 128, :], in_=xi),
                      reads=["XIN%d" % (blk % 2)])

        for t in range(NT):
            tile(xp[t * TP:(t + 1) * TP, :], yp[t * TP:(t + 1) * TP, :], TP, 1, False, t == NT - 1)
        if with_sample:
            tile(xs, ys, 128, 4, True, False)

        for e_ in ("sp", "act", "pool"):
            pass
        S.final_wait("sp")
        import os as _os2
        if _os2.environ.get("DUMPLOG"):
            with open(_os2.environ["DUMPLOG"], "w") as f:
                for t in S.log:
                    f.write(repr(t) + "\n")
            print("counts", S.cnt, {k: v[1] for k, v in S.dsem.items()})
        block = es.enter_context(nc.Block())
        S.emit(block)
    return nc


def _pack_params(inp):
    out = np.zeros((128, NPARTOT), np.float32)

    def fm(v, nch):
        return np.ascontiguousarray(v.reshape(nch, 128).T)

    def put(l, name, arr):
        o, w = _PC[name]
        assert arr.shape == (128, w), (name, arr.shape, w)
        out[:, l * NPAR + o: l * NPAR + o + w] = arr

    for l in range(NL):
        put(l, "g1", fm(inp["norm_mix_g"][l], 8))
        put(l, "g2", fm(inp["norm_ffn_g"][l], 8))
        put(l, "gssm", fm(inp["ssm_norm_g"][l], 8))
        put(l, "dsk", fm(np.repeat(inp["d_skip"][l], 64), 8))
        def cw(w, nch):
            K = w.shape[0]
            return np.ascontiguousarray(w.reshape(K, nch, 128).transpose(2, 1, 0).reshape(128, nch * K))
        put(l, "caw", cw(inp["conv_a_w"][l], 4))
        put(l, "cbw", cw(inp["conv_b_w"][l], 12))
        put(l, "cbb", fm(inp["conv_b_bias"][l], 12))
        put(l, "ccw", cw(inp["conv_c_w"][l], 4))
        put(l, "ccb", fm(inp["conv_c_bias"][l], 4))
        put(l, "lng", fm(inp["ln_c_g"][l], 4))
        put(l, "lnb", fm(inp["ln_c_b"][l], 4))
        put(l, "cfw", cw(inp["conv_ffn_w"][l], 22))
        put(l, "dtb", np.broadcast_to(inp["dt_bias"][l][None, :], (128, 16)))
        put(l, "alog", np.broadcast_to(inp["a_log"][l][None, :], (128, 16)))
    out[:, NL * NPAR: NL * NPAR + 8] = fm(inp["final_norm_g"], 8)
    return out


_NC_CACHE = {}


def kernel(**inp):
    inp = {k: np.asarray(v) for k, v in inp.items()}
    xp = inp["x_prompt"]
    seq = xp.shape[1]
    nt = seq // TP
    key = (nt,)
    if key not in _NC_CACHE:
        _NC_CACHE[key] = build_program(nt, True)
    nc = _NC_CACHE[key]
    par = _pack_params(inp)
    win = np.ascontiguousarray(inp["w_in"].reshape(NL * D, DIN))
    wout = np.ascontiguousarray(inp["w_out"].reshape(NL * 2048, D))
    wup = np.ascontiguousarray(inp["w_up"].reshape(NL * D, 2 * DFF))
    wdn = np.ascontiguousarray(inp["w_down"].reshape(NL * DFF, D))
    in_maps = []
    for c in range(NCORES):
        b0 = c * 4
        in_maps.append({
            "xp": np.ascontiguousarray(xp[c]),
            "xs": np.ascontiguousarray(inp["x_sample"][b0:b0 + 4].reshape(128, D)),
            "sta": np.ascontiguousarray(inp["state_conv_a"][:, b0:b0 + 4].reshape(NL * 8, 512)),
            "sts": np.ascontiguousarray(inp["state_ssm"][:, b0:b0 + 4].reshape(NL * 4 * 1024, 128)),
            "stb": np.ascontiguousarray(inp["state_conv_b"][:, b0:b0 + 4].reshape(NL * 12, 1536)),
            "stc": np.ascontiguousarray(inp["state_conv_c"][:, b0:b0 + 4].reshape(NL * 120, 512)),
            "stf": np.ascontiguousarray(inp["state_conv_ffn"][:, b0:b0 + 4].reshape(NL * 8, DFF)),
            "win": win, "wout": wout, "wup": wup, "wdn": wdn, "par": par,
        })
    res = run_bass_kernel_spmd(nc, in_maps, core_ids=list(range(NCORES)))
    R = res.results
    y_prompt = np.stack([R[c]["yp"] for c in range(NCORES)], 0)
    y_sample = np.concatenate([R[c]["ys"].reshape(4, 32, D) for c in range(NCORES)], 0)

    def pst(name, shp):
        return np.stack([R[c][name].reshape((NL,) + shp) for c in range(NCORES)], 1)

    def sst(name, shp):
        return np.concatenate([R[c][name].reshape((NL, 4) + shp) for c in range(NCORES)], 1)

    return (y_prompt.astype(np.float32), y_sample.astype(np.float32),
            pst("pa", (2, 512)), pst("pss", (16, 64, 128)), pst("pb", (3, 1536)), pst("pc", (30, 512)),
            pst("pf", (2, DFF)),
            sst("sa", (2, 512)), sst("sss", (16, 64, 128)), sst("sb", (3, 1536)), sst("sc", (30, 512)),
            sst("sf", (2, DFF)))
```
